# Optimizing a Trainium2 kernel written in Bass

```python
import math
import jax
import jax.numpy as jnp
from jax import lax
import numpy as np

D_MODEL = 1024
BATCH = 4
SEQ = 4096
DEPTH = 1

GRID_W = 64
CTX_LEN = 256
NORM_EPS = 1e-6

N_HEADS = 8
N_KV_HEADS = 2
HEAD_DIM = 64
WINDOW = 128
BLOCK = 128
ROPE_BASE = 10000.0

HY_WIDTH = 512
HY_SHORT = 3
HY_EMB_BANDS = 16
HY_EMB_DIM = 1 + 2 * HY_EMB_BANDS
HY_FILTER_HIDDEN = 64
HY_FAST_DECAY = 0.3
HY_SLOW_DECAY = 1.5
HY_DECAY_TARGET = 1e-2

PEER_HEADS = 8
PEER_KEYS = 128
PEER_EXPERTS = PEER_KEYS * PEER_KEYS
PEER_TOPK = 16
PEER_DKEY = 128
PEER_CHUNK = 128

Q_W = N_HEADS * HEAD_DIM
KV_W = N_KV_HEADS * HEAD_DIM
HY_IN = 3 * HY_WIDTH
GATE_W = 2 * D_MODEL
OFF_K = Q_W
OFF_V = OFF_K + KV_W
OFF_HY = OFF_V + KV_W
OFF_G = OFF_HY + HY_IN
IN_W = OFF_G + GATE_W

kernel_name = 'hybrid_swa_hyena_peer_dit_block'


def _rmsnorm(x, g):
    xf = x.astype(jnp.float32)
    y = xf * lax.rsqrt(jnp.mean(xf * xf, axis=-1, keepdims=True) + NORM_EPS)
    return (y * g.astype(jnp.float32)).astype(x.dtype)


def _modulate(h, shift, scale):
    return h * (1 + scale) + shift


def _rope_1d(x, pos):
    f = x.shape[-1] // 2
    inv = ROPE_BASE ** (-jnp.arange(f, dtype=jnp.float32) / f)
    ang = pos.astype(jnp.float32)[:, None] * inv[None, :]
    cos = jnp.cos(ang)[:, None, :]
    sin = jnp.sin(ang)[:, None, :]
    xf = x.astype(jnp.float32)
    x1, x2 = xf[..., :f], xf[..., f:]
    return jnp.concatenate([x1 * cos - x2 * sin, x2 * cos + x1 * sin], axis=-1).astype(x.dtype)


def _rope_2d(x, row, col):
    half = x.shape[-1] // 2
    return jnp.concatenate([_rope_1d(x[..., :half], row), _rope_1d(x[..., half:], col)], axis=-1)


def _window_attention(q, k, v, k_ctx, v_ctx, sink):
    B, L, H, dh = q.shape
    C = k_ctx.shape[1]
    G = H // N_KV_HEADS
    nb = L // BLOCK
    scale = dh ** -0.5
    qb = q.reshape(B, nb, BLOCK, N_KV_HEADS, G, dh)
    pad = ((0, 0), (BLOCK, BLOCK), (0, 0), (0, 0))
    kp = jnp.pad(k, pad).reshape(B, nb + 2, BLOCK, N_KV_HEADS, dh)
    vp = jnp.pad(v, pad).reshape(B, nb + 2, BLOCK, N_KV_HEADS, dh)
    kw = jnp.concatenate([kp[:, :-2], kp[:, 1:-1], kp[:, 2:]], axis=2)
    vw = jnp.concatenate([vp[:, :-2], vp[:, 1:-1], vp[:, 2:]], axis=2)
    s_loc = jnp.einsum('bnqhgd,bnkhd->bnhgqk', qb, kw).astype(jnp.float32) * scale
    s_ctx = jnp.einsum('bnqhgd,bchd->bnhgqc', qb, k_ctx).astype(jnp.float32) * scale
    blk = jnp.arange(nb, dtype=jnp.int32)
    qpos = blk[:, None] * BLOCK + jnp.arange(BLOCK, dtype=jnp.int32)[None, :]
    kpos = (blk[:, None] - 1) * BLOCK + jnp.arange(3 * BLOCK, dtype=jnp.int32)[None, :]
    rel = kpos[:, None, :] - qpos[:, :, None]
    valid = (jnp.abs(rel) <= WINDOW) & (kpos[:, None, :] >= 0) & (kpos[:, None, :] < L)
    s_loc = jnp.where(valid[None, :, None, None], s_loc, -jnp.inf)
    s_sink = jnp.broadcast_to(sink.astype(jnp.float32).reshape(1, 1, N_KV_HEADS, G, 1, 1),
                              s_loc.shape[:-1] + (1,))
    prob = jax.nn.softmax(jnp.concatenate([s_loc, s_ctx, s_sink], axis=-1), axis=-1)
    p_loc = prob[..., :3 * BLOCK].astype(v.dtype)
    p_ctx = prob[..., 3 * BLOCK:3 * BLOCK + C].astype(v.dtype)
    o = (jnp.einsum('bnhgqk,bnkhd->bnqhgd', p_loc, vw)
         + jnp.einsum('bnhgqc,bchd->bnqhgd', p_ctx, v_ctx))
    return o.reshape(B, L, H * dh)


def _context_attention(q, k, v, sink):
    B, C, H, dh = q.shape
    G = H // N_KV_HEADS
    qg = q.reshape(B, C, N_KV_HEADS, G, dh)
    s = jnp.einsum('bqhgd,bkhd->bhgqk', qg, k).astype(jnp.float32) * (dh ** -0.5)
    s_sink = jnp.broadcast_to(sink.astype(jnp.float32).reshape(1, N_KV_HEADS, G, 1, 1), s.shape[:-1] + (1,))
    prob = jax.nn.softmax(jnp.concatenate([s, s_sink], axis=-1), axis=-1)[..., :C]
    o = jnp.einsum('bhgqk,bkhd->bqhgd', prob.astype(v.dtype), v)
    return o.reshape(B, C, H * dh)


def _hyena_filter(L, fw1, fb1, fw2, fb2, fw3, fb3, freq):
    fw1, fb1, fw2, fb2, fw3, fb3, freq = [a.astype(jnp.float32) for a in (fw1, fb1, fw2, fb2, fw3, fb3, freq)]
    t = jnp.arange(L, dtype=jnp.float32) / L
    bands = jnp.arange(1, HY_EMB_BANDS + 1, dtype=jnp.float32)
    ang = 2.0 * math.pi * t[:, None] * bands[None, :]
    z = jnp.concatenate([t[:, None], jnp.cos(ang), jnp.sin(ang)], axis=-1)
    h = jnp.sin(freq * (z @ fw1 + fb1))
    h = jnp.sin(freq * (h @ fw2 + fb2))
    h = h @ fw3 + fb3
    deltas = jnp.linspace(math.log(HY_DECAY_TARGET) / HY_SLOW_DECAY,
                          math.log(HY_DECAY_TARGET) / HY_FAST_DECAY, HY_WIDTH, dtype=jnp.float32)
    decay = jnp.exp(-t[:, None] * jnp.abs(deltas)[None, :])
    h_fwd = h[:, :HY_WIDTH] * decay
    h_bwd = h[:, HY_WIDTH:] * decay
    kern = jnp.concatenate([h_fwd, jnp.zeros((1, HY_WIDTH), jnp.float32), jnp.flip(h_bwd[:L - 1], axis=0)], axis=0)
    return kern / jnp.sum(jnp.abs(kern), axis=0, keepdims=True)


def _short_conv(u, w, b):
    up = jnp.pad(u, ((0, 0), (1, 1), (0, 0)))
    return up[:, :-2] * w[0] + up[:, 1:-1] * w[1] + up[:, 2:] * w[2] + b


def _fft_long_conv(u, kern, skip):
    L = u.shape[1]
    uf = jnp.fft.rfft(u.astype(jnp.float32), n=2 * L, axis=1)
    kf = jnp.fft.rfft(kern, n=2 * L, axis=0)
    y = jnp.fft.irfft(uf * kf[None], n=2 * L, axis=1)[:, :L]
    return (y + u.astype(jnp.float32) * skip.astype(jnp.float32)).astype(u.dtype)


def _hyena_branch(z, conv_w, conv_b, kern, skip):
    z = _short_conv(z, conv_w, conv_b)
    x0, x1, v = z[..., :HY_WIDTH], z[..., HY_WIDTH:2 * HY_WIDTH], z[..., 2 * HY_WIDTH:]
    v = _fft_long_conv(v * x1, kern, skip)
    return v * x0


def _branch_merge(p, y_attn, y_hy, w_o_attn, w_o_hy, w_out):
    gate = jax.nn.sigmoid(p[..., OFF_G:])
    merged = gate[..., :D_MODEL] * (y_attn @ w_o_attn) + gate[..., D_MODEL:] * (y_hy @ w_o_hy)
    return merged @ w_out


def _peer_route(h, w_q, sub_keys):
    T = h.shape[0]
    q = (h @ w_q).reshape(T, PEER_HEADS, 2, PEER_DKEY // 2)
    s = jnp.einsum('thpd,hpkd->thpk', q, sub_keys).astype(jnp.float32)
    s_top, i_top = lax.top_k(s, PEER_TOPK)
    cand = s_top[:, :, 0, :, None] + s_top[:, :, 1, None, :]
    cand_idx = i_top[:, :, 0, :, None] * PEER_KEYS + i_top[:, :, 1, None, :]
    best, pos = lax.top_k(cand.reshape(T, PEER_HEADS, PEER_TOPK * PEER_TOPK), PEER_TOPK)
    idx = jnp.take_along_axis(cand_idx.reshape(T, PEER_HEADS, PEER_TOPK * PEER_TOPK), pos, axis=-1)
    g = jax.nn.softmax(best, axis=-1)
    return idx.reshape(T, PEER_HEADS * PEER_TOPK), g.reshape(T, PEER_HEADS * PEER_TOPK)


def _peer_ffn(h, w_q, sub_keys, u_tab, v_tab):
    B, L, D = h.shape
    T = B * L
    ht = h.reshape(T, D)
    idx, g = _peer_route(ht, w_q, sub_keys)
    nc = T // PEER_CHUNK

    def chunk(args):
        hc, ic, gc = args
        u = u_tab[ic]
        a = jax.nn.gelu(jnp.einsum('tkd,td->tk', u, hc))
        return jnp.einsum('tk,tkd->td', (gc * a).astype(hc.dtype), v_tab[ic])

    out = lax.map(chunk, (ht.reshape(nc, PEER_CHUNK, D),
                          idx.reshape(nc, PEER_CHUNK, -1),
                          g.reshape(nc, PEER_CHUNK, -1)))
    return out.reshape(B, L, D)


def setup_inputs(seed: int = 0) -> dict:
    key = jax.random.key(seed)
    ks = jax.random.split(key, 28)

    def nrm(k, shape, scale):
        return jax.random.normal(k, shape, jnp.float32) * scale

    Dp = DEPTH
    return {
        'x': nrm(ks[0], (BATCH, SEQ, D_MODEL), 1.0),
        'c': nrm(ks[1], (BATCH, D_MODEL), 1.0),
        'ctx': nrm(ks[2], (BATCH, CTX_LEN, D_MODEL), 1.0),
        'c_ctx': nrm(ks[3], (D_MODEL,), 1.0),
        'w_mod': nrm(ks[4], (Dp, D_MODEL, 6 * D_MODEL), 0.5 * D_MODEL ** -0.5),
        'b_mod': nrm(ks[5], (Dp, 6 * D_MODEL), 0.02),
        'norm1_g': 1.0 + nrm(ks[6], (Dp, D_MODEL), 0.02),
        'w_in': nrm(ks[7], (Dp, D_MODEL, IN_W), D_MODEL ** -0.5),
        'attn_sink': nrm(ks[8], (Dp, N_HEADS), 0.5),
        'hy_conv_w': nrm(ks[9], (Dp, HY_SHORT, HY_IN), HY_SHORT ** -0.5),
        'hy_conv_b': nrm(ks[10], (Dp, HY_IN), 0.02),
        'hy_fw1': nrm(ks[11], (Dp, HY_EMB_DIM, HY_FILTER_HIDDEN), HY_EMB_DIM ** -0.5),
        'hy_fb1': nrm(ks[12], (Dp, HY_FILTER_HIDDEN), 0.1),
        'hy_fw2': nrm(ks[13], (Dp, HY_FILTER_HIDDEN, HY_FILTER_HIDDEN), HY_FILTER_HIDDEN ** -0.5),
        'hy_fb2': nrm(ks[14], (Dp, HY_FILTER_HIDDEN), 0.1),
        'hy_fw3': nrm(ks[15], (Dp, HY_FILTER_HIDDEN, 2 * HY_WIDTH), HY_FILTER_HIDDEN ** -0.5),
        'hy_fb3': nrm(ks[16], (Dp, 2 * HY_WIDTH), 0.02),
        'hy_freq': 1.0 + nrm(ks[17], (Dp, HY_FILTER_HIDDEN), 0.02),
        'hy_skip': nrm(ks[18], (Dp, HY_WIDTH), 0.5),
        'w_o_attn': nrm(ks[19], (Dp, Q_W, D_MODEL), Q_W ** -0.5),
        'w_o_hy': nrm(ks[20], (Dp, HY_WIDTH, D_MODEL), HY_WIDTH ** -0.5),
        'w_out': nrm(ks[21], (Dp, D_MODEL, D_MODEL), D_MODEL ** -0.5),
        'norm2_g': 1.0 + nrm(ks[22], (Dp, D_MODEL), 0.02),
        'peer_wq': nrm(ks[23], (Dp, D_MODEL, PEER_HEADS * PEER_DKEY), D_MODEL ** -0.5),
        'peer_keys': nrm(ks[24], (Dp, PEER_HEADS, 2, PEER_KEYS, PEER_DKEY // 2), (PEER_DKEY // 2) ** -0.5),
        'peer_u': nrm(ks[25], (Dp, PEER_EXPERTS, D_MODEL), D_MODEL ** -0.5),
        'peer_v': nrm(ks[26], (Dp, PEER_EXPERTS, D_MODEL), 0.2),
        'final_g': 1.0 + nrm(ks[27], (D_MODEL,), 0.02),
    }


def reference(x, c, ctx, c_ctx, w_mod, b_mod, norm1_g, w_in, attn_sink, hy_conv_w, hy_conv_b,
              hy_fw1, hy_fb1, hy_fw2, hy_fb2, hy_fw3, hy_fb3, hy_freq, hy_skip,
              w_o_attn, w_o_hy, w_out, norm2_g, peer_wq, peer_keys, peer_u, peer_v, final_g):
    B, L, _ = x.shape
    C = ctx.shape[1]
    rows = L // GRID_W
    row = jnp.repeat(jnp.arange(rows, dtype=jnp.int32), GRID_W)
    col = jnp.tile(jnp.arange(GRID_W, dtype=jnp.int32), rows)
    for li in range(DEPTH):
        last = li == DEPTH - 1
        hy = (hy_fw1[li], hy_fb1[li], hy_fw2[li], hy_fb2[li], hy_fw3[li], hy_fb3[li], hy_freq[li])
        mod = jax.nn.silu(c) @ w_mod[li] + b_mod[li]
        mod_c = jax.nn.silu(c_ctx) @ w_mod[li] + b_mod[li]
        sh1, sc1, g1, sh2, sc2, g2 = jnp.split(mod[:, None, :], 6, axis=-1)
        csh1, csc1, cg1, csh2, csc2, cg2 = jnp.split(mod_c[None, None, :], 6, axis=-1)

        hc = _modulate(_rmsnorm(ctx, norm1_g[li]), csh1, csc1)
        if last:
            pkv = hc @ w_in[li][:, OFF_K:OFF_HY]
            kc = pkv[..., :KV_W].reshape(B, C, N_KV_HEADS, HEAD_DIM)
            vc = pkv[..., KV_W:].reshape(B, C, N_KV_HEADS, HEAD_DIM)
        else:
            pc = hc @ w_in[li]
            qc = pc[..., :OFF_K].reshape(B, C, N_HEADS, HEAD_DIM)
            kc = pc[..., OFF_K:OFF_V].reshape(B, C, N_KV_HEADS, HEAD_DIM)
            vc = pc[..., OFF_V:OFF_HY].reshape(B, C, N_KV_HEADS, HEAD_DIM)
            yc_attn = _context_attention(qc, kc, vc, attn_sink[li])
            yc_hy = _hyena_branch(pc[..., OFF_HY:OFF_G], hy_conv_w[li], hy_conv_b[li],
                                  _hyena_filter(C, *hy), hy_skip[li])
            ctx_mid = ctx + cg1 * _branch_merge(pc, yc_attn, yc_hy, w_o_attn[li], w_o_hy[li], w_out[li])
            hc2 = _modulate(_rmsnorm(ctx_mid, norm2_g[li]), csh2, csc2)
            ctx_next = ctx_mid + cg2 * _peer_ffn(hc2, peer_wq[li], peer_keys[li], peer_u[li], peer_v[li])

        h = _modulate(_rmsnorm(x, norm1_g[li]), sh1, sc1)
        p = h @ w_in[li]
        q = _rope_2d(p[..., :OFF_K].reshape(B, L, N_HEADS, HEAD_DIM), row, col)
        k = _rope_2d(p[..., OFF_K:OFF_V].reshape(B, L, N_KV_HEADS, HEAD_DIM), row, col)
        v = p[..., OFF_V:OFF_HY].reshape(B, L, N_KV_HEADS, HEAD_DIM)
        y_attn = _window_attention(q, k, v, kc, vc, attn_sink[li])
        y_hy = _hyena_branch(p[..., OFF_HY:OFF_G], hy_conv_w[li], hy_conv_b[li],
                             _hyena_filter(L, *hy), hy_skip[li])
        x = x + g1 * _branch_merge(p, y_attn, y_hy, w_o_attn[li], w_o_hy[li], w_out[li])
        h2 = _modulate(_rmsnorm(x, norm2_g[li]), sh2, sc2)
        x = x + g2 * _peer_ffn(h2, peer_wq[li], peer_keys[li], peer_u[li], peer_v[li])
        if not last:
            ctx = ctx_next
    return _rmsnorm(x, final_g)
```

```python
from contextlib import ExitStack, contextmanager
import math
import numpy as np
import ml_dtypes
import concourse.bass as bass
import concourse.mybir as mybir
from concourse.bass_utils import run_bass_kernel_spmd

F32 = mybir.dt.float32
BF16 = mybir.dt.bfloat16
U32 = mybir.dt.uint32
F32R = mybir.dt.float32r
AF = mybir.ActivationFunctionType
ALU = mybir.AluOpType
AX = mybir.AxisListType
ENGS = ["sync", "scalar", "vector", "gpsimd", "tensor"]

L = 4096
D = 1024
NEG = -30000.0


class Prog:
    def __init__(self, nc, n_dma_sync=24, n_dma_pool=32):
        self.nc = nc
        self.es = ExitStack()
        self.stacks = [self.es]
        self.ops = {e: [] for e in ENGS}
        self.cnt = {e: 0 for e in ENGS}
        self.sem = {e: self.es.enter_context(nc.semaphore("s_" + e)) for e in ENGS}
        self.dpool = {
            "sync": [self.es.enter_context(nc.semaphore(f"ds{i}")) for i in range(n_dma_sync)],
            "gpsimd": [self.es.enter_context(nc.semaphore(f"dg{i}")) for i in range(n_dma_pool)],
        }
        self.dval = {q: [0] * len(p) for q, p in self.dpool.items()}
        self.dnext = {q: 0 for q in self.dpool}
        self.waited = {e: {} for e in ENGS}
        self.lastw = {}
        self.readers = {}
        self.uid = 0

    def sb(self, name, shape, dtype=F32):
        self.uid += 1
        return self.stacks[-1].enter_context(self.nc.sbuf_tensor(f"{name}_{self.uid}", list(shape), dtype))

    def ps(self, name, shape, dtype=F32):
        self.uid += 1
        return self.stacks[-1].enter_context(self.nc.psum_tensor(f"{name}_{self.uid}", list(shape), dtype))

    @contextmanager
    def scope(self):
        st = ExitStack()
        self.stacks.append(st)
        try:
            yield
        finally:
            self.barrier()
            self.stacks.pop()
            st.close()

    def all_tokens(self):
        toks = [("c" + e, self.sem[e], self.cnt[e]) for e in ENGS if self.cnt[e] > 0]
        for q, pool in self.dpool.items():
            for j, sem in enumerate(pool):
                if self.dval[q][j] > 0:
                    toks.append((f"d{q}{j}", sem, self.dval[q][j]))
        return toks

    def barrier(self):
        toks = self.all_tokens()
        for e in ENGS:
            waits = []
            for sid, sem, val in toks:
                if self.waited[e].get(sid, 0) < val:
                    waits.append((sem, val))
                    self.waited[e][sid] = val
            self.ops[e].append((waits, None, None))
        self.lastw = {}
        self.readers = {}

    def _deps(self, eng, reads, writes):
        toks = []
        for k in reads:
            t = self.lastw.get(k)
            if t is not None:
                toks.append(t)
        for k in writes:
            t = self.lastw.get(k)
            if t is not None:
                toks.append(t)
            toks.extend(self.readers.get(k, ()))
        wd = self.waited[eng]
        best = {}
        for (sid, sem, val) in toks:
            if sid == "c" + eng and val > self.cnt[eng]:
                continue
            if wd.get(sid, 0) < val and best.get(sid, (None, 0))[1] < val:
                best[sid] = (sem, val)
        waits = []
        for sid, (sem, val) in best.items():
            wd[sid] = val
            waits.append((sem, val))
        return waits

    def _commit(self, tok, reads, writes):
        for k in writes:
            self.lastw[k] = tok
            self.readers[k] = []
        for k in reads:
            self.readers.setdefault(k, []).append(tok)

    def op(self, eng, fn, reads=(), writes=(), silent=False):
        reads, writes = list(reads), list(writes)
        waits = self._deps(eng, reads, writes)
        if silent:
            tok = ("c" + eng, self.sem[eng], self.cnt[eng] + 1)
            self.ops[eng].append((waits, fn, None))
        else:
            self.cnt[eng] += 1
            tok = ("c" + eng, self.sem[eng], self.cnt[eng])
            self.ops[eng].append((waits, fn, (self.sem[eng], 1)))
        self._commit(tok, reads, writes)
        return tok

    def g(self, eng, name, reads, writes, *args, silent=False, **kw):
        return self.op(eng, lambda e: getattr(e, name)(*args, **kw), reads, writes, silent=silent)

    def dma(self, out, in_, reads=(), writes=(), q="sync", fn=None):
        reads, writes = list(reads), list(writes)
        waits = self._deps(q, reads, writes)
        j = self.dnext[q]
        self.dnext[q] = (j + 1) % len(self.dpool[q])
        sem = self.dpool[q][j]
        sid = f"d{q}{j}"
        prev = self.dval[q][j]
        if prev > 0 and self.waited[q].get(sid, 0) < prev:
            waits.append((sem, prev))
            self.waited[q][sid] = prev
        self.dval[q][j] = prev + 16
        tok = (sid, sem, prev + 16)
        if fn is None:
            fn = lambda e: e.dma_start(out=out, in_=in_)
        self.ops[q].append((waits, fn, (sem, 16)))
        self._commit(tok, reads, writes)
        return tok

    def gather(self, out, table, idx_ap, reads, writes):
        fn = lambda e: e.indirect_dma_start(out=out, out_offset=None, in_=table,
                                            in_offset=bass.IndirectOffsetOnAxis(ap=idx_ap, axis=0))
        return self.dma(None, None, reads, writes, q="gpsimd", fn=fn)

    def finish(self, tokens, eng="sync"):
        waits = []
        for (sid, sem, val) in tokens:
            if self.waited[eng].get(sid, 0) < val:
                waits.append((sem, val))
                self.waited[eng][sid] = val
        self.ops[eng].append((waits, None, None))

    def emit(self):
        with self.nc.Block() as block:
            def mk(engname):
                def body(e):
                    for waits, fn, inc in self.ops[engname]:
                        for sem, val in waits:
                            e.wait_ge(sem, val)
                        if fn is not None:
                            ins = fn(e)
                            if inc is not None:
                                ins.then_inc(inc[0], inc[1])
                return body
            block.sync(mk("sync"))
            block.scalar(mk("scalar"))
            block.vector(mk("vector"))
            block.gpsimd(mk("gpsimd"))
            block.tensor(mk("tensor"))
        for st in reversed(self.stacks):
            st.close()


CQ, CQS, CK, CKS, CV, CHY, CG = 0, 4, 8, 9, 10, 11, 23
NWCH = 39


def build_program(half, stop_after=None, dbg=False):
    nc = bass.Bass("TRN2", target_bir_lowering=False)

    def din(name, shape, dt=F32):
        return nc.dram_tensor(name, list(shape), dt, kind="ExternalInput").ap()

    x_d = din("x", [L, D]); ctx_d = din("ctx", [256, D]); cvec_d = din("cvec", [128, 16])
    wmod_d = din("w_mod", [12, 128, 8, 512]); bmod_d = din("b_mod", [128, 6144]); ng_d = din("ng", [128, 3, 1024])
    win_d = din("w_in", [NWCH, 128, 8, 128]); rope_d = din("rope", [2, 128, L]); sink_d = din("sink", [128, 8])
    convw_d = din("convw", [128, 12, 4]); fw1_d = din("fw1", [33, 64]); fvec_d = din("fvec", [64, 4])
    fw2_d = din("fw2", [64, 64]); fw3_d = din("fw3", [64, 1024]); fb3_d = din("fb3", [128, 1024])
    skip_d = din("skip", [128, 512]); zf_d = din("zf", [2, 33, L]); ndelta_d = din("ndelta", [128, 512])
    tpos_d = din("tpos", [128, 64]); woa_d = din("w_oa", [128, 4, 1024]); woh_d = din("w_oh", [128, 4, 1024])
    wout_d = din("w_out", [128, 8, 1024]); wq_d = din("wq", [128, 8, 1024]); kbd_d = din("kbd", [128, 8, 256])
    pu_d = din("peer_u", [16384, D]); pv_d = din("peer_v", [16384, D])
    dftc_d = din("dftc", [33, 128, 32, 128], BF16); dfts_d = din("dfts", [33, 128, 32, 128], BF16)
    idc_d = din("idftc", [33, 128, 2048], BF16); ids_d = din("idfts", [33, 128, 2048], BF16)
    cst_d = din("consts", [128, 1024]); cstb_d = din("constsb", [128, 128], BF16)
    out_d = nc.dram_tensor("out", [2048, D], F32, kind="ExternalOutput").ap()
    ya_d = nc.dram_tensor("ya_s", [4, 128, 2048], F32, kind="Internal").ap()
    yh_d = nc.dram_tensor("yh_s", [4, 128, 2048], F32, kind="Internal").ap()
    x0c_d = nc.dram_tensor("x0c_s", [4, 128, 2048], F32, kind="Internal").ap()
    xm_d = nc.dram_tensor("xm_s", [2048, D], F32, kind="Internal").ap()
    dbg_d = nc.dram_tensor("dbg", [128, 4096], F32, kind="ExternalOutput").ap() if dbg else None
    uv16_d = nc.dram_tensor("uv16_s", [16384, 2 * D], BF16, kind="Internal").ap()

    P = Prog(nc)
    own0 = 2048 * half
    oblk0 = 4 * half
    kblk0 = oblk0 - 1

    cst = P.sb("cst", [128, 1024])
    cstb = P.sb("cstb", [128, 128], BF16)
    P.dma(cst[:], cst_d, writes=["cst"]); P.dma(cstb[:], cstb_d, writes=["cstb"])
    ident = cst[:, 0:128]; bmask = cst[:, 128:512]; iota16 = cst[:, 512:528]; ones = cst[:, 528:656]
    MODX = P.sb("MODX", [128, 6144])
    FG = P.sb("FG", [128, 1024])
    P.dma(FG[:], ng_d[:, 2, :], writes=["FG"])
    outer = P.scope(); outer.__enter__()
    MODC = P.sb("MODC", [128, 2048])
    uT = P.sb("uT", [128, 4, L], BF16)

    def norm_tile(xt, np_, G, SH, out, key_x, key_out, tag):
        ss = P.sb("ss" + tag, [128, 4]) if tag not in norm_tile.cache else norm_tile.cache[tag]
        norm_tile.cache[tag] = ss
        k = "ss" + tag
        P.g("vector", "scalar_tensor_tensor", [key_x], [key_out, k], out=out[0:np_, :], in0=xt[0:np_, :], scalar=1.0, in1=xt[0:np_, :],
            op0=ALU.mult, op1=ALU.mult, accum_out=ss[0:np_, 0:1])
        P.g("vector", "tensor_scalar", [k], [k + "b"], out=ss[0:np_, 1:2], in0=ss[0:np_, 0:1], scalar1=1.0 / D, scalar2=1e-6,
            op0=ALU.mult, op1=ALU.add)
        P.g("scalar", "activation", [k + "b"], [k + "c"], out=ss[0:np_, 2:3], in_=ss[0:np_, 1:2], func=AF.Sqrt)
        P.g("vector", "reciprocal", [k + "c"], [k + "d"], out=ss[0:np_, 3:4], in_=ss[0:np_, 2:3])
        if SH is None:
            P.g("vector", "scalar_tensor_tensor", [key_x, k + "d"], [key_out], out=out[0:np_, :], in0=xt[0:np_, :],
                scalar=ss[0:np_, 3:4], in1=G[0:np_, :], op0=ALU.mult, op1=ALU.mult)
        else:
            P.g("vector", "scalar_tensor_tensor", [key_x, k + "d"], [key_out], out=out[0:np_, :], in0=xt[0:np_, :],
                scalar=ss[0:np_, 3:4], in1=G[0:np_, :], op0=ALU.mult, op1=ALU.mult)
            P.g("vector", "tensor_tensor", [key_out], [key_out], out=out[0:np_, :], in0=out[0:np_, :], in1=SH[0:np_, :], op=ALU.add)
    norm_tile.cache = {}

    with P.scope():
        cv = P.sb("cv", [128, 16]); cs = P.sb("cs", [128, 16]); CB = P.sb("CB", [128, 2, 8, 128])
        ngt = P.sb("ngt", [128, 2, 1024])
        wm = [P.sb("wm0", [128, 8, 512]), P.sb("wm1", [128, 8, 512])]
        bm = [P.sb("bm0", [128, 512]), P.sb("bm1", [128, 512])]
        pm = [P.ps("pm0", [128, 512]), P.ps("pm1", [128, 512])]
        P.dma(cv[:], cvec_d, writes=["cv"]); P.dma(ngt[:], ng_d[:, 0:2, :], writes=["ngt"])
        P.g("scalar", "activation", ["cv"], ["cs"], out=cs[:], in_=cv[:], func=AF.Silu)
        for v in range(2):
            P.g("vector", "tensor_copy", ["cs"], ["CB"], out=CB[:, v, :, :],
                in_=cs[:, v * 8:(v + 1) * 8].unsqueeze(2).to_broadcast([128, 8, 128]))
        it = 0
        for v, nblk, dst in ((0, 12, MODX), (1, 4, MODC)):
            for j in range(nblk):
                b = it % 2; it += 1
                P.dma(wm[b][:], wmod_d[j], writes=[f"wm{b}"])
                P.dma(bm[b][:], bmod_d[:, j * 512:(j + 1) * 512], writes=[f"bm{b}"])
                for c in range(8):
                    P.g("tensor", "matmul", ["CB", f"wm{b}"], [f"pm{b}"], pm[b][:], lhsT=CB[:, v, c, :], rhs=wm[b][:, c, :],
                        start=(c == 0), stop=(c == 7), silent=(c != 7))
                P.g("vector", "tensor_tensor", [f"pm{b}", f"bm{b}"], ["MOD"], out=dst[:, j * 512:(j + 1) * 512], in0=pm[b][:],
                    in1=bm[b][:], op=ALU.add)
        for dst, lo, gi in ((MODX, 1024, 0), (MODX, 4096, 1), (MODC, 1024, 0)):
            P.g("vector", "scalar_tensor_tensor", ["MOD", "ngt"], ["MOD"], out=dst[:, lo:lo + 1024], in0=dst[:, lo:lo + 1024],
                scalar=1.0, in1=ngt[:, gi, :], op0=ALU.add, op1=ALU.mult)
    SH1 = MODX[:, 0:1024]; G1 = MODX[:, 1024:2048]; g1 = MODX[:, 2048:3072]
    SH2 = MODX[:, 3072:4096]; G2 = MODX[:, 4096:5120]; g2 = MODX[:, 5120:6144]
    if dbg and stop_after == 0:
        t = P.dma(dbg_d[:, 0:4096], MODX[:, 0:4096], reads=[])
        P.finish(P.all_tokens()); P.emit(); return nc

    with P.scope():
        QT = P.sb("QT", [128, 4, 2048]); KT = P.sb("KT", [128, 6 * 512]); V = P.sb("V", [128, 24, 128], F32R)
        CKT = P.sb("CKT", [128, 256]); CVt = P.sb("CV", [128, 2, 128], F32R)
        sinkt = P.sb("sinkt", [128, 8]); convw = P.sb("convw", [128, 12, 4])
        P.dma(sinkt[:], sink_d, writes=["sink"]); P.dma(convw[:], convw_d, writes=["convw"])
        with P.scope():
            xt = [P.sb("xt0", [128, 1024]), P.sb("xt1", [128, 1024])]
            ht = [P.sb("ht0", [128, 1024]), P.sb("ht1", [128, 1024])]
            hT = P.sb("hT", [128, 8, 514], F32R)
            Wf = [P.sb(f"Wf{i}", [128, 8, 128]) for i in range(3)]
            W = [P.sb(f"W{i}", [128, 8, 128], F32R) for i in range(3)]
            rp = P.sb("rp", [128, 2, 512])
            ZT = [P.sb(f"ZT{i}", [128, 514]) for i in range(2)]
            ca = P.sb("ca", [128, 512]); cb = P.sb("cb", [128, 512]); x0s_ = P.sb("x0s0", [128, 512]); x0s = [x0s_, x0s_]
            tq = P.sb("tq", [128, 512]); tq2 = P.sb("tq2", [128, 512])
            pT = P.ps("pT", [128, 1024]); pA = [P.ps("pA0", [128, 1024]), P.ps("pA1", [128, 1024])]
            pB0_ = P.ps("pB0", [128, 512]); pB = [pB0_, pB0_]
            wcnt = [0]; zcnt = [0]; acnt = [0]; xcnt = [0]

            def load_w(ci):
                i = wcnt[0] % 3; wcnt[0] += 1
                P.dma(Wf[i][:], win_d[ci], writes=[f"Wf{i}"])
                if wcnt[0] % 3 == 0:
                    P.g("gpsimd", "tensor_copy", [f"Wf{i}"], [f"W{i}"], out=W[i][:], in_=Wf[i][:])
                else:
                    P.g("scalar", "activation", [f"Wf{i}"], [f"W{i}"], out=W[i][:], in_=Wf[i][:], func=AF.Copy)
                return W[i], f"W{i}"

            def make_hT(src_d, row0, ntile, G, SH, col0):
                for t in range(ntile):
                    b = xcnt[0] % 2; xcnt[0] += 1
                    P.dma(xt[b][:], src_d[row0 + t * 128: row0 + (t + 1) * 128, :], writes=[f"xt{b}"])
                    norm_tile(xt[b], 128, G, SH, ht[b], f"xt{b}", f"ht{b}", "m")
                    for c in range(8):
                        P.g("tensor", "transpose", [f"ht{b}", "cst"], ["pT"], out=pT[:, c * 128:(c + 1) * 128],
                            in_=ht[b][:, c * 128:(c + 1) * 128], identity=ident, silent=(c != 7))
                    P.g("scalar", "activation", ["pT"], ["hT"], out=hT[:, :, col0 + t * 128: col0 + (t + 1) * 128],
                        in_=pT[:].rearrange("p (c n) -> p c n", c=8), func=AF.Copy)

            def proj_fm(ci, n0, n1, pdst, pkey):
                w, wk = load_w(ci)
                for c in range(8):
                    P.g("tensor", "matmul", [wk, "hT"], [pkey], pdst, lhsT=w[:, c, :], rhs=hT[:, c, n0:n1],
                        start=(c == 0), stop=(c == 7), silent=(c != 7))

            def rope_fm(ci, cis, dst, dkey, tcol):
                a = acnt[0] % 2; acnt[0] += 1
                proj_fm(ci, 1, 513, pA[a][:, 0:512], f"pA{a}")
                proj_fm(cis, 1, 513, pB[a][:], "pB0")
                P.g("scalar", "activation", ["pB0"], ["tq"], out=tq[:], in_=pB[a][:], func=AF.Copy)
                P.g("vector", "tensor_tensor", ["tq", "rp"], ["tq"], out=tq[:], in0=tq[:], in1=rp[:, 1, :], op=ALU.mult)
                P.g("vector", "tensor_tensor", [f"pA{a}", "rp"], ["tq2"], out=tq2[:], in0=pA[a][:, 0:512], in1=rp[:, 0, :], op=ALU.mult)
                P.g("vector", "tensor_tensor", ["tq", "tq2"], [dkey], out=dst, in0=tq[:], in1=tq2[:], op=ALU.add)

            def conv3(z, zk, ci, dst, dkey):
                cw = convw[:, ci - CHY, :]
                P.g("vector", "tensor_scalar", [zk, "convw"], [dkey], out=dst, in0=z[:, 1:513], scalar1=cw[:, 1:2], scalar2=cw[:, 3:4],
                    op0=ALU.mult, op1=ALU.add)
                P.g("vector", "scalar_tensor_tensor", [zk, "convw", dkey], [dkey], out=dst, in0=z[:, 0:512], scalar=cw[:, 0:1], in1=dst,
                    op0=ALU.mult, op1=ALU.add)
                P.g("vector", "scalar_tensor_tensor", [zk, "convw", dkey], [dkey], out=dst, in0=z[:, 2:514], scalar=cw[:, 2:3], in1=dst,
                    op0=ALU.mult, op1=ALU.add)

            def proj_z(ci):
                a = acnt[0] % 2; acnt[0] += 1
                zi = zcnt[0] % 2; zcnt[0] += 1
                w, wk = load_w(ci)
                for (c0, c1, o0) in ((0, 258, 0), (258, 514, 512)):
                    for c in range(8):
                        P.g("tensor", "matmul", [wk, "hT"], [f"pA{a}"], pA[a][:, o0:o0 + (c1 - c0)], lhsT=w[:, c, :],
                            rhs=hT[:, c, c0:c1], start=(c == 0), stop=(c == 7), silent=(c != 7))
                P.g("scalar", "activation", [f"pA{a}"], [f"ZT{zi}"], out=ZT[zi][:, 0:258], in_=pA[a][:, 0:258], func=AF.Copy)
                P.g("scalar", "activation", [f"pA{a}"], [f"ZT{zi}"], out=ZT[zi][:, 258:514], in_=pA[a][:, 512:768], func=AF.Copy)
                return ZT[zi], f"ZT{zi}"

            make_hT(ctx_d, 0, 2, MODC[:, 1024:2048], MODC[:, 0:1024], 1)
            proj_fm(CK, 1, 257, pA[0][:, 0:256], "pA0")
            P.g("scalar", "activation", ["pA0"], ["CKT"], out=CKT[:], in_=pA[0][:, 0:256], func=AF.Copy)
            w, wk = load_w(CV)
            for t in range(2):
                for c in range(8):
                    P.g("tensor", "matmul", [wk, "hT"], ["pB0"], pB[0][:, t * 128:(t + 1) * 128], lhsT=hT[:, c, 1 + t * 128: 1 + (t + 1) * 128],
                        rhs=w[:, c, :], start=(c == 0), stop=(c == 7), silent=(c != 7))
            P.g("scalar", "activation", ["pB0"], ["CV"], out=CVt[:], in_=pB[0][:, 0:256].rearrange("p (t n) -> p t n", t=2), func=AF.Copy)

            pTh = P.ps("pTh", [128, 16])
            for B in range(8):
                t0 = B * 512
                own = oblk0 <= B < oblk0 + 4
                kv = kblk0 <= B < kblk0 + 6
                make_hT(x_d, t0, 4, G1, SH1, 1)
                r0 = max(t0 - 1, 0); r1 = min(t0 + 512, L - 1)
                hb = xcnt[0] % 2; xcnt[0] += 1
                hx = xt[hb]; hh = ht[hb]
                P.dma(hx[0:1, :], x_d[r0:r0 + 1, :], writes=[f"xt{hb}"]); P.dma(hx[1:2, :], x_d[r1:r1 + 1, :], writes=[f"xt{hb}"])
                norm_tile(hx, 2, G1, SH1, hh, f"xt{hb}", f"ht{hb}", "h")
                for c in range(8):
                    P.g("tensor", "transpose", [f"ht{hb}", "cst"], ["pTh"], out=pTh[:, c * 2:(c + 1) * 2], in_=hh[0:2, c * 128:(c + 1) * 128],
                        identity=cst[0:2, 0:2], silent=(c != 7))
                pv = pTh[:].rearrange("p (c n) -> p c n", c=8)
                P.g("vector", "tensor_copy", ["pTh"], ["hT"], out=hT[:, :, 0:1], in_=pv[:, :, 0:1])
                P.g("vector", "tensor_copy", ["pTh"], ["hT"], out=hT[:, :, 513:514], in_=pv[:, :, 1:2])
                if B == 0:
                    P.g("vector", "tensor_copy", ["cst"], ["hT"], out=hT[:, :, 0:1], in_=cst[:, 664:672].unsqueeze(2))
                if B == 7:
                    P.g("vector", "tensor_copy", ["cst"], ["hT"], out=hT[:, :, 513:514], in_=cst[:, 664:672].unsqueeze(2))
                if own or kv:
                    P.dma(rp[:], rope_d[:, :, t0:t0 + 512].rearrange("a p n -> p a n"), writes=["rp"])
                if kv:
                    kc0 = (B - kblk0) * 512
                    rope_fm(CK, CKS, KT[:, kc0:kc0 + 512], "KT", t0)
                    w, wk = load_w(CV)
                    for t in range(4):
                        for c in range(8):
                            P.g("tensor", "matmul", [wk, "hT"], ["pB0"], pB[0][:, t * 128:(t + 1) * 128],
                                lhsT=hT[:, c, 1 + t * 128: 1 + (t + 1) * 128], rhs=w[:, c, :], start=(c == 0), stop=(c == 7), silent=(c != 7))
                    vt0 = (B - kblk0) * 4
                    P.g("scalar", "activation", ["pB0"], ["V"], out=V[:, vt0:vt0 + 4, :], in_=pB[0][:].rearrange("p (t n) -> p t n", t=4),
                        func=AF.Copy)
                if own:
                    oc0 = t0 - own0
                    for c in range(4):
                        rope_fm(CQ + c, CQS + c, QT[:, c, oc0:oc0 + 512], "QT", t0)
                    for c in range(4):
                        z, zk = proj_z(CHY + c)
                        b = 0
                        conv3(z, zk, CHY + c, x0s[b][:], f"x0s{b}")
                        P.dma(x0c_d[c, :, oc0:oc0 + 512], x0s[b][:], reads=[f"x0s{b}"], writes=["x0c_d"])
                for c in range(4):
                    z1, z1k = proj_z(CHY + 4 + c)
                    z2, z2k = proj_z(CHY + 8 + c)
                    conv3(z1, z1k, CHY + 4 + c, ca[:], "ca")
                    conv3(z2, z2k, CHY + 8 + c, cb[:], "cb")
                    P.g("vector", "tensor_tensor", ["ca", "cb"], ["uT"], out=uT[:, c, t0:t0 + 512], in0=ca[:], in1=cb[:], op=ALU.mult)
        if dbg and stop_after == 1:
            P.dma(dbg_d[:, 0:2048], QT[:, 0, :], reads=["QT"]); P.dma(dbg_d[:, 2048:4096], KT[:, 512:2560], reads=["KT"])
            P.finish(P.all_tokens()); P.emit(); return nc

        with P.scope():
            Sm = [P.sb("Sm0", [128, 648]), P.sb("Sm1", [128, 648])]
            Pm = [P.sb("Pm0", [128, 648]), P.sb("Pm1", [128, 648])]
            PTs = [P.sb("PTs0", [128, 640], F32R), P.sb("PTs1", [128, 640], F32R)]
            st = [P.sb("st0", [128, 4]), P.sb("st1", [128, 4])]
            Ysb = P.sb("Ysb", [128, 512]); YAs = [P.sb("YAs0", [128, 4, 128]), P.sb("YAs1", [128, 4, 128])]
            pS = [P.ps("pS0", [128, 1024]), P.ps("pS1", [128, 1024])]
            pPT0_ = P.ps("pPT0", [128, 1024]); pPT = [pPT0_, pPT0_]
            pY = P.ps("pY", [128, 512]); pYT = P.ps("pYT", [128, 512])
            for b in range(2):
                P.g("vector", "memset", [], [f"Sm{b}"], Sm[b][:], NEG)
            P.g("vector", "tensor_copy", ["cst"], ["KT"], out=KT[:, 0:512], in_=cst[:, 664:665].to_broadcast([128, 512]))
            it = 0
            cin = [P.sb(f"cin{i}", [128, 2, 1024]) for i in range(3)]
            cou = [P.sb(f"cou{i}", [128, 2, 1024], BF16) for i in range(3)]
            cast_steps = [(src, dst, r) for (src, dst) in ((pu_d, uv16_d[:, 0:D]), (pv_d, uv16_d[:, D:2 * D])) for r in range(64)]

            def cast_step(k):
                src, dst, r = cast_steps[k]
                i = k % 3
                P.dma(cin[i][:], src[r * 256:(r + 1) * 256, :].rearrange("(a p) d -> p a d", p=128), writes=[f"cin{i}"])
                P.g("scalar", "activation", [f"cin{i}"], [f"cou{i}"], out=cou[i][:], in_=cin[i][:], func=AF.Copy)
                P.dma(dst[r * 256:(r + 1) * 256, :].rearrange("(a p) d -> p a d", p=128), cou[i][:], reads=[f"cou{i}"], writes=["p16"])
            def geom(n):
                gt = 16 * half + n
                lo = 128 if gt == 0 else 0
                hi = 256 if gt == 31 else 384
                kc0 = (gt - 1) * 128 - kblk0 * 512
                vt0 = (gt - 1) - kblk0 * 4
                return lo, hi, kc0, vt0

            def stage_a(i):
                n, hd = divmod(i, 8)
                lo, hi, kc0, vt0 = geom(n)
                cast_step(i)
                b = i % 2
                c = hd % 4; po = (hd // 4) * 64
                qv = QT[po:po + 64, c, n * 128:(n + 1) * 128]
                P.g("tensor", "matmul", ["QT", "KT"], [f"pS{b}"], pS[b][:, 0:hi], lhsT=qv, rhs=KT[po:po + 64, kc0: kc0 + hi],
                    start=True, stop=True, silent=True)
                P.g("tensor", "matmul", ["QT", "CKT"], [f"pS{b}"], pS[b][:, 512:768], lhsT=qv, rhs=CKT[po:po + 64, :],
                    start=True, stop=True)
                if lo > 0:
                    P.g("vector", "memset", [], [f"Sm{b}"], Sm[b][:, 0:lo], NEG)
                if hi < 384:
                    P.g("vector", "memset", [], [f"Sm{b}"], Sm[b][:, hi:384], NEG)
                P.g("vector", "scalar_tensor_tensor", [f"pS{b}", "cst"], [f"Sm{b}"], out=Sm[b][:, lo:hi], in0=pS[b][:, lo:hi],
                    scalar=0.125, in1=bmask[:, lo:hi], op0=ALU.mult, op1=ALU.add)
                P.g("scalar", "activation", [f"pS{b}"], [f"Sm{b}"], out=Sm[b][:, 384:640], in_=pS[b][:, 512:768], func=AF.Copy, scale=0.125)
                P.g("vector", "tensor_copy", ["sink"], [f"Sm{b}"], out=Sm[b][:, 640:641], in_=sinkt[:, hd:hd + 1])
                P.g("vector", "tensor_reduce", [f"Sm{b}"], [f"st{b}"], out=st[b][:, 0:1], in_=Sm[b][:, 0:641], axis=AX.X, op=ALU.max, negate=True)

            def stage_b(i):
                n, hd = divmod(i, 8)
                lo, hi, kc0, vt0 = geom(n)
                b = i % 2
                po = (hd // 4) * 64
                P.g("scalar", "activation", [f"Sm{b}", f"st{b}"], [f"Pm{b}", f"st{b}b"], out=Pm[b][:, 0:641], in_=Sm[b][:, 0:641], func=AF.Exp,
                    bias=st[b][:, 0:1], scale=1.0, accum_out=st[b][:, 1:2])
                P.g("vector", "reciprocal", [f"st{b}b"], [f"st{b}c"], out=st[b][:, 2:3], in_=st[b][:, 1:2])
                P.g("vector", "tensor_scalar", [f"Pm{b}", f"st{b}c"], [f"Pm{b}"], out=Pm[b][:, 0:640], in0=Pm[b][:, 0:640], scalar1=st[b][:, 2:3],
                    scalar2=None, op0=ALU.mult)
                for kt in range(5):
                    P.g("tensor", "transpose", [f"Pm{b}", "cst"], ["pPT0"], out=pPT[b][:, kt * 128:(kt + 1) * 128],
                        in_=Pm[b][:, kt * 128:(kt + 1) * 128], identity=ident, silent=(kt != 4))
                P.g("scalar", "activation", ["pPT0"], [f"PTs{b}"], out=PTs[b][:], in_=pPT[b][:, 0:640], func=AF.Copy)
                mms = []
                for kt in range(3):
                    if lo <= kt * 128 < hi:
                        mms.append((kt, V[:, vt0 + kt, po:po + 64], "V"))
                mms.append((3, CVt[:, 0, po:po + 64], "CV")); mms.append((4, CVt[:, 1, po:po + 64], "CV"))
                for j, (kt, rv, rk) in enumerate(mms):
                    P.g("tensor", "matmul", [f"PTs{b}", rk], ["pY"], pY[:, hd * 64:(hd + 1) * 64], lhsT=PTs[b][:, kt * 128:(kt + 1) * 128], rhs=rv,
                        start=(j == 0), stop=(j == len(mms) - 1), silent=(j != len(mms) - 1))
                if hd == 7:
                    P.g("vector", "tensor_copy", ["pY"], ["Ysb"], out=Ysb[:], in_=pY[:])
                    for c in range(4):
                        P.g("tensor", "transpose", ["Ysb", "cst"], ["pYT"], out=pYT[:, c * 128:(c + 1) * 128], in_=Ysb[:, c * 128:(c + 1) * 128], identity=ident, silent=(c != 3))
                    yb = n % 2
                    P.g("scalar", "activation", ["pYT"], [f"YAs{yb}"], out=YAs[yb][:], in_=pYT[:].rearrange("p (c n) -> p c n", c=4), func=AF.Copy)
                    P.dma(ya_d[:, :, n * 128:(n + 1) * 128].rearrange("c p n -> p c n"), YAs[yb][:], reads=[f"YAs{yb}"], writes=["ya_d"])

            stage_a(0)
            for i in range(128):
                if i + 1 < 128:
                    stage_a(i + 1)
                stage_b(i)
    if dbg and stop_after == 2:
        P.dma(dbg_d[:, 0:2048], ya_d[0], reads=[]); P.dma(dbg_d[:, 2048:4096], ya_d[3], reads=[])
        P.finish(P.all_tokens()); P.emit(); return nc

    with P.scope():
        fw1 = P.sb("fw1", [33, 64]); fvec = P.sb("fvec", [64, 8]); fw2 = P.sb("fw2", [64, 64]); fw3 = P.sb("fw3", [64, 1024])
        tpos = P.sb("tpos", [128, 64])
        P.dma(fw1[:], fw1_d, writes=["fw"]); P.dma(fvec[:, 0:4], fvec_d, writes=["fvec"]); P.dma(fw2[:], fw2_d, writes=["fw"])
        P.dma(fw3[:], fw3_d, writes=["fw"]); P.dma(tpos[:], tpos_d, writes=["tpos"])
        P.g("vector", "tensor_tensor", ["fvec"], ["fvec2"], out=fvec[:, 4:5], in0=fvec[:, 0:1], in1=fvec[:, 2:3], op=ALU.mult)
        P.g("vector", "tensor_tensor", ["fvec"], ["fvec2"], out=fvec[:, 5:6], in0=fvec[:, 1:2], in1=fvec[:, 2:3], op=ALU.mult)
        RH = P.sb("RH", [128, 32, 768], BF16)
        YFr = P.sb("YFr", [128, 33, 256], BF16); YFi = P.sb("YFi", [128, 33, 256], BF16)
        rinv = P.sb("rinv", [128, 256])
        for ps_ in range(2):
            ch0 = ps_ * 256
            with P.scope():
                pU = [P.ps("pU0", [128, 256], BF16), P.ps("pU1", [128, 256], BF16)]
                for s in range(32):
                    b = s % 2
                    for c2 in range(2):
                        P.g("tensor", "transpose", ["uT", "cstb"], [f"pU{b}"], out=pU[b][:, c2 * 128:(c2 + 1) * 128],
                            in_=uT[:, ps_ * 2 + c2, s * 128:(s + 1) * 128], identity=cstb[:], silent=(c2 != 1))
                    P.g("scalar", "activation", [f"pU{b}"], ["RHu"], out=RH[:, s, 256:512], in_=pU[b][:], func=AF.Copy)
            with P.scope():
                zf = [P.sb("zf0", [33, 512]), P.sb("zf1", [33, 512])]
                h1 = [P.sb("h1a", [64, 512]), P.sb("h1b", [64, 512])]; h2 = [P.sb("h2f", [64, 512]), P.sb("h2b", [64, 512])]
                wa = [P.sb("wa0", [64, 512]), P.sb("wa1", [64, 512])]; wb = [P.sb("wb0", [64, 512]), P.sb("wb1", [64, 512])]
                fb3p = P.sb("fb3p", [128, 512]); ndl = P.sb("ndl", [128, 256])
                dec = [P.sb("dec0", [128, 512]), P.sb("dec1", [128, 512])]
                hfd = [P.sb("hfd0", [128, 512]), P.sb("hfd1", [128, 512])]; ab = [P.sb("ab0", [128, 512]), P.sb("ab1", [128, 512])]
                pF = [P.ps("pF0", [128, 512]), P.ps("pF1", [128, 512])]; pH = [P.ps("pH0", [128, 512]), P.ps("pH1", [128, 512])]; pN = P.ps("pN", [128, 256])
                P.dma(fb3p[:, 0:256], fb3_d[:, ch0:ch0 + 256], writes=["fb3p"]); P.dma(fb3p[:, 256:512], fb3_d[:, 512 + ch0:512 + ch0 + 256], writes=["fb3p"])
                P.dma(ndl[:], ndelta_d[:, ch0:ch0 + 256], writes=["ndl"])

                def sin_layer(v, bias_col, dst, dkey):
                    A_, B_ = wa[v], wb[v]
                    P.g("vector", "tensor_scalar", [f"pF{v}", "fvec", "fvec2"], [f"wa{v}"], out=A_[:], in0=pF[v][0:64, :], scalar1=fvec[:, 2:3], scalar2=fvec[:, bias_col:bias_col + 1],
                        op0=ALU.mult, op1=ALU.add)
                    P.g("vector", "tensor_scalar", [f"wa{v}"], [f"wb{v}"], out=B_[:], in0=A_[:], scalar1=-math.pi, scalar2=2 * math.pi, op0=ALU.is_lt, op1=ALU.mult)
                    P.g("vector", "tensor_tensor", [f"wa{v}", f"wb{v}"], [f"wb{v}"], out=B_[:], in0=A_[:], in1=B_[:], op=ALU.add)
                    P.g("vector", "tensor_scalar", [f"wa{v}"], [f"wa{v}"], out=A_[:], in0=A_[:], scalar1=math.pi, scalar2=-2 * math.pi, op0=ALU.is_gt, op1=ALU.mult)
                    P.g("vector", "tensor_tensor", [f"wa{v}", f"wb{v}"], [f"wb{v}"], out=B_[:], in0=A_[:], in1=B_[:], op=ALU.add)
                    P.g("scalar", "activation", [f"wb{v}"], [dkey], out=dst, in_=B_[:], func=AF.Sin)

                for nb in range(8):
                    for v in range(2):
                        P.dma(zf[v][:], zf_d[v, :, nb * 512:(nb + 1) * 512], writes=[f"zf{v}"])
                        P.g("tensor", "matmul", ["fw", f"zf{v}"], [f"pF{v}"], pF[v][0:64, :], lhsT=fw1[:], rhs=zf[v][:], start=True, stop=True)
                    for v in range(2):
                        sin_layer(v, 4, h1[v][:], f"h1{v}")
                    for v in range(2):
                        P.g("tensor", "matmul", ["fw", f"h1{v}"], [f"pF{v}"], pF[v][0:64, :], lhsT=fw2[:], rhs=h1[v][:], start=True, stop=True)
                    for v in range(2):
                        sin_layer(v, 5, h2[v][:], f"h2{v}")
                    for q in range(4):
                        s = nb * 4 + q; b = s % 2
                        P.g("tensor", "matmul", ["fw", "h20"], [f"pH{b}"], pH[b][:, 0:256], lhsT=h2[0][:, q * 128:(q + 1) * 128], rhs=fw3[:, ch0:ch0 + 256], start=True, stop=True, silent=True)
                        P.g("tensor", "matmul", ["fw", "h21"], [f"pH{b}"], pH[b][:, 256:512], lhsT=h2[1][:, q * 128:(q + 1) * 128], rhs=fw3[:, 512 + ch0:512 + ch0 + 256],
                            start=True, stop=True)
                        P.g("scalar", "activation", ["ndl", "tpos"], [f"dec{b}"], out=dec[b][:, 0:256], in_=ndl[:], func=AF.Exp, scale=tpos[:, s:s + 1])
                        P.g("scalar", "activation", ["ndl", "tpos"], [f"dec{b}"], out=dec[b][:, 256:512], in_=ndl[:], func=AF.Exp, scale=tpos[:, 32 + s:33 + s])
                        P.g("vector", "tensor_tensor", [f"pH{b}", "fb3p"], [f"hfd{b}"], out=hfd[b][:], in0=pH[b][:], in1=fb3p[:], op=ALU.add)
                        P.g("vector", "tensor_tensor", [f"hfd{b}", f"dec{b}"], [f"hfd{b}"], out=hfd[b][:], in0=hfd[b][:], in1=dec[b][:], op=ALU.mult)
                        if s == 0:
                            P.g("vector", "memset", [], [f"hfd{b}"], hfd[b][0:1, 256:512], 0.0)
                        P.g("scalar", "activation", [f"hfd{b}"], [f"ab{b}"], out=ab[b][:], in_=hfd[b][:], func=AF.Abs)
                        P.g("tensor", "matmul", [f"ab{b}", "cst"], ["pN"], pN[:], lhsT=ones, rhs=ab[b][:, 0:256], start=(s == 0), stop=False, silent=True)
                        P.g("tensor", "matmul", [f"ab{b}", "cst"], ["pN"], pN[:], lhsT=ones, rhs=ab[b][:, 256:512], start=False, stop=(s == 31))
                        P.g("vector", "tensor_tensor", [f"hfd{b}"], ["RHke"], out=RH[:, s, 0:256], in0=hfd[b][:, 0:256], in1=hfd[b][:, 256:512], op=ALU.add)
                        P.g("gpsimd", "tensor_tensor", [f"hfd{b}"], ["RHko"], out=RH[:, s, 512:768], in0=hfd[b][:, 0:256], in1=hfd[b][:, 256:512], op=ALU.subtract)
                P.g("vector", "reciprocal", ["pN"], ["rinv"], out=rinv[:], in_=pN[:])
            with P.scope():
                TC = [P.sb("TC0", [128, 32, 128], BF16), P.sb("TC1", [128, 32, 128], BF16)]
                TS = [P.sb("TS0", [128, 32, 128], BF16), P.sb("TS1", [128, 32, 128], BF16)]
                skp = P.sb("skp", [128, 256]); Ap = P.sb("Ap", [128, 256]); Bp = P.sb("Bp", [128, 256])
                t1 = P.sb("t1", [128, 256]); t2 = P.sb("t2", [128, 256])
                pC = [P.ps("pC0", [128, 512]), P.ps("pC1", [128, 512])]; pSn = [P.ps("pSn0", [128, 512]), P.ps("pSn1", [128, 512])]
                P.dma(skp[:], skip_d[:, ch0:ch0 + 256], writes=["skp"])
                for j in range(33):
                    b = j % 2
                    P.dma(TC[b][:], dftc_d[j], writes=[f"TC{b}"]); P.dma(TS[b][:], dfts_d[j], writes=[f"TS{b}"])
                    for s in range(32):
                        P.g("tensor", "matmul", [f"TC{b}", "RHu", "RHke"], [f"pC{b}"], pC[b][:], lhsT=TC[b][:, s, :], rhs=RH[:, s, 0:512], start=(s == 0), stop=(s == 31), silent=(s != 31))
                    for s in range(32):
                        P.g("tensor", "matmul", [f"TS{b}", "RHu", "RHko"], [f"pSn{b}"], pSn[b][:], lhsT=TS[b][:, s, :], rhs=RH[:, s, 256:768], start=(s == 0), stop=(s == 31), silent=(s != 31))
                    P.g("vector", "tensor_tensor", [f"pC{b}", "rinv"], ["Ap"], out=Ap[:], in0=pC[b][:, 0:256], in1=rinv[:], op=ALU.mult)
                    P.g("vector", "tensor_tensor", ["Ap", "skp"], ["Ap"], out=Ap[:], in0=Ap[:], in1=skp[:], op=ALU.add)
                    P.g("vector", "tensor_tensor", [f"pSn{b}", "rinv"], ["Bp"], out=Bp[:], in0=pSn[b][:, 256:512], in1=rinv[:], op=ALU.mult)
                    P.g("vector", "tensor_scalar", ["Bp", "cst"], ["Bp"], out=Bp[:], in0=Bp[:], scalar1=cst[:, 656:657], scalar2=None, op0=ALU.mult)
                    P.g("vector", "tensor_tensor", [f"pC{b}", "Ap"], ["t1"], out=t1[:], in0=pC[b][:, 256:512], in1=Ap[:], op=ALU.mult)
                    P.g("vector", "tensor_tensor", [f"pSn{b}", "Bp"], ["t2"], out=t2[:], in0=pSn[b][:, 0:256], in1=Bp[:], op=ALU.mult)
                    P.g("vector", "tensor_tensor", ["t1", "t2"], ["YF"], out=YFr[:, j, :], in0=t1[:], in1=t2[:], op=ALU.subtract)
                    P.g("vector", "tensor_tensor", [f"pC{b}", "Bp"], ["t1"], out=t1[:], in0=pC[b][:, 256:512], in1=Bp[:], op=ALU.mult)
                    P.g("vector", "tensor_tensor", [f"pSn{b}", "Ap"], ["t2"], out=t2[:], in0=pSn[b][:, 0:256], in1=Ap[:], op=ALU.mult)
                    P.g("vector", "tensor_tensor", ["t1", "t2"], ["YF"], out=YFi[:, j, :], in0=t1[:], in1=t2[:], op=ALU.add)
            with P.scope():
                IC = [P.sb("IC0", [128, 2048], BF16), P.sb("IC1", [128, 2048], BF16)]
                IS = [P.sb("IS0", [128, 2048], BF16), P.sb("IS1", [128, 2048], BF16)]
                x0c = P.sb("x0c", [128, 2048]); yst = [P.sb("yst0", [128, 512]), P.sb("yst1", [128, 512])]
                pO = [[P.ps(f"pO{cc}{tb}", [128, 512]) for tb in range(4)] for cc in range(2)]
                for kc in range(33):
                    b = kc % 2
                    P.dma(IC[b][:], idc_d[kc], writes=[f"IC{b}"]); P.dma(IS[b][:], ids_d[kc], writes=[f"IS{b}"])
                    for cc in range(2):
                        for tb in range(4):
                            P.g("tensor", "matmul", ["YF", f"IC{b}"], [f"pO{cc}{tb}"], pO[cc][tb][:], lhsT=YFr[:, kc, cc * 128:(cc + 1) * 128],
                                rhs=IC[b][:, tb * 512:(tb + 1) * 512], start=(kc == 0), stop=False, silent=True)
                            P.g("tensor", "matmul", ["YF", f"IS{b}"], [f"pO{cc}{tb}"], pO[cc][tb][:], lhsT=YFi[:, kc, cc * 128:(cc + 1) * 128],
                                rhs=IS[b][:, tb * 512:(tb + 1) * 512], start=False, stop=(kc == 32), silent=not (cc == 1 and tb == 3))
                i = 0
                for cc in range(2):
                    cg = ps_ * 2 + cc
                    P.dma(x0c[:], x0c_d[cg], reads=["x0c_d"], writes=["x0c"])
                    for tb in range(4):
                        b = i % 2; i += 1
                        P.g("vector", "tensor_tensor", [f"pO{cc}{tb}", "x0c"], [f"yst{b}"], out=yst[b][:], in0=pO[cc][tb][:], in1=x0c[:, tb * 512:(tb + 1) * 512], op=ALU.mult)
                        P.dma(yh_d[cg, :, tb * 512:(tb + 1) * 512], yst[b][:], reads=[f"yst{b}"], writes=["yh_d"])
    if dbg and stop_after == 3:
        P.dma(dbg_d[:, 0:2048], yh_d[0], reads=[]); P.dma(dbg_d[:, 2048:4096], yh_d[3], reads=[])
        P.finish(P.all_tokens()); P.emit(); return nc

    outer.__exit__(None, None, None)
    with P.scope():
        woa = P.sb("woa", [128, 4, 1024], F32R); woh = P.sb("woh", [128, 4, 1024], F32R); wout = P.sb("wout", [128, 8, 1024], F32R)
        Wf = [P.sb(f"Wf{i}", [128, 8, 128]) for i in range(2)]
        W = [P.sb(f"W{i}", [128, 8, 128], F32R) for i in range(2)]
        k_ = 0
        for (wt_, wd_, wk_, nk_) in ((woa, woa_d, "woa", 4), (woh, woh_d, "woh", 4), (wout, wout_d, "wout", 8)):
            for c in range(nk_):
                i = k_ % 2; k_ += 1
                P.dma(Wf[i][:].rearrange("p a b -> p (a b)"), wd_[:, c, :], writes=[f"Wf{i}"])
                P.g("gpsimd", "tensor_copy", [f"Wf{i}"], [wk_], out=wt_[:, c, :], in_=Wf[i][:].rearrange("p a b -> p (a b)"))
        xt = [P.sb("xt0", [128, 1024]), P.sb("xt1", [128, 1024])]; ht = [P.sb("ht0", [128, 1024]), P.sb("ht1", [128, 1024])]
        hT = P.sb("hT", [128, 8, 512], F32R)
        YA = P.sb("YA", [128, 4, 512]); YH = P.sb("YH", [128, 4, 512]); MT = P.sb("MT", [128, 8, 512], F32R)
        YAr = P.sb("YAr", [128, 4, 512], F32R); YHr = P.sb("YHr", [128, 4, 512], F32R)
        gs = [P.sb("gs0", [128, 512]), P.sb("gs1", [128, 512])]; m1 = P.sb("m1", [128, 512]); m2 = P.sb("m2", [128, 512])
        xm0_ = P.sb("xm0", [128, 1024]); xm = [xm0_, xm0_]
        pT = P.ps("pT", [128, 1024]); pG = [P.ps("pG0", [128, 512]), P.ps("pG1", [128, 512])]
        pM = [P.ps("pM0", [128, 512]), P.ps("pM1", [128, 512])]; pX = [P.ps("pX0", [128, 512]), P.ps("pX1", [128, 512])]
        wc = 0; xc = 0
        for B in range(4):
            t0 = own0 + B * 512
            xts = []
            for t in range(4):
                b = xc % 2; xc += 1
                P.dma(xt[b][:], x_d[t0 + t * 128: t0 + (t + 1) * 128, :], writes=[f"xt{b}"])
                norm_tile(xt[b], 128, G1, SH1, ht[b], f"xt{b}", f"ht{b}", "m4")
                for c in range(8):
                    P.g("tensor", "transpose", [f"ht{b}", "cst"], ["pT"], out=pT[:, c * 128:(c + 1) * 128], in_=ht[b][:, c * 128:(c + 1) * 128], identity=ident, silent=(c != 7))
                P.g("scalar", "activation", ["pT"], ["hT"], out=hT[:, :, t * 128:(t + 1) * 128], in_=pT[:].rearrange("p (c n) -> p c n", c=8), func=AF.Copy)
            P.dma(YA[:], ya_d[:, :, B * 512:(B + 1) * 512].rearrange("c p n -> p c n"), reads=["ya_d"], writes=["YA"])
            P.dma(YH[:], yh_d[:, :, B * 512:(B + 1) * 512].rearrange("c p n -> p c n"), reads=["yh_d"], writes=["YH"])
            P.g("gpsimd", "tensor_copy", ["YA"], ["YAr"], out=YAr[:], in_=YA[:])
            P.g("gpsimd", "tensor_copy", ["YH"], ["YHr"], out=YHr[:], in_=YH[:])
            for oc in range(8):
                for gi in range(2):
                    i = wc % 2; j_ = wc % 2; wc += 1
                    P.dma(Wf[j_][:], win_d[CG + gi * 8 + oc], writes=[f"Wf{j_}"])
                    if wc % 3 == 0:
                        P.g("gpsimd", "tensor_copy", [f"Wf{j_}"], [f"W{i}"], out=W[i][:], in_=Wf[j_][:])
                    else:
                        P.g("scalar", "activation", [f"Wf{j_}"], [f"W{i}"], out=W[i][:], in_=Wf[j_][:], func=AF.Copy)
                    for c in range(8):
                        P.g("tensor", "matmul", [f"W{i}", "hT"], [f"pG{gi}"], pG[gi][:], lhsT=W[i][:, c, :], rhs=hT[:, c, :], start=(c == 0), stop=(c == 7), silent=(c != 7))
                    P.g("scalar", "activation", [f"pG{gi}"], [f"gs{gi}"], out=gs[gi][:], in_=pG[gi][:], func=AF.Sigmoid)
                for c in range(4):
                    P.g("tensor", "matmul", ["woa", "YAr"], ["pM0"], pM[0][:], lhsT=woa[:, c, oc * 128:(oc + 1) * 128], rhs=YAr[:, c, :], start=(c == 0), stop=(c == 3), silent=(c != 3))
                for c in range(4):
                    P.g("tensor", "matmul", ["woh", "YHr"], ["pM1"], pM[1][:], lhsT=woh[:, c, oc * 128:(oc + 1) * 128], rhs=YHr[:, c, :], start=(c == 0), stop=(c == 3), silent=(c != 3))
                P.g("vector", "tensor_tensor", ["pM0", "gs0"], ["m1"], out=m1[:], in0=pM[0][:], in1=gs[0][:], op=ALU.mult)
                P.g("vector", "tensor_tensor", ["pM1", "gs1"], ["m2"], out=m2[:], in0=pM[1][:], in1=gs[1][:], op=ALU.mult)
                P.g("vector", "tensor_tensor", ["m1", "m2"], ["MT"], out=MT[:, oc, :], in0=m1[:], in1=m2[:], op=ALU.add)
            for t in range(4):
                b = xc % 2; xc += 1
                P.dma(xt[b][:], x_d[t0 + t * 128: t0 + (t + 1) * 128, :], writes=[f"xt{b}"])
                for hf in range(2):
                    for c in range(8):
                        P.g("tensor", "matmul", ["MT", "wout"], [f"pX{hf}"], pX[hf][:], lhsT=MT[:, c, t * 128:(t + 1) * 128], rhs=wout[:, c, hf * 512:(hf + 1) * 512],
                            start=(c == 0), stop=(c == 7), silent=(c != 7))
                    P.g("vector", "tensor_tensor", [f"pX{hf}"], ["xm0"], out=xm[b][:, hf * 512:(hf + 1) * 512], in0=pX[hf][:], in1=g1[:, hf * 512:(hf + 1) * 512], op=ALU.mult)
                P.g("vector", "tensor_tensor", ["xm0", f"xt{b}"], ["xm0"], out=xm[b][:], in0=xm[b][:], in1=xt[b][:], op=ALU.add)
                r = B * 512 + t * 128
                P.dma(xm_d[r:r + 128, :], xm[b][:], reads=["xm0"], writes=["xm_d"])
    if dbg and stop_after == 4:
        P.dma(dbg_d[:, 0:1024], xm_d[0:128, :], reads=[]); P.dma(dbg_d[:, 1024:2048], xm_d[1920:2048, :], reads=[])
        P.finish(P.all_tokens()); P.emit(); return nc

    out_tokens = []
    with P.scope():
        wq = P.sb("wq", [128, 8, 1024]); kbd = P.sb("kbd", [128, 8, 256])
        P.dma(wq[:], wq_d, writes=["wq"]); P.dma(kbd[:], kbd_d, writes=["kbd"])
        NBU = 12
        UV = [P.sb(f"UV{i}", [128, 2048], BF16) for i in range(NBU)]
        Dg = [P.sb(f"Dg{i}", [128, 128], BF16) for i in range(4)]
        xmt = [P.sb(f"xmt{i}", [128, 1024]) for i in range(3)]; h2 = [P.sb(f"h2_{i}", [128, 1024]) for i in range(3)]
        h2T = P.sb("h2T", [128, 8, 128]); QTs = P.sb("QTs", [128, 8, 128]); SCs = [P.sb("SCa", [128, 16, 128]), P.sb("SCb", [128, 16, 128])]; SC2 = P.sb("SC2", [128, 16, 128])
        V16 = P.sb("V16", [128, 16, 16]); I16 = P.sb("I16", [128, 16, 16], U32); I16f = P.sb("I16f", [128, 16, 16])
        cand = P.sb("cand", [128, 8, 256]); B16 = P.sb("B16", [128, 8, 16])
        cand2 = SC2[:].rearrange("p g k -> p (g k)").rearrange("p (h k) -> p h k", h=8)
        PI = P.sb("PI", [128, 8, 16], U32); PA = P.sb("PA", [128, 8, 16], U32); PB = P.sb("PB", [128, 8, 16], U32)
        paf = P.sb("paf", [128, 8, 16]); pbf = P.sb("pbf", [128, 8, 16]); OH = SC2[:].rearrange("p g k -> p (g k)").rearrange("p (h k) -> p h k", h=8)
        isel = P.sb("isel", [128, 128]); jsel = P.sb("jsel", [128, 128]); eif = P.sb("eif", [128, 128])
        EI = [P.sb("EI0", [128, 128], U32), P.sb("EI1", [128, 128], U32)]
        EG = P.sb("EG", [128, 8, 16]); EGs = [P.sb("EGs0", [128, 8, 16]), P.sb("EGs1", [128, 8, 16])]; gsm = P.sb("gsm", [128, 16]); GT = [P.sb("GT0", [128, 128]), P.sb("GT1", [128, 128])]
        Adot = [P.sb("Ad0", [128, 128]), P.sb("Ad1", [128, 128])]; junk = P.sb("junk", [128, 1024], BF16)
        gw = [P.sb("gwa", [128, 8]), P.sb("gwb", [128, 8])]; gw2 = [P.sb("gw2a", [128, 8]), P.sb("gw2b", [128, 8])]
        GA = [P.sb("GA0", [128, 128]), P.sb("GA1", [128, 128])]; acc0_ = P.sb("acc0", [128, 1024]); acc = [acc0_, acc0_]
        pT = P.ps("pT", [128, 1024]); pQ = P.ps("pQ", [128, 1024]); pSc = P.ps("pSc", [128, 1024]); pX = P.ps("pX", [128, 1024])
        cnt = {"u": 0, "v": 0, "d": 0}
        V16v = V16[:].rearrange("p (h t) k -> p h t k", t=2)
        I16v = I16f[:].rearrange("p (h t) k -> p h t k", t=2)
        NT = 1 if (dbg and stop_after == 5) else 16

        def front_pieces(n):
            b3 = n % 3; sb_ = n % 2; SC = SCs[sb_]

            def pa():
                P.dma(xmt[b3][:], xm_d[n * 128:(n + 1) * 128, :], reads=["xm_d"], writes=[f"xmt{b3}"])
                norm_tile(xmt[b3], 128, G2, SH2, h2[b3], f"xmt{b3}", f"h2{b3}", "p")

            def pb():
                for c in range(8):
                    P.g("tensor", "transpose", [f"h2{b3}", "cst"], ["pT"], out=pT[:, c * 128:(c + 1) * 128], in_=h2[b3][:, c * 128:(c + 1) * 128], identity=ident, silent=(c != 7))

            def pc():
                P.g("scalar", "activation", ["pT"], ["h2T"], out=h2T[:], in_=pT[:].rearrange("p (c n) -> p c n", c=8), func=AF.Copy)

            def pd(h0):
                def f():
                    for hd in range(h0, h0 + 4):
                        for c in range(8):
                            P.g("tensor", "matmul", ["wq", "h2T"], ["pQ"], pQ[:, hd * 128:(hd + 1) * 128], lhsT=wq[:, c, hd * 128:(hd + 1) * 128], rhs=h2T[:, c, :],
                                start=(c == 0), stop=(c == 7), silent=(c != 7))
                return f

            def pe():
                P.g("scalar", "activation", ["pQ"], ["QTs"], out=QTs[:], in_=pQ[:].rearrange("p (h n) -> p h n", h=8), func=AF.Copy)

            def pf(q):
                def f():
                    for h4 in range(4):
                        hd = q * 4 + h4
                        P.g("tensor", "matmul", ["QTs", "kbd"], ["pSc"], pSc[:, h4 * 256:(h4 + 1) * 256], lhsT=QTs[:, hd, :], rhs=kbd[:, hd, :], start=True, stop=True, silent=(h4 != 3))
                return f

            def pg(q):
                def f():
                    P.g("scalar", "activation", ["pSc"], [f"SC{sb_}_{q}"], out=SC[:, q * 8:(q + 1) * 8, :], in_=pSc[:].rearrange("p (g k) -> p g k", g=8), func=AF.Copy)
                return f
            return [pa, pb, pc, pd(0), pd(4), pe, pf(0), pg(0), pf(1), pg(1)]

        def front(n):
            for f in front_pieces(n):
                f()

        def routing_gen(n):
            b = n % 2; sb_ = n % 2; SC = SCs[sb_]
            for gI in range(16):
                P.g("vector", "max", [f"SC{sb_}_{gI // 8}"], [f"V16a{gI}"], out=V16[:, gI, 0:8], in_=SC[:, gI, :])
            yield
            for gI in range(16):
                P.g("vector", "match_replace", [f"SC{sb_}_{gI // 8}", f"V16a{gI}"], [f"SC2{gI}"], out=SC2[:, gI, :], in_to_replace=V16[:, gI, 0:8], in_values=SC[:, gI, :], imm_value=-1e30)
            yield
            for gI in range(16):
                P.g("vector", "max", [f"SC2{gI}"], [f"V16b{gI}"], out=V16[:, gI, 8:16], in_=SC2[:, gI, :])
            yield
            for gI in range(16):
                P.g("vector", "max_index", [f"SC{sb_}_{gI // 8}", f"V16a{gI}"], [f"I16a{gI}"], out=I16[:, gI, 0:8], in_max=V16[:, gI, 0:8], in_values=SC[:, gI, :])
            yield
            for gI in range(16):
                P.g("vector", "max_index", [f"SC{sb_}_{gI // 8}", f"V16b{gI}"], [f"I16b{gI}"], out=I16[:, gI, 8:16], in_max=V16[:, gI, 8:16], in_values=SC[:, gI, :])
            yield
            allV = [f"V16a{g_}" for g_ in range(16)] + [f"V16b{g_}" for g_ in range(16)]
            allI = [f"I16a{g_}" for g_ in range(16)] + [f"I16b{g_}" for g_ in range(16)]
            P.g("vector", "tensor_copy", allI, ["I16f"], out=I16f[:], in_=I16[:])
            P.g("vector", "tensor_tensor", allV + [f"cand2{h_}" for h_ in range(8)], ["cand"], out=cand[:].rearrange("p h (a c) -> p h a c", a=16),
                in0=V16v[:, :, 0, :].unsqueeze(3).to_broadcast([128, 8, 16, 16]), in1=V16v[:, :, 1, :].unsqueeze(2).to_broadcast([128, 8, 16, 16]), op=ALU.add)
            yield
            for hd in range(8):
                P.g("vector", "max", ["cand"], [f"B16a{hd}"], out=B16[:, hd, 0:8], in_=cand[:, hd, :])
            for hd in range(8):
                P.g("vector", "match_replace", ["cand", f"B16a{hd}"], [f"cand2{hd}"], out=cand2[:, hd, :], in_to_replace=B16[:, hd, 0:8], in_values=cand[:, hd, :], imm_value=-1e30)
            yield
            for hd in range(8):
                P.g("vector", "max", [f"cand2{hd}"], [f"B16b{hd}"], out=B16[:, hd, 8:16], in_=cand2[:, hd, :])
            for hd in range(8):
                P.g("vector", "max_index", ["cand", f"B16a{hd}"], [f"PIa{hd}"], out=PI[:, hd, 0:8], in_max=B16[:, hd, 0:8], in_values=cand[:, hd, :])
            yield
            for hd in range(8):
                P.g("vector", "max_index", ["cand", f"B16b{hd}"], [f"PIb{hd}"], out=PI[:, hd, 8:16], in_max=B16[:, hd, 8:16], in_values=cand[:, hd, :])
            allB = [f"B16a{h_}" for h_ in range(8)] + [f"B16b{h_}" for h_ in range(8)]
            allP = [f"PIa{h_}" for h_ in range(8)] + [f"PIb{h_}" for h_ in range(8)]
            P.g("vector", "tensor_single_scalar", allP, ["PA"], out=PA[:], in_=PI[:], scalar=4, op=ALU.logical_shift_right)
            P.g("vector", "tensor_single_scalar", allP, ["PB"], out=PB[:], in_=PI[:], scalar=15, op=ALU.bitwise_and)
            P.g("vector", "tensor_copy", ["PA"], ["paf"], out=paf[:], in_=PA[:])
            P.g("vector", "tensor_copy", ["PB"], ["pbf"], out=pbf[:], in_=PB[:])
            yield
            for (pf_, pk, tt, dst, dk) in ((paf, "paf", 0, isel, "isel"), (pbf, "pbf", 1, jsel, "jsel")):
                OHv = OH.rearrange("p h (k a) -> p h k a", k=16)
                P.g("vector", "tensor_tensor", [pk, "cst"] + [f"cand2{h_}" for h_ in range(8)], ["OH"], out=OHv, in0=iota16.unsqueeze(1).unsqueeze(1).to_broadcast([128, 8, 16, 16]),
                    in1=pf_[:].unsqueeze(3).to_broadcast([128, 8, 16, 16]), op=ALU.is_equal)
                yield
                P.g("vector", "tensor_tensor", ["OH", "I16f"], ["OH"], out=OHv, in0=OHv, in1=I16v[:, :, tt, :].unsqueeze(2).to_broadcast([128, 8, 16, 16]), op=ALU.mult)
                P.g("vector", "tensor_reduce", ["OH"], [dk], out=dst[:], in_=OH.rearrange("p h (k a) -> p (h k) a", k=16), axis=AX.X, op=ALU.add)
                yield
            P.g("vector", "scalar_tensor_tensor", ["isel", "jsel"], ["eif"], out=eif[:], in0=isel[:], scalar=128.0, in1=jsel[:], op0=ALU.mult, op1=ALU.add)
            P.g("vector", "tensor_copy", ["eif"], [f"EI{b}"], out=EI[b][:], in_=eif[:])
            P.g("vector", "tensor_tensor", allB, [f"EGs{b}"], out=EGs[b][:], in0=B16[:], in1=B16[:, :, 0:1].to_broadcast([128, 8, 16]), op=ALU.subtract)

        def routing(n):
            for _ in routing_gen(n):
                pass

        def routing_b(n):
            b = n % 2
            P.g("scalar", "activation", [f"EGs{b}"], ["EG"], out=EG[:], in_=EGs[b][:], func=AF.Exp)
            P.g("vector", "tensor_reduce", ["EG"], ["gsm"], out=gsm[:, 0:8], in_=EG[:], axis=AX.X, op=ALU.add)
            P.g("vector", "reciprocal", ["gsm"], ["gsm2"], out=gsm[:, 8:16], in_=gsm[:, 0:8])
            P.g("vector", "tensor_tensor", ["EG", "gsm2"], [f"GT{b}"], out=GT[b][:].rearrange("p (h k) -> p h k", h=8), in0=EG[:],
                in1=gsm[:, 8:16].unsqueeze(2).to_broadcast([128, 8, 16]), op=ALU.mult)

        def tile_body(n):
            b = n % 2; b3 = n % 3; SC = SCs[n % 2]
            fp = front_pieces(n + 2) if n + 2 < NT else None
            rg = routing_gen(n + 1) if n + 1 < NT else None
            for g in range(16):
                gb = g % 2
                ks = []
                for j in range(8):
                    s = g * 8 + j
                    k = cnt["u"] % NBU; cnt["u"] += 1; ks.append(k)
                    P.gather(UV[k][:], uv16_d, EI[b][:, s:s + 1], reads=[f"EI{b}", "p16"], writes=[f"UV{k}"])
                    P.g("vector", "scalar_tensor_tensor", [f"UV{k}", f"h2{b3}"], ["junk", f"Ad{b}_{s}"], out=junk[:], in0=UV[k][:, 0:1024], scalar=1.0, in1=h2[b3][:],
                        op0=ALU.mult, op1=ALU.mult, accum_out=Adot[b][:, s:s + 1])
                akeys = [f"Ad{b}_{g * 8 + j}" for j in range(8)]
                av = Adot[b][:, g * 8:(g + 1) * 8]
                P.g("vector", "tensor_tensor", akeys, [f"gw{gb}"], out=gw[gb][:], in0=av, in1=av, op=ALU.mult)
                P.g("vector", "tensor_scalar", [f"gw{gb}"], [f"gw{gb}"], out=gw[gb][:], in0=gw[gb][:], scalar1=0.044715, scalar2=1.0, op0=ALU.mult, op1=ALU.add)
                P.g("vector", "tensor_tensor", [f"gw{gb}"] + akeys, [f"gw{gb}"], out=gw[gb][:], in0=gw[gb][:], in1=av, op=ALU.mult)
                P.g("scalar", "activation", [f"gw{gb}"], [f"gw2{gb}"], out=gw2[gb][:], in_=gw[gb][:], func=AF.Sigmoid, scale=2.0 * math.sqrt(2.0 / math.pi))
                P.g("vector", "tensor_tensor", [f"gw2{gb}"] + akeys, [f"gw2{gb}"], out=gw2[gb][:], in0=gw2[gb][:], in1=av, op=ALU.mult)
                P.g("vector", "tensor_tensor", [f"gw2{gb}", f"GT{b}"], [f"GA{b}_{g}"], out=GA[b][:, g * 8:(g + 1) * 8], in0=gw2[gb][:], in1=GT[b][:, g * 8:(g + 1) * 8], op=ALU.mult)
                for j in range(8):
                    s = g * 8 + j; k = ks[j]
                    kd = cnt["d"] % 4; cnt["d"] += 1
                    P.g("scalar", "activation", ["cst", f"GA{b}_{g}"], [f"Dg{kd}"], out=Dg[kd][:], in_=ident, func=AF.Copy, scale=GA[b][:, s:s + 1])
                    for hf in range(2):
                        P.g("tensor", "matmul", [f"Dg{kd}", f"UV{k}"], ["pX"], pX[:, hf * 512:(hf + 1) * 512], lhsT=Dg[kd][:], rhs=UV[k][:, 1024 + hf * 512:1024 + (hf + 1) * 512],
                            start=(s == 0), stop=(s == 127), silent=(hf == 0))
                if fp is not None and g < len(fp):
                    fp[g]()
                if rg is not None:
                    if next(rg, "done") == "done":
                        rg = None
            if dbg and stop_after == 5:
                P.barrier()
                P.g("vector", "tensor_copy", [], ["acc0"], out=acc[0][:], in_=pX[:])
                P.barrier()
                P.dma(dbg_d[:, 0:2048], SC[:].rearrange("p g k -> p (g k)"), reads=[])
                P.dma(dbg_d[:, 2048:2304], V16[:].rearrange("p g k -> p (g k)"), reads=[])
                P.dma(dbg_d[:, 2304:2560], I16f[:].rearrange("p g k -> p (g k)"), reads=[])
                P.dma(dbg_d[:, 2560:2688], B16[:].rearrange("p g k -> p (g k)"), reads=[])
                P.dma(dbg_d[:, 2688:2816], eif[:], reads=[])
                P.dma(dbg_d[:, 2816:2944], GT[0][:], reads=[])
                P.dma(dbg_d[:, 2944:3072], Adot[0][:], reads=[])
                P.dma(dbg_d[:, 3072:4096], acc[0][:], reads=[])
                P.barrier()
            if rg is not None:
                for _ in rg:
                    pass
            if n + 1 < NT:
                routing_b(n + 1)
            P.g("vector", "tensor_tensor", ["pX"], ["acc0"], out=acc[b][:], in0=pX[:], in1=g2, op=ALU.mult)
            P.g("vector", "tensor_tensor", ["acc0", f"xmt{b3}"], ["acc0"], out=acc[b][:], in0=acc[b][:], in1=xmt[b3][:], op=ALU.add)
            norm_tile(acc[b], 128, FG, None, xmt[b3], "acc0", f"xmt{b3}", "f")
            out_tokens.append(P.dma(out_d[n * 128:(n + 1) * 128, :], xmt[b3][:], reads=[f"xmt{b3}"], writes=["out_d"]))

        front(0); routing(0); routing_b(0)
        if NT > 1:
            front(1)
        for n in range(NT):
            tile_body(n)
    P.finish(P.all_tokens())
    P.emit()
    return nc


_CONST_CACHE = {}


def _constants():
    if _CONST_CACHE:
        return _CONST_CACHE
    N = 2 * L
    base = np.cos(2 * np.pi * np.arange(N) / N)
    bases = np.sin(2 * np.pi * np.arange(N) / N)
    s = np.arange(L, dtype=np.int64)[:, None]
    k = np.arange(33 * 128, dtype=np.int64)[None, :]
    idx = (s * k) % N
    valid = (k <= L)
    C = np.where(valid, base[idx], 0.0).astype(np.float32)
    S = np.where(valid, bases[idx], 0.0).astype(np.float32)
    def fwd(T):
        return np.ascontiguousarray(T.reshape(32, 128, 33, 128).transpose(2, 1, 0, 3)).astype(ml_dtypes.bfloat16)
    _CONST_CACHE["dftc"] = fwd(C); _CONST_CACHE["dfts"] = fwd(S)
    kk = np.arange(33 * 128, dtype=np.int64)[:, None]
    t = np.arange(L, dtype=np.int64)[None, :]
    idx2 = (kk * t) % N
    wk = np.where((kk == 0) | (kk == L), 1.0, 2.0) / N
    wk = np.where(kk <= L, wk, 0.0)
    IC = (base[idx2] * wk).astype(np.float32).reshape(33, 128, L)
    IS = (bases[idx2] * wk).astype(np.float32).reshape(33, 128, L)
    _CONST_CACHE["idc"] = [np.ascontiguousarray(IC[:, :, h * 2048:(h + 1) * 2048]).astype(ml_dtypes.bfloat16) for h in range(2)]
    _CONST_CACHE["ids"] = [np.ascontiguousarray(IS[:, :, h * 2048:(h + 1) * 2048]).astype(ml_dtypes.bfloat16) for h in range(2)]
    f = 16
    inv = (10000.0 ** (-np.arange(f, dtype=np.float32) / f)).astype(np.float32)
    tok = np.arange(L)
    row = (tok // 64).astype(np.float32); col = (tok % 64).astype(np.float32)
    cosT = np.zeros((64, L), np.float32); sinT = np.zeros((64, L), np.float32)
    for d in range(64):
        pos = row if d < 32 else col
        j = d % 32
        ang = pos * inv[j % 16]
        cosT[d] = np.cos(ang)
        sinT[d] = -np.sin(ang) if j < 16 else np.sin(ang)
    _CONST_CACHE["rope"] = np.ascontiguousarray(np.stack([np.tile(cosT, (2, 1)), np.tile(sinT, (2, 1))])).astype(np.float32)
    def feats(tt):
        bands = np.arange(1, 17, dtype=np.float32)
        ang = (2.0 * np.pi * tt[:, None] * bands[None, :]).astype(np.float32)
        return np.concatenate([tt[:, None], np.cos(ang), np.sin(ang)], axis=-1).astype(np.float32)
    tt = (np.arange(L, dtype=np.float32) / L).astype(np.float32)
    tts = (np.maximum(np.arange(L) - 1, 0).astype(np.float32) / L).astype(np.float32)
    _CONST_CACHE["zf"] = np.ascontiguousarray(np.stack([feats(tt).T, feats(tts).T])).astype(np.float32)
    deltas = np.linspace(math.log(1e-2) / 1.5, math.log(1e-2) / 0.3, 512, dtype=np.float32)
    _CONST_CACHE["ndelta"] = np.ascontiguousarray(np.tile(-np.abs(deltas)[None, :], (128, 1))).astype(np.float32)
    tp = np.zeros((128, 64), np.float32)
    tp[:, 0:32] = tt.reshape(32, 128).T
    tp[:, 32:64] = tts.reshape(32, 128).T
    _CONST_CACHE["tpos"] = tp
    cst = np.zeros((128, 1024), np.float32)
    cst[:, 0:128] = np.eye(128, dtype=np.float32)
    i = np.arange(128)[:, None]; j = np.arange(384)[None, :]
    cst[:, 128:512] = np.where((j >= i) & (j <= i + 256), 0.0, NEG)
    cst[:, 512:528] = np.arange(16, dtype=np.float32)[None, :]
    cst[:, 528:656] = 1.0
    _CONST_CACHE["consts"] = cst
    _CONST_CACHE["constsb"] = np.eye(128, dtype=np.float32).astype(ml_dtypes.bfloat16)
    return _CONST_CACHE


def _chunk_rows(w, nk):
    return np.ascontiguousarray(w.reshape(nk, 128, -1).transpose(1, 0, 2))


def prepare_inputs(inp):
    cs = _constants()
    f32 = lambda a: np.ascontiguousarray(np.asarray(a, dtype=np.float32))
    w_in = f32(inp["w_in"])[0]
    swap64 = np.concatenate([np.arange(16, 32), np.arange(0, 16), np.arange(48, 64), np.arange(32, 48)])
    cols = []
    for c in range(4):
        cols.append(np.concatenate([c * 64 + np.arange(64), (4 + c) * 64 + np.arange(64)]))
    for c in range(4):
        cols.append(np.concatenate([c * 64 + swap64, (4 + c) * 64 + swap64]))
    cols.append(512 + np.arange(128))
    cols.append(512 + np.concatenate([swap64, 64 + swap64]))
    cols.append(640 + np.arange(128))
    for c in range(12):
        cols.append(768 + c * 128 + np.arange(128))
    for c in range(16):
        cols.append(2304 + c * 128 + np.arange(128))
    wch = np.stack([_chunk_rows(w_in[:, cc], 8) for cc in cols])
    w_mod = f32(inp["w_mod"])[0]
    wm = np.ascontiguousarray(w_mod.reshape(8, 128, 12, 512).transpose(2, 1, 0, 3))
    bc = lambda v: np.ascontiguousarray(np.tile(f32(v).reshape(1, -1), (128, 1)))
    ng = np.ascontiguousarray(np.stack([bc(inp["norm1_g"][0]), bc(inp["norm2_g"][0]), bc(inp["final_g"])], axis=1))
    convw = np.zeros((128, 12, 4), np.float32)
    cw = f32(inp["hy_conv_w"])[0]; cbias = f32(inp["hy_conv_b"])[0]
    for j in range(3):
        convw[:, :, j] = cw[j].reshape(12, 128).T
    convw[:, :, 3] = cbias.reshape(12, 128).T
    fvec = np.stack([f32(inp["hy_fb1"])[0], f32(inp["hy_fb2"])[0], f32(inp["hy_freq"])[0], np.zeros(64, np.float32)], axis=1)
    keys = f32(inp["peer_keys"])[0]
    kbd = np.zeros((128, 8, 256), np.float32)
    for h in range(8):
        for p in range(2):
            kbd[p * 64:(p + 1) * 64, h, p * 128:(p + 1) * 128] = keys[h, p].T
    shared = {
        "w_mod": wm, "b_mod": bc(inp["b_mod"][0]), "ng": ng, "w_in": wch, "rope": cs["rope"], "sink": bc(inp["attn_sink"][0]),
        "convw": convw, "fw1": f32(inp["hy_fw1"])[0], "fvec": np.ascontiguousarray(fvec), "fw2": f32(inp["hy_fw2"])[0],
        "fw3": f32(inp["hy_fw3"])[0], "fb3": bc(inp["hy_fb3"][0]), "skip": bc(inp["hy_skip"][0]), "zf": cs["zf"],
        "ndelta": cs["ndelta"], "tpos": cs["tpos"], "w_oa": _chunk_rows(f32(inp["w_o_attn"])[0], 4),
        "w_oh": _chunk_rows(f32(inp["w_o_hy"])[0], 4), "w_out": _chunk_rows(f32(inp["w_out"])[0], 8),
        "wq": _chunk_rows(f32(inp["peer_wq"])[0], 8), "kbd": kbd, "peer_u": f32(inp["peer_u"])[0], "peer_v": f32(inp["peer_v"])[0],
        "dftc": cs["dftc"], "dfts": cs["dfts"], "consts": cs["consts"], "constsb": cs["constsb"],
    }
    x = f32(inp["x"]); ctx = f32(inp["ctx"]); c = f32(inp["c"]); c_ctx = f32(inp["c_ctx"])
    maps = []
    for core in range(8):
        b, half = core // 2, core % 2
        cvec = np.concatenate([c[b].reshape(8, 128).T, c_ctx.reshape(8, 128).T], axis=1)
        m = dict(shared)
        cst = cs["consts"].copy()
        cst[:, 656] = 1.0 if half == 0 else -1.0
        xb = x[b] if half == 0 else np.ascontiguousarray(x[b][::-1])
        m.update({"x": xb, "ctx": ctx[b], "cvec": np.ascontiguousarray(cvec), "idftc": cs["idc"][0], "idfts": cs["ids"][0], "consts": cst})
        if half == 1:
            m["rope"] = np.ascontiguousarray(cs["rope"][:, :, ::-1])
            m["convw"] = np.ascontiguousarray(convw[:, :, [2, 1, 0, 3]])
        maps.append(m)
    return maps


_PROG_CACHE = {}


def kernel(**inputs):
    maps = prepare_inputs(inputs)
    nc = build_program(0)
    res = run_bass_kernel_spmd(nc, maps, core_ids=list(range(8)))
    out = np.zeros((4, L, D), np.float32)
    for core in range(8):
        b, half = core // 2, core % 2
        o = res.results[core]["out"]
        if half == 0:
            out[b, 0:2048] = o
        else:
            out[b, 2048:4096] = o[::-1]
    return out
```

```python
from contextlib import ExitStack, contextmanager
import math
import numpy as np
import ml_dtypes
import concourse.bass as bass
import concourse.mybir as mybir
from concourse.bass_utils import run_bass_kernel_spmd

F32 = mybir.dt.float32
BF16 = mybir.dt.bfloat16
U32 = mybir.dt.uint32
F32R = mybir.dt.float32r
AF = mybir.ActivationFunctionType
ALU = mybir.AluOpType
AX = mybir.AxisListType
ENGS = ["sync", "scalar", "vector", "gpsimd", "tensor"]

L = 4096
D = 1024
NEG = -30000.0


class Prog:
    def __init__(self, nc, n_dma_sync=24, n_dma_pool=32):
        self.nc = nc
        self.es = ExitStack()
        self.stacks = [self.es]
        self.ops = {e: [] for e in ENGS}
        self.cnt = {e: 0 for e in ENGS}
        self.sem = {e: self.es.enter_context(nc.semaphore("s_" + e)) for e in ENGS}
        self.dpool = {
            "sync": [self.es.enter_context(nc.semaphore(f"ds{i}")) for i in range(n_dma_sync)],
            "gpsimd": [self.es.enter_context(nc.semaphore(f"dg{i}")) for i in range(n_dma_pool)],
        }
        self.dval = {q: [0] * len(p) for q, p in self.dpool.items()}
        self.dnext = {q: 0 for q in self.dpool}
        self.waited = {e: {} for e in ENGS}
        self.lastw = {}
        self.readers = {}
        self.uid = 0

    def sb(self, name, shape, dtype=F32):
        self.uid += 1
        return self.stacks[-1].enter_context(self.nc.sbuf_tensor(f"{name}_{self.uid}", list(shape), dtype))

    def ps(self, name, shape, dtype=F32):
        self.uid += 1
        return self.stacks[-1].enter_context(self.nc.psum_tensor(f"{name}_{self.uid}", list(shape), dtype))

    @contextmanager
    def scope(self):
        st = ExitStack()
        self.stacks.append(st)
        try:
            yield
        finally:
            self.barrier()
            self.stacks.pop()
            st.close()

    def all_tokens(self):
        toks = [("c" + e, self.sem[e], self.cnt[e]) for e in ENGS if self.cnt[e] > 0]
        for q, pool in self.dpool.items():
            for j, sem in enumerate(pool):
                if self.dval[q][j] > 0:
                    toks.append((f"d{q}{j}", sem, self.dval[q][j]))
        return toks

    def barrier(self):
        toks = self.all_tokens()
        for e in ENGS:
            waits = []
            for sid, sem, val in toks:
                if self.waited[e].get(sid, 0) < val:
                    waits.append((sem, val))
                    self.waited[e][sid] = val
            self.ops[e].append((waits, None, None))
        self.lastw = {}
        self.readers = {}

    def _deps(self, eng, reads, writes):
        toks = []
        for k in reads:
            t = self.lastw.get(k)
            if t is not None:
                toks.append(t)
        for k in writes:
            t = self.lastw.get(k)
            if t is not None:
                toks.append(t)
            toks.extend(self.readers.get(k, ()))
        wd = self.waited[eng]
        best = {}
        for (sid, sem, val) in toks:
            if sid == "c" + eng and val > self.cnt[eng]:
                continue
            if wd.get(sid, 0) < val and best.get(sid, (None, 0))[1] < val:
                best[sid] = (sem, val)
        waits = []
        for sid, (sem, val) in best.items():
            wd[sid] = val
            waits.append((sem, val))
        return waits

    def _commit(self, tok, reads, writes):
        for k in writes:
            self.lastw[k] = tok
            self.readers[k] = []
        for k in reads:
            self.readers.setdefault(k, []).append(tok)

    def op(self, eng, fn, reads=(), writes=(), silent=False):
        reads, writes = list(reads), list(writes)
        waits = self._deps(eng, reads, writes)
        if silent:
            tok = ("c" + eng, self.sem[eng], self.cnt[eng] + 1)
            self.ops[eng].append((waits, fn, None))
        else:
            self.cnt[eng] += 1
            tok = ("c" + eng, self.sem[eng], self.cnt[eng])
            self.ops[eng].append((waits, fn, (self.sem[eng], 1)))
        self._commit(tok, reads, writes)
        return tok

    def g(self, eng, name, reads, writes, *args, silent=False, **kw):
        return self.op(eng, lambda e: getattr(e, name)(*args, **kw), reads, writes, silent=silent)

    def dma(self, out, in_, reads=(), writes=(), q="sync", fn=None):
        reads, writes = list(reads), list(writes)
        waits = self._deps(q, reads, writes)
        j = self.dnext[q]
        self.dnext[q] = (j + 1) % len(self.dpool[q])
        sem = self.dpool[q][j]
        sid = f"d{q}{j}"
        prev = self.dval[q][j]
        if prev > 0 and self.waited[q].get(sid, 0) < prev:
            waits.append((sem, prev))
            self.waited[q][sid] = prev
        self.dval[q][j] = prev + 16
        tok = (sid, sem, prev + 16)
        if fn is None:
            fn = lambda e: e.dma_start(out=out, in_=in_)
        self.ops[q].append((waits, fn, (sem, 16)))
        self._commit(tok, reads, writes)
        return tok

    def gather(self, out, table, idx_ap, reads, writes):
        fn = lambda e: e.indirect_dma_start(out=out, out_offset=None, in_=table,
                                            in_offset=bass.IndirectOffsetOnAxis(ap=idx_ap, axis=0))
        return self.dma(None, None, reads, writes, q="gpsimd", fn=fn)

    def finish(self, tokens, eng="sync"):
        waits = []
        for (sid, sem, val) in tokens:
            if self.waited[eng].get(sid, 0) < val:
                waits.append((sem, val))
                self.waited[eng][sid] = val
        self.ops[eng].append((waits, None, None))

    def emit(self):
        with self.nc.Block() as block:
            def mk(engname):
                def body(e):
                    for waits, fn, inc in self.ops[engname]:
                        for sem, val in waits:
                            e.wait_ge(sem, val)
                        if fn is not None:
                            ins = fn(e)
                            if inc is not None:
                                ins.then_inc(inc[0], inc[1])
                return body
            block.sync(mk("sync"))
            block.scalar(mk("scalar"))
            block.vector(mk("vector"))
            block.gpsimd(mk("gpsimd"))
            block.tensor(mk("tensor"))
        for st in reversed(self.stacks):
            st.close()


CQ, CQS, CK, CKS, CV, CHY, CG = 0, 4, 8, 9, 10, 11, 23
NWCH = 39


def build_program(half, stop_after=None, dbg=False):
    nc = bass.Bass("TRN2", target_bir_lowering=False)

    def din(name, shape, dt=F32):
        return nc.dram_tensor(name, list(shape), dt, kind="ExternalInput").ap()

    x_d = din("x", [L, D]); ctx_d = din("ctx", [256, D]); cvec_d = din("cvec", [128, 16])
    wmod_d = din("w_mod", [12, 128, 8, 512]); bmod_d = din("b_mod", [128, 6144]); ng_d = din("ng", [128, 3, 1024])
    win_d = din("w_in", [NWCH, 128, 8, 128]); rope_d = din("rope", [2, 128, L]); sink_d = din("sink", [128, 8])
    convw_d = din("convw", [128, 12, 4]); fw1_d = din("fw1", [33, 64]); fvec_d = din("fvec", [64, 4])
    fw2_d = din("fw2", [64, 64]); fw3_d = din("fw3", [64, 1024]); fb3_d = din("fb3", [128, 1024])
    skip_d = din("skip", [128, 512]); zf_d = din("zf", [2, 33, L]); ndelta_d = din("ndelta", [128, 512])
    tpos_d = din("tpos", [128, 64]); woa_d = din("w_oa", [128, 4, 1024]); woh_d = din("w_oh", [128, 4, 1024])
    wout_d = din("w_out", [128, 8, 1024]); wq_d = din("wq", [128, 8, 1024]); kbd_d = din("kbd", [128, 8, 256])
    pu_d = din("peer_u", [16384, D]); pv_d = din("peer_v", [16384, D])
    dftc_d = din("dftc", [33, 128, 32, 128], BF16); dfts_d = din("dfts", [33, 128, 32, 128], BF16)
    idc_d = din("idftc", [33, 128, 2048], BF16); ids_d = din("idfts", [33, 128, 2048], BF16)
    cst_d = din("consts", [128, 1024]); cstb_d = din("constsb", [128, 128], BF16)
    out_d = nc.dram_tensor("out", [2048, D], F32, kind="ExternalOutput").ap()
    ya_d = nc.dram_tensor("ya_s", [4, 128, 2048], F32, kind="Internal").ap()
    yh_d = nc.dram_tensor("yh_s", [4, 128, 2048], F32, kind="Internal").ap()
    x0c_d = nc.dram_tensor("x0c_s", [4, 128, 2048], F32, kind="Internal").ap()
    xm_d = nc.dram_tensor("xm_s", [2048, D], F32, kind="Internal").ap()
    dbg_d = nc.dram_tensor("dbg", [128, 4096], F32, kind="ExternalOutput").ap() if dbg else None
    uv16_d = nc.dram_tensor("uv16_s", [16384, 2 * D], BF16, kind="Internal").ap()

    P = Prog(nc)
    own0 = 2048 * half
    oblk0 = 4 * half
    kblk0 = oblk0 - 1

    cst = P.sb("cst", [128, 1024])
    cstb = P.sb("cstb", [128, 128], BF16)
    P.dma(cst[:], cst_d, writes=["cst"]); P.dma(cstb[:], cstb_d, writes=["cstb"])
    ident = cst[:, 0:128]; bmask = cst[:, 128:512]; iota16 = cst[:, 512:528]; ones = cst[:, 528:656]
    MODX = P.sb("MODX", [128, 6144])
    FG = P.sb("FG", [128, 1024])
    P.dma(FG[:], ng_d[:, 2, :], writes=["FG"])
    outer = P.scope(); outer.__enter__()
    MODC = P.sb("MODC", [128, 2048])
    uT = P.sb("uT", [128, 4, L], BF16)

    def norm_tile(xt, np_, G, SH, out, key_x, key_out, tag):
        ss = P.sb("ss" + tag, [128, 4]) if tag not in norm_tile.cache else norm_tile.cache[tag]
        norm_tile.cache[tag] = ss
        k = "ss" + tag
        P.g("vector", "scalar_tensor_tensor", [key_x], [key_out, k], out=out[0:np_, :], in0=xt[0:np_, :], scalar=1.0, in1=xt[0:np_, :],
            op0=ALU.mult, op1=ALU.mult, accum_out=ss[0:np_, 0:1])
        P.g("vector", "tensor_scalar", [k], [k + "b"], out=ss[0:np_, 1:2], in0=ss[0:np_, 0:1], scalar1=1.0 / D, scalar2=1e-6,
            op0=ALU.mult, op1=ALU.add)
        P.g("scalar", "activation", [k + "b"], [k + "c"], out=ss[0:np_, 2:3], in_=ss[0:np_, 1:2], func=AF.Sqrt)
        P.g("vector", "reciprocal", [k + "c"], [k + "d"], out=ss[0:np_, 3:4], in_=ss[0:np_, 2:3])
        if SH is None:
            P.g("vector", "scalar_tensor_tensor", [key_x, k + "d"], [key_out], out=out[0:np_, :], in0=xt[0:np_, :],
                scalar=ss[0:np_, 3:4], in1=G[0:np_, :], op0=ALU.mult, op1=ALU.mult)
        else:
            P.g("vector", "scalar_tensor_tensor", [key_x, k + "d"], [key_out], out=out[0:np_, :], in0=xt[0:np_, :],
                scalar=ss[0:np_, 3:4], in1=G[0:np_, :], op0=ALU.mult, op1=ALU.mult)
            P.g("vector", "tensor_tensor", [key_out], [key_out], out=out[0:np_, :], in0=out[0:np_, :], in1=SH[0:np_, :], op=ALU.add)
    norm_tile.cache = {}

    with P.scope():
        cv = P.sb("cv", [128, 16]); cs = P.sb("cs", [128, 16]); CB = P.sb("CB", [128, 2, 8, 128])
        ngt = P.sb("ngt", [128, 2, 1024])
        wm = [P.sb("wm0", [128, 8, 512]), P.sb("wm1", [128, 8, 512])]
        bm = [P.sb("bm0", [128, 512]), P.sb("bm1", [128, 512])]
        pm = [P.ps("pm0", [128, 512]), P.ps("pm1", [128, 512])]
        P.dma(cv[:], cvec_d, writes=["cv"]); P.dma(ngt[:], ng_d[:, 0:2, :], writes=["ngt"])
        P.g("scalar", "activation", ["cv"], ["cs"], out=cs[:], in_=cv[:], func=AF.Silu)
        for v in range(2):
            P.g("vector", "tensor_copy", ["cs"], ["CB"], out=CB[:, v, :, :],
                in_=cs[:, v * 8:(v + 1) * 8].unsqueeze(2).to_broadcast([128, 8, 128]))
        it = 0
        for v, nblk, dst in ((0, 12, MODX), (1, 4, MODC)):
            for j in range(nblk):
                b = it % 2; it += 1
                P.dma(wm[b][:], wmod_d[j], writes=[f"wm{b}"])
                P.dma(bm[b][:], bmod_d[:, j * 512:(j + 1) * 512], writes=[f"bm{b}"])
                for c in range(8):
                    P.g("tensor", "matmul", ["CB", f"wm{b}"], [f"pm{b}"], pm[b][:], lhsT=CB[:, v, c, :], rhs=wm[b][:, c, :],
                        start=(c == 0), stop=(c == 7), silent=(c != 7))
                P.g("vector", "tensor_tensor", [f"pm{b}", f"bm{b}"], ["MOD"], out=dst[:, j * 512:(j + 1) * 512], in0=pm[b][:],
                    in1=bm[b][:], op=ALU.add)
        for dst, lo, gi in ((MODX, 1024, 0), (MODX, 4096, 1), (MODC, 1024, 0)):
            P.g("vector", "scalar_tensor_tensor", ["MOD", "ngt"], ["MOD"], out=dst[:, lo:lo + 1024], in0=dst[:, lo:lo + 1024],
                scalar=1.0, in1=ngt[:, gi, :], op0=ALU.add, op1=ALU.mult)
    SH1 = MODX[:, 0:1024]; G1 = MODX[:, 1024:2048]; g1 = MODX[:, 2048:3072]
    SH2 = MODX[:, 3072:4096]; G2 = MODX[:, 4096:5120]; g2 = MODX[:, 5120:6144]
    if dbg and stop_after == 0:
        t = P.dma(dbg_d[:, 0:4096], MODX[:, 0:4096], reads=[])
        P.finish(P.all_tokens()); P.emit(); return nc

    with P.scope():
        QT = P.sb("QT", [128, 4, 2048]); KT = P.sb("KT", [128, 6 * 512]); V = P.sb("V", [128, 24, 128], F32R)
        CKT = P.sb("CKT", [128, 256]); CVt = P.sb("CV", [128, 2, 128], F32R)
        sinkt = P.sb("sinkt", [128, 8]); convw = P.sb("convw", [128, 12, 4])
        P.dma(sinkt[:], sink_d, writes=["sink"]); P.dma(convw[:], convw_d, writes=["convw"])
        with P.scope():
            xt = [P.sb("xt0", [128, 1024]), P.sb("xt1", [128, 1024])]
            ht = [P.sb("ht0", [128, 1024]), P.sb("ht1", [128, 1024])]
            hT = P.sb("hT", [128, 8, 514], F32R)
            Wf = [P.sb(f"Wf{i}", [128, 8, 128]) for i in range(3)]
            W = [P.sb(f"W{i}", [128, 8, 128], F32R) for i in range(3)]
            rp = P.sb("rp", [128, 2, 512])
            ZT = [P.sb(f"ZT{i}", [128, 514]) for i in range(2)]
            ca = P.sb("ca", [128, 512]); cb = P.sb("cb", [128, 512]); x0s_ = P.sb("x0s0", [128, 512]); x0s = [x0s_, x0s_]
            tq = P.sb("tq", [128, 512]); tq2 = P.sb("tq2", [128, 512])
            pT = P.ps("pT", [128, 1024]); pA = [P.ps("pA0", [128, 1024]), P.ps("pA1", [128, 1024])]
            pB0_ = P.ps("pB0", [128, 512]); pB = [pB0_, pB0_]
            wcnt = [0]; zcnt = [0]; acnt = [0]; xcnt = [0]

            def load_w(ci):
                i = wcnt[0] % 3; wcnt[0] += 1
                P.dma(Wf[i][:], win_d[ci], writes=[f"Wf{i}"])
                if wcnt[0] % 3 == 0:
                    P.g("gpsimd", "tensor_copy", [f"Wf{i}"], [f"W{i}"], out=W[i][:], in_=Wf[i][:])
                else:
                    P.g("scalar", "activation", [f"Wf{i}"], [f"W{i}"], out=W[i][:], in_=Wf[i][:], func=AF.Copy)
                return W[i], f"W{i}"

            def make_hT(src_d, row0, ntile, G, SH, col0):
                for t in range(ntile):
                    b = xcnt[0] % 2; xcnt[0] += 1
                    P.dma(xt[b][:], src_d[row0 + t * 128: row0 + (t + 1) * 128, :], writes=[f"xt{b}"])
                    norm_tile(xt[b], 128, G, SH, ht[b], f"xt{b}", f"ht{b}", "m")
                    for c in range(8):
                        P.g("tensor", "transpose", [f"ht{b}", "cst"], ["pT"], out=pT[:, c * 128:(c + 1) * 128],
                            in_=ht[b][:, c * 128:(c + 1) * 128], identity=ident, silent=(c != 7))
                    P.g("scalar", "activation", ["pT"], ["hT"], out=hT[:, :, col0 + t * 128: col0 + (t + 1) * 128],
                        in_=pT[:].rearrange("p (c n) -> p c n", c=8), func=AF.Copy)

            def proj_fm(ci, n0, n1, pdst, pkey):
                w, wk = load_w(ci)
                for c in range(8):
                    P.g("tensor", "matmul", [wk, "hT"], [pkey], pdst, lhsT=w[:, c, :], rhs=hT[:, c, n0:n1],
                        start=(c == 0), stop=(c == 7), silent=(c != 7))

            def rope_fm(ci, cis, dst, dkey, tcol):
                a = acnt[0] % 2; acnt[0] += 1
                proj_fm(ci, 1, 513, pA[a][:, 0:512], f"pA{a}")
                proj_fm(cis, 1, 513, pB[a][:], "pB0")
                P.g("scalar", "activation", ["pB0"], ["tq"], out=tq[:], in_=pB[a][:], func=AF.Copy)
                P.g("vector", "tensor_tensor", ["tq", "rp"], ["tq"], out=tq[:], in0=tq[:], in1=rp[:, 1, :], op=ALU.mult)
                P.g("vector", "tensor_tensor", [f"pA{a}", "rp"], ["tq2"], out=tq2[:], in0=pA[a][:, 0:512], in1=rp[:, 0, :], op=ALU.mult)
                P.g("vector", "tensor_tensor", ["tq", "tq2"], [dkey], out=dst, in0=tq[:], in1=tq2[:], op=ALU.add)

            def conv3(z, zk, ci, dst, dkey):
                cw = convw[:, ci - CHY, :]
                P.g("vector", "tensor_scalar", [zk, "convw"], [dkey], out=dst, in0=z[:, 1:513], scalar1=cw[:, 1:2], scalar2=cw[:, 3:4],
                    op0=ALU.mult, op1=ALU.add)
                P.g("vector", "scalar_tensor_tensor", [zk, "convw", dkey], [dkey], out=dst, in0=z[:, 0:512], scalar=cw[:, 0:1], in1=dst,
                    op0=ALU.mult, op1=ALU.add)
                P.g("vector", "scalar_tensor_tensor", [zk, "convw", dkey], [dkey], out=dst, in0=z[:, 2:514], scalar=cw[:, 2:3], in1=dst,
                    op0=ALU.mult, op1=ALU.add)

            def proj_z(ci):
                a = acnt[0] % 2; acnt[0] += 1
                zi = zcnt[0] % 2; zcnt[0] += 1
                w, wk = load_w(ci)
                for (c0, c1, o0) in ((0, 258, 0), (258, 514, 512)):
                    for c in range(8):
                        P.g("tensor", "matmul", [wk, "hT"], [f"pA{a}"], pA[a][:, o0:o0 + (c1 - c0)], lhsT=w[:, c, :],
                            rhs=hT[:, c, c0:c1], start=(c == 0), stop=(c == 7), silent=(c != 7))
                P.g("scalar", "activation", [f"pA{a}"], [f"ZT{zi}"], out=ZT[zi][:, 0:258], in_=pA[a][:, 0:258], func=AF.Copy)
                P.g("scalar", "activation", [f"pA{a}"], [f"ZT{zi}"], out=ZT[zi][:, 258:514], in_=pA[a][:, 512:768], func=AF.Copy)
                return ZT[zi], f"ZT{zi}"

            make_hT(ctx_d, 0, 2, MODC[:, 1024:2048], MODC[:, 0:1024], 1)
            proj_fm(CK, 1, 257, pA[0][:, 0:256], "pA0")
            P.g("scalar", "activation", ["pA0"], ["CKT"], out=CKT[:], in_=pA[0][:, 0:256], func=AF.Copy)
            w, wk = load_w(CV)
            for t in range(2):
                for c in range(8):
                    P.g("tensor", "matmul", [wk, "hT"], ["pB0"], pB[0][:, t * 128:(t + 1) * 128], lhsT=hT[:, c, 1 + t * 128: 1 + (t + 1) * 128],
                        rhs=w[:, c, :], start=(c == 0), stop=(c == 7), silent=(c != 7))
            P.g("scalar", "activation", ["pB0"], ["CV"], out=CVt[:], in_=pB[0][:, 0:256].rearrange("p (t n) -> p t n", t=2), func=AF.Copy)

            pTh = P.ps("pTh", [128, 16])
            for B in range(8):
                t0 = B * 512
                own = oblk0 <= B < oblk0 + 4
                kv = kblk0 <= B < kblk0 + 6
                make_hT(x_d, t0, 4, G1, SH1, 1)
                r0 = max(t0 - 1, 0); r1 = min(t0 + 512, L - 1)
                hb = xcnt[0] % 2; xcnt[0] += 1
                hx = xt[hb]; hh = ht[hb]
                P.dma(hx[0:1, :], x_d[r0:r0 + 1, :], writes=[f"xt{hb}"]); P.dma(hx[1:2, :], x_d[r1:r1 + 1, :], writes=[f"xt{hb}"])
                norm_tile(hx, 2, G1, SH1, hh, f"xt{hb}", f"ht{hb}", "h")
                for c in range(8):
                    P.g("tensor", "transpose", [f"ht{hb}", "cst"], ["pTh"], out=pTh[:, c * 2:(c + 1) * 2], in_=hh[0:2, c * 128:(c + 1) * 128],
                        identity=cst[0:2, 0:2], silent=(c != 7))
                pv = pTh[:].rearrange("p (c n) -> p c n", c=8)
                P.g("vector", "tensor_copy", ["pTh"], ["hT"], out=hT[:, :, 0:1], in_=pv[:, :, 0:1])
                P.g("vector", "tensor_copy", ["pTh"], ["hT"], out=hT[:, :, 513:514], in_=pv[:, :, 1:2])
                if B == 0:
                    P.g("vector", "tensor_copy", ["cst"], ["hT"], out=hT[:, :, 0:1], in_=cst[:, 664:672].unsqueeze(2))
                if B == 7:
                    P.g("vector", "tensor_copy", ["cst"], ["hT"], out=hT[:, :, 513:514], in_=cst[:, 664:672].unsqueeze(2))
                if own or kv:
                    P.dma(rp[:], rope_d[:, :, t0:t0 + 512].rearrange("a p n -> p a n"), writes=["rp"])
                if kv:
                    kc0 = (B - kblk0) * 512
                    rope_fm(CK, CKS, KT[:, kc0:kc0 + 512], "KT", t0)
                    w, wk = load_w(CV)
                    for t in range(4):
                        for c in range(8):
                            P.g("tensor", "matmul", [wk, "hT"], ["pB0"], pB[0][:, t * 128:(t + 1) * 128],
                                lhsT=hT[:, c, 1 + t * 128: 1 + (t + 1) * 128], rhs=w[:, c, :], start=(c == 0), stop=(c == 7), silent=(c != 7))
                    vt0 = (B - kblk0) * 4
                    P.g("scalar", "activation", ["pB0"], ["V"], out=V[:, vt0:vt0 + 4, :], in_=pB[0][:].rearrange("p (t n) -> p t n", t=4),
                        func=AF.Copy)
                if own:
                    oc0 = t0 - own0
                    for c in range(4):
                        rope_fm(CQ + c, CQS + c, QT[:, c, oc0:oc0 + 512], "QT", t0)
                    for c in range(4):
                        z, zk = proj_z(CHY + c)
                        b = 0
                        conv3(z, zk, CHY + c, x0s[b][:], f"x0s{b}")
                        P.dma(x0c_d[c, :, oc0:oc0 + 512], x0s[b][:], reads=[f"x0s{b}"], writes=["x0c_d"])
                for c in range(4):
                    z1, z1k = proj_z(CHY + 4 + c)
                    z2, z2k = proj_z(CHY + 8 + c)
                    conv3(z1, z1k, CHY + 4 + c, ca[:], "ca")
                    conv3(z2, z2k, CHY + 8 + c, cb[:], "cb")
                    P.g("vector", "tensor_tensor", ["ca", "cb"], ["uT"], out=uT[:, c, t0:t0 + 512], in0=ca[:], in1=cb[:], op=ALU.mult)
        if dbg and stop_after == 1:
            P.dma(dbg_d[:, 0:2048], QT[:, 0, :], reads=["QT"]); P.dma(dbg_d[:, 2048:4096], KT[:, 512:2560], reads=["KT"])
            P.finish(P.all_tokens()); P.emit(); return nc

        with P.scope():
            Sm = [P.sb("Sm0", [128, 648]), P.sb("Sm1", [128, 648])]
            Pm = [P.sb("Pm0", [128, 648]), P.sb("Pm1", [128, 648])]
            PTs = [P.sb("PTs0", [128, 640], F32R), P.sb("PTs1", [128, 640], F32R)]
            st = [P.sb("st0", [128, 4]), P.sb("st1", [128, 4])]
            Ysb = P.sb("Ysb", [128, 512]); YAs = [P.sb("YAs0", [128, 4, 128]), P.sb("YAs1", [128, 4, 128])]
            pS = [P.ps("pS0", [128, 1024]), P.ps("pS1", [128, 1024])]
            pPT0_ = P.ps("pPT0", [128, 1024]); pPT = [pPT0_, pPT0_]
            pY = P.ps("pY", [128, 512]); pYT = P.ps("pYT", [128, 512])
            for b in range(2):
                P.g("vector", "memset", [], [f"Sm{b}"], Sm[b][:], NEG)
            P.g("vector", "tensor_copy", ["cst"], ["KT"], out=KT[:, 0:512], in_=cst[:, 664:665].to_broadcast([128, 512]))
            it = 0
            cin = [P.sb(f"cin{i}", [128, 2, 1024]) for i in range(3)]
            cou = [P.sb(f"cou{i}", [128, 2, 1024], BF16) for i in range(3)]
            cast_steps = [(src, dst, r) for (src, dst) in ((pu_d, uv16_d[:, 0:D]), (pv_d, uv16_d[:, D:2 * D])) for r in range(64)]

            def cast_step(k):
                src, dst, r = cast_steps[k]
                i = k % 3
                P.dma(cin[i][:], src[r * 256:(r + 1) * 256, :].rearrange("(a p) d -> p a d", p=128), writes=[f"cin{i}"])
                P.g("scalar", "activation", [f"cin{i}"], [f"cou{i}"], out=cou[i][:], in_=cin[i][:], func=AF.Copy)
                P.dma(dst[r * 256:(r + 1) * 256, :].rearrange("(a p) d -> p a d", p=128), cou[i][:], reads=[f"cou{i}"], writes=["p16"])
            def geom(n):
                gt = 16 * half + n
                lo = 128 if gt == 0 else 0
                hi = 256 if gt == 31 else 384
                kc0 = (gt - 1) * 128 - kblk0 * 512
                vt0 = (gt - 1) - kblk0 * 4
                return lo, hi, kc0, vt0

            def stage_a(i):
                n, hd = divmod(i, 8)
                lo, hi, kc0, vt0 = geom(n)
                cast_step(i)
                b = i % 2
                c = hd % 4; po = (hd // 4) * 64
                qv = QT[po:po + 64, c, n * 128:(n + 1) * 128]
                P.g("tensor", "matmul", ["QT", "KT"], [f"pS{b}"], pS[b][:, 0:hi], lhsT=qv, rhs=KT[po:po + 64, kc0: kc0 + hi],
                    start=True, stop=True, silent=True)
                P.g("tensor", "matmul", ["QT", "CKT"], [f"pS{b}"], pS[b][:, 512:768], lhsT=qv, rhs=CKT[po:po + 64, :],
                    start=True, stop=True)
                if lo > 0:
                    P.g("vector", "memset", [], [f"Sm{b}"], Sm[b][:, 0:lo], NEG)
                if hi < 384:
                    P.g("vector", "memset", [], [f"Sm{b}"], Sm[b][:, hi:384], NEG)
                P.g("vector", "scalar_tensor_tensor", [f"pS{b}", "cst"], [f"Sm{b}"], out=Sm[b][:, lo:hi], in0=pS[b][:, lo:hi],
                    scalar=0.125, in1=bmask[:, lo:hi], op0=ALU.mult, op1=ALU.add)
                P.g("scalar", "activation", [f"pS{b}"], [f"Sm{b}"], out=Sm[b][:, 384:640], in_=pS[b][:, 512:768], func=AF.Copy, scale=0.125)
                P.g("vector", "tensor_copy", ["sink"], [f"Sm{b}"], out=Sm[b][:, 640:641], in_=sinkt[:, hd:hd + 1])
                P.g("vector", "tensor_reduce", [f"Sm{b}"], [f"st{b}"], out=st[b][:, 0:1], in_=Sm[b][:, 0:641], axis=AX.X, op=ALU.max, negate=True)

            def stage_b(i):
                n, hd = divmod(i, 8)
                lo, hi, kc0, vt0 = geom(n)
                b = i % 2
                po = (hd // 4) * 64
                P.g("scalar", "activation", [f"Sm{b}", f"st{b}"], [f"Pm{b}", f"st{b}b"], out=Pm[b][:, 0:641], in_=Sm[b][:, 0:641], func=AF.Exp,
                    bias=st[b][:, 0:1], scale=1.0, accum_out=st[b][:, 1:2])
                P.g("vector", "reciprocal", [f"st{b}b"], [f"st{b}c"], out=st[b][:, 2:3], in_=st[b][:, 1:2])
                P.g("vector", "tensor_scalar", [f"Pm{b}", f"st{b}c"], [f"Pm{b}"], out=Pm[b][:, 0:640], in0=Pm[b][:, 0:640], scalar1=st[b][:, 2:3],
                    scalar2=None, op0=ALU.mult)
                for kt in range(5):
                    P.g("tensor", "transpose", [f"Pm{b}", "cst"], ["pPT0"], out=pPT[b][:, kt * 128:(kt + 1) * 128],
                        in_=Pm[b][:, kt * 128:(kt + 1) * 128], identity=ident, silent=(kt != 4))
                P.g("scalar", "activation", ["pPT0"], [f"PTs{b}"], out=PTs[b][:], in_=pPT[b][:, 0:640], func=AF.Copy)
                mms = []
                for kt in range(3):
                    if lo <= kt * 128 < hi:
                        mms.append((kt, V[:, vt0 + kt, po:po + 64], "V"))
                mms.append((3, CVt[:, 0, po:po + 64], "CV")); mms.append((4, CVt[:, 1, po:po + 64], "CV"))
                for j, (kt, rv, rk) in enumerate(mms):
                    P.g("tensor", "matmul", [f"PTs{b}", rk], ["pY"], pY[:, hd * 64:(hd + 1) * 64], lhsT=PTs[b][:, kt * 128:(kt + 1) * 128], rhs=rv,
                        start=(j == 0), stop=(j == len(mms) - 1), silent=(j != len(mms) - 1))
                if hd == 7:
                    P.g("vector", "tensor_copy", ["pY"], ["Ysb"], out=Ysb[:], in_=pY[:])
                    for c in range(4):
                        P.g("tensor", "transpose", ["Ysb", "cst"], ["pYT"], out=pYT[:, c * 128:(c + 1) * 128], in_=Ysb[:, c * 128:(c + 1) * 128], identity=ident, silent=(c != 3))
                    yb = n % 2
                    P.g("scalar", "activation", ["pYT"], [f"YAs{yb}"], out=YAs[yb][:], in_=pYT[:].rearrange("p (c n) -> p c n", c=4), func=AF.Copy)
                    P.dma(ya_d[:, :, n * 128:(n + 1) * 128].rearrange("c p n -> p c n"), YAs[yb][:], reads=[f"YAs{yb}"], writes=["ya_d"])

            stage_a(0)
            for i in range(128):
                if i + 1 < 128:
                    stage_a(i + 1)
                stage_b(i)
    if dbg and stop_after == 2:
        P.dma(dbg_d[:, 0:2048], ya_d[0], reads=[]); P.dma(dbg_d[:, 2048:4096], ya_d[3], reads=[])
        P.finish(P.all_tokens()); P.emit(); return nc

    with P.scope():
        fw1 = P.sb("fw1", [33, 64]); fvec = P.sb("fvec", [64, 8]); fw2 = P.sb("fw2", [64, 64]); fw3 = P.sb("fw3", [64, 1024])
        tpos = P.sb("tpos", [128, 64])
        P.dma(fw1[:], fw1_d, writes=["fw"]); P.dma(fvec[:, 0:4], fvec_d, writes=["fvec"]); P.dma(fw2[:], fw2_d, writes=["fw"])
        P.dma(fw3[:], fw3_d, writes=["fw"]); P.dma(tpos[:], tpos_d, writes=["tpos"])
        P.g("vector", "tensor_tensor", ["fvec"], ["fvec2"], out=fvec[:, 4:5], in0=fvec[:, 0:1], in1=fvec[:, 2:3], op=ALU.mult)
        P.g("vector", "tensor_tensor", ["fvec"], ["fvec2"], out=fvec[:, 5:6], in0=fvec[:, 1:2], in1=fvec[:, 2:3], op=ALU.mult)
        RH = P.sb("RH", [128, 32, 768], BF16)
        YFr = P.sb("YFr", [128, 33, 256], BF16); YFi = P.sb("YFi", [128, 33, 256], BF16)
        rinv = P.sb("rinv", [128, 256])
        for ps_ in range(2):
            ch0 = ps_ * 256
            with P.scope():
                pU = [P.ps("pU0", [128, 256], BF16), P.ps("pU1", [128, 256], BF16)]
                for s in range(32):
                    b = s % 2
                    for c2 in range(2):
                        P.g("tensor", "transpose", ["uT", "cstb"], [f"pU{b}"], out=pU[b][:, c2 * 128:(c2 + 1) * 128],
                            in_=uT[:, ps_ * 2 + c2, s * 128:(s + 1) * 128], identity=cstb[:], silent=(c2 != 1))
                    P.g("scalar", "activation", [f"pU{b}"], ["RHu"], out=RH[:, s, 256:512], in_=pU[b][:], func=AF.Copy)
            with P.scope():
                zf = [P.sb("zf0", [33, 512]), P.sb("zf1", [33, 512])]
                h1 = [P.sb("h1a", [64, 512]), P.sb("h1b", [64, 512])]; h2 = [[P.sb(f"h2f{i}", [64, 512]), P.sb(f"h2b{i}", [64, 512])] for i in range(2)]
                wa = [P.sb("wa0", [64, 512]), P.sb("wa1", [64, 512])]; wb = [P.sb("wb0", [64, 512]), P.sb("wb1", [64, 512])]
                fb3p = P.sb("fb3p", [128, 512]); ndl = P.sb("ndl", [128, 256])
                dec = [P.sb("dec0", [128, 512]), P.sb("dec1", [128, 512])]
                hfd = [P.sb("hfd0", [128, 512]), P.sb("hfd1", [128, 512])]; ab = [P.sb("ab0", [128, 512]), P.sb("ab1", [128, 512])]
                pF = [P.ps("pF0", [128, 512]), P.ps("pF1", [128, 512])]; pH = [P.ps("pH0", [128, 512]), P.ps("pH1", [128, 512])]; pN = P.ps("pN", [128, 256])
                P.dma(fb3p[:, 0:256], fb3_d[:, ch0:ch0 + 256], writes=["fb3p"]); P.dma(fb3p[:, 256:512], fb3_d[:, 512 + ch0:512 + ch0 + 256], writes=["fb3p"])
                P.dma(ndl[:], ndelta_d[:, ch0:ch0 + 256], writes=["ndl"])

                def sin_layer(v, bias_col, dst, dkey):
                    A_, B_ = wa[v], wb[v]
                    P.g("vector", "tensor_scalar", [f"pF{v}", "fvec", "fvec2"], [f"wa{v}"], out=A_[:], in0=pF[v][0:64, :], scalar1=fvec[:, 2:3], scalar2=fvec[:, bias_col:bias_col + 1],
                        op0=ALU.mult, op1=ALU.add)
                    P.g("vector", "tensor_scalar", [f"wa{v}"], [f"wb{v}"], out=B_[:], in0=A_[:], scalar1=-math.pi, scalar2=2 * math.pi, op0=ALU.is_lt, op1=ALU.mult)
                    P.g("vector", "tensor_tensor", [f"wa{v}", f"wb{v}"], [f"wb{v}"], out=B_[:], in0=A_[:], in1=B_[:], op=ALU.add)
                    P.g("vector", "tensor_scalar", [f"wa{v}"], [f"wa{v}"], out=A_[:], in0=A_[:], scalar1=math.pi, scalar2=-2 * math.pi, op0=ALU.is_gt, op1=ALU.mult)
                    P.g("vector", "tensor_tensor", [f"wa{v}", f"wb{v}"], [f"wb{v}"], out=B_[:], in0=A_[:], in1=B_[:], op=ALU.add)
                    P.g("scalar", "activation", [f"wb{v}"], [dkey], out=dst, in_=B_[:], func=AF.Sin)

                def sin_stage(nb):
                    hb = nb % 2
                    for v in range(2):
                        P.dma(zf[v][:], zf_d[v, :, nb * 512:(nb + 1) * 512], writes=[f"zf{v}"])
                        P.g("tensor", "matmul", ["fw", f"zf{v}"], [f"pF{v}"], pF[v][0:64, :], lhsT=fw1[:], rhs=zf[v][:], start=True, stop=True)
                    for v in range(2):
                        sin_layer(v, 4, h1[v][:], f"h1{v}")
                    for v in range(2):
                        P.g("tensor", "matmul", ["fw", f"h1{v}"], [f"pF{v}"], pF[v][0:64, :], lhsT=fw2[:], rhs=h1[v][:], start=True, stop=True)
                    for v in range(2):
                        sin_layer(v, 5, h2[hb][v][:], f"h2{hb}{v}")

                def chunk_a(s):
                    nb, q = divmod(s, 4); hb = nb % 2; b = s % 2
                    P.g("tensor", "matmul", ["fw", f"h2{hb}0"], [f"pH{b}"], pH[b][:, 0:256], lhsT=h2[hb][0][:, q * 128:(q + 1) * 128], rhs=fw3[:, ch0:ch0 + 256], start=True, stop=True, silent=True)
                    P.g("tensor", "matmul", ["fw", f"h2{hb}1"], [f"pH{b}"], pH[b][:, 256:512], lhsT=h2[hb][1][:, q * 128:(q + 1) * 128], rhs=fw3[:, 512 + ch0:512 + ch0 + 256],
                        start=True, stop=True)
                    P.g("scalar", "activation", ["ndl", "tpos"], [f"dec{b}"], out=dec[b][:, 0:256], in_=ndl[:], func=AF.Exp, scale=tpos[:, s:s + 1])
                    P.g("scalar", "activation", ["ndl", "tpos"], [f"dec{b}"], out=dec[b][:, 256:512], in_=ndl[:], func=AF.Exp, scale=tpos[:, 32 + s:33 + s])
                    P.g("vector", "tensor_tensor", [f"pH{b}", "fb3p"], [f"hfd{b}"], out=hfd[b][:], in0=pH[b][:], in1=fb3p[:], op=ALU.add)
                    P.g("vector", "tensor_tensor", [f"hfd{b}", f"dec{b}"], [f"hfd{b}"], out=hfd[b][:], in0=hfd[b][:], in1=dec[b][:], op=ALU.mult)
                    if s == 0:
                        P.g("vector", "memset", [], [f"hfd{b}"], hfd[b][0:1, 256:512], 0.0)
                    P.g("scalar", "activation", [f"hfd{b}"], [f"ab{b}"], out=ab[b][:], in_=hfd[b][:], func=AF.Abs)
                    P.g("vector", "tensor_tensor", [f"hfd{b}"], ["RHke"], out=RH[:, s, 0:256], in0=hfd[b][:, 0:256], in1=hfd[b][:, 256:512], op=ALU.add)
                    P.g("gpsimd", "tensor_tensor", [f"hfd{b}"], ["RHko"], out=RH[:, s, 512:768], in0=hfd[b][:, 0:256], in1=hfd[b][:, 256:512], op=ALU.subtract)

                def chunk_b(s):
                    b = s % 2
                    P.g("tensor", "matmul", [f"ab{b}", "cst"], ["pN"], pN[:], lhsT=ones, rhs=ab[b][:, 0:256], start=(s == 0), stop=False, silent=True)
                    P.g("tensor", "matmul", [f"ab{b}", "cst"], ["pN"], pN[:], lhsT=ones, rhs=ab[b][:, 256:512], start=False, stop=(s == 31))

                sin_stage(0)
                for nb in range(8):
                    if nb + 1 < 8:
                        sin_stage(nb + 1)
                    for q in range(4):
                        s = nb * 4 + q
                        chunk_a(s)
                        if s > 0:
                            chunk_b(s - 1)
                chunk_b(31)
                P.g("vector", "reciprocal", ["pN"], ["rinv"], out=rinv[:], in_=pN[:])
            with P.scope():
                TC = [P.sb("TC0", [128, 32, 128], BF16), P.sb("TC1", [128, 32, 128], BF16)]
                TS = [P.sb("TS0", [128, 32, 128], BF16), P.sb("TS1", [128, 32, 128], BF16)]
                skp = P.sb("skp", [128, 256]); Ap = P.sb("Ap", [128, 256]); Bp = P.sb("Bp", [128, 256])
                t1 = P.sb("t1", [128, 256]); t2 = P.sb("t2", [128, 256])
                pC = [P.ps("pC0", [128, 512]), P.ps("pC1", [128, 512])]; pSn = [P.ps("pSn0", [128, 512]), P.ps("pSn1", [128, 512])]
                P.dma(skp[:], skip_d[:, ch0:ch0 + 256], writes=["skp"])
                for j in range(33):
                    b = j % 2
                    P.dma(TC[b][:], dftc_d[j], writes=[f"TC{b}"]); P.dma(TS[b][:], dfts_d[j], writes=[f"TS{b}"])
                    for s in range(32):
                        P.g("tensor", "matmul", [f"TC{b}", "RHu", "RHke"], [f"pC{b}"], pC[b][:], lhsT=TC[b][:, s, :], rhs=RH[:, s, 0:512], start=(s == 0), stop=(s == 31), silent=(s != 31))
                    for s in range(32):
                        P.g("tensor", "matmul", [f"TS{b}", "RHu", "RHko"], [f"pSn{b}"], pSn[b][:], lhsT=TS[b][:, s, :], rhs=RH[:, s, 256:768], start=(s == 0), stop=(s == 31), silent=(s != 31))
                    P.g("vector", "tensor_tensor", [f"pC{b}", "rinv"], ["Ap"], out=Ap[:], in0=pC[b][:, 0:256], in1=rinv[:], op=ALU.mult)
                    P.g("vector", "tensor_tensor", ["Ap", "skp"], ["Ap"], out=Ap[:], in0=Ap[:], in1=skp[:], op=ALU.add)
                    P.g("vector", "tensor_tensor", [f"pSn{b}", "rinv"], ["Bp"], out=Bp[:], in0=pSn[b][:, 256:512], in1=rinv[:], op=ALU.mult)
                    P.g("vector", "tensor_scalar", ["Bp", "cst"], ["Bp"], out=Bp[:], in0=Bp[:], scalar1=cst[:, 656:657], scalar2=None, op0=ALU.mult)
                    P.g("vector", "tensor_tensor", [f"pC{b}", "Ap"], ["t1"], out=t1[:], in0=pC[b][:, 256:512], in1=Ap[:], op=ALU.mult)
                    P.g("vector", "tensor_tensor", [f"pSn{b}", "Bp"], ["t2"], out=t2[:], in0=pSn[b][:, 0:256], in1=Bp[:], op=ALU.mult)
                    P.g("vector", "tensor_tensor", ["t1", "t2"], ["YF"], out=YFr[:, j, :], in0=t1[:], in1=t2[:], op=ALU.subtract)
                    P.g("vector", "tensor_tensor", [f"pC{b}", "Bp"], ["t1"], out=t1[:], in0=pC[b][:, 256:512], in1=Bp[:], op=ALU.mult)
                    P.g("vector", "tensor_tensor", [f"pSn{b}", "Ap"], ["t2"], out=t2[:], in0=pSn[b][:, 0:256], in1=Ap[:], op=ALU.mult)
                    P.g("vector", "tensor_tensor", ["t1", "t2"], ["YF"], out=YFi[:, j, :], in0=t1[:], in1=t2[:], op=ALU.add)
            with P.scope():
                IC = [P.sb("IC0", [128, 2048], BF16), P.sb("IC1", [128, 2048], BF16)]
                IS = [P.sb("IS0", [128, 2048], BF16), P.sb("IS1", [128, 2048], BF16)]
                x0c = P.sb("x0c", [128, 2048]); yst = [P.sb("yst0", [128, 512]), P.sb("yst1", [128, 512])]
                pO = [[P.ps(f"pO{cc}{tb}", [128, 512]) for tb in range(4)] for cc in range(2)]
                for kc in range(33):
                    b = kc % 2
                    P.dma(IC[b][:], idc_d[kc], writes=[f"IC{b}"]); P.dma(IS[b][:], ids_d[kc], writes=[f"IS{b}"])
                    for cc in range(2):
                        for tb in range(4):
                            P.g("tensor", "matmul", ["YF", f"IC{b}"], [f"pO{cc}{tb}"], pO[cc][tb][:], lhsT=YFr[:, kc, cc * 128:(cc + 1) * 128],
                                rhs=IC[b][:, tb * 512:(tb + 1) * 512], start=(kc == 0), stop=False, silent=True)
                            P.g("tensor", "matmul", ["YF", f"IS{b}"], [f"pO{cc}{tb}"], pO[cc][tb][:], lhsT=YFi[:, kc, cc * 128:(cc + 1) * 128],
                                rhs=IS[b][:, tb * 512:(tb + 1) * 512], start=False, stop=(kc == 32), silent=not (cc == 1 and tb == 3))
                i = 0
                for cc in range(2):
                    cg = ps_ * 2 + cc
                    P.dma(x0c[:], x0c_d[cg], reads=["x0c_d"], writes=["x0c"])
                    for tb in range(4):
                        b = i % 2; i += 1
                        P.g("vector", "tensor_tensor", [f"pO{cc}{tb}", "x0c"], [f"yst{b}"], out=yst[b][:], in0=pO[cc][tb][:], in1=x0c[:, tb * 512:(tb + 1) * 512], op=ALU.mult)
                        P.dma(yh_d[cg, :, tb * 512:(tb + 1) * 512], yst[b][:], reads=[f"yst{b}"], writes=["yh_d"])
    if dbg and stop_after == 3:
        P.dma(dbg_d[:, 0:2048], yh_d[0], reads=[]); P.dma(dbg_d[:, 2048:4096], yh_d[3], reads=[])
        P.finish(P.all_tokens()); P.emit(); return nc

    outer.__exit__(None, None, None)
    with P.scope():
        woa = P.sb("woa", [128, 4, 1024], F32R); woh = P.sb("woh", [128, 4, 1024], F32R); wout = P.sb("wout", [128, 8, 1024], F32R)
        Wf = [P.sb(f"Wf{i}", [128, 8, 128]) for i in range(2)]
        W = [P.sb(f"W{i}", [128, 8, 128], F32R) for i in range(2)]
        k_ = 0
        for (wt_, wd_, wk_, nk_) in ((woa, woa_d, "woa", 4), (woh, woh_d, "woh", 4), (wout, wout_d, "wout", 8)):
            for c in range(nk_):
                i = k_ % 2; k_ += 1
                P.dma(Wf[i][:].rearrange("p a b -> p (a b)"), wd_[:, c, :], writes=[f"Wf{i}"])
                P.g("gpsimd", "tensor_copy", [f"Wf{i}"], [wk_], out=wt_[:, c, :], in_=Wf[i][:].rearrange("p a b -> p (a b)"))
        xt = [P.sb("xt0", [128, 1024]), P.sb("xt1", [128, 1024])]; ht = [P.sb("ht0", [128, 1024]), P.sb("ht1", [128, 1024])]
        hT = P.sb("hT", [128, 8, 512], F32R)
        YA = P.sb("YA", [128, 4, 512]); YH = P.sb("YH", [128, 4, 512]); MT = P.sb("MT", [128, 8, 512], F32R)
        YAr = P.sb("YAr", [128, 4, 512], F32R); YHr = P.sb("YHr", [128, 4, 512], F32R)
        gs = [P.sb("gs0", [128, 512]), P.sb("gs1", [128, 512])]; m1 = P.sb("m1", [128, 512]); m2 = P.sb("m2", [128, 512])
        xm0_ = P.sb("xm0", [128, 1024]); xm = [xm0_, xm0_]
        pT = P.ps("pT", [128, 1024]); pG = [P.ps("pG0", [128, 512]), P.ps("pG1", [128, 512])]
        pM = [P.ps("pM0", [128, 512]), P.ps("pM1", [128, 512])]; pX = [P.ps("pX0", [128, 512]), P.ps("pX1", [128, 512])]
        wc = 0; xc = 0
        for B in range(4):
            t0 = own0 + B * 512
            xts = []
            for t in range(4):
                b = xc % 2; xc += 1
                P.dma(xt[b][:], x_d[t0 + t * 128: t0 + (t + 1) * 128, :], writes=[f"xt{b}"])
                norm_tile(xt[b], 128, G1, SH1, ht[b], f"xt{b}", f"ht{b}", "m4")
                for c in range(8):
                    P.g("tensor", "transpose", [f"ht{b}", "cst"], ["pT"], out=pT[:, c * 128:(c + 1) * 128], in_=ht[b][:, c * 128:(c + 1) * 128], identity=ident, silent=(c != 7))
                P.g("scalar", "activation", ["pT"], ["hT"], out=hT[:, :, t * 128:(t + 1) * 128], in_=pT[:].rearrange("p (c n) -> p c n", c=8), func=AF.Copy)
            P.dma(YA[:], ya_d[:, :, B * 512:(B + 1) * 512].rearrange("c p n -> p c n"), reads=["ya_d"], writes=["YA"])
            P.dma(YH[:], yh_d[:, :, B * 512:(B + 1) * 512].rearrange("c p n -> p c n"), reads=["yh_d"], writes=["YH"])
            P.g("gpsimd", "tensor_copy", ["YA"], ["YAr"], out=YAr[:], in_=YA[:])
            P.g("gpsimd", "tensor_copy", ["YH"], ["YHr"], out=YHr[:], in_=YH[:])
            for oc in range(8):
                for gi in range(2):
                    i = wc % 2; j_ = wc % 2; wc += 1
                    P.dma(Wf[j_][:], win_d[CG + gi * 8 + oc], writes=[f"Wf{j_}"])
                    if wc % 3 == 0:
                        P.g("gpsimd", "tensor_copy", [f"Wf{j_}"], [f"W{i}"], out=W[i][:], in_=Wf[j_][:])
                    else:
                        P.g("scalar", "activation", [f"Wf{j_}"], [f"W{i}"], out=W[i][:], in_=Wf[j_][:], func=AF.Copy)
                    for c in range(8):
                        P.g("tensor", "matmul", [f"W{i}", "hT"], [f"pG{gi}"], pG[gi][:], lhsT=W[i][:, c, :], rhs=hT[:, c, :], start=(c == 0), stop=(c == 7), silent=(c != 7))
                    P.g("scalar", "activation", [f"pG{gi}"], [f"gs{gi}"], out=gs[gi][:], in_=pG[gi][:], func=AF.Sigmoid)
                for c in range(4):
                    P.g("tensor", "matmul", ["woa", "YAr"], ["pM0"], pM[0][:], lhsT=woa[:, c, oc * 128:(oc + 1) * 128], rhs=YAr[:, c, :], start=(c == 0), stop=(c == 3), silent=(c != 3))
                for c in range(4):
                    P.g("tensor", "matmul", ["woh", "YHr"], ["pM1"], pM[1][:], lhsT=woh[:, c, oc * 128:(oc + 1) * 128], rhs=YHr[:, c, :], start=(c == 0), stop=(c == 3), silent=(c != 3))
                P.g("vector", "tensor_tensor", ["pM0", "gs0"], ["m1"], out=m1[:], in0=pM[0][:], in1=gs[0][:], op=ALU.mult)
                P.g("vector", "tensor_tensor", ["pM1", "gs1"], ["m2"], out=m2[:], in0=pM[1][:], in1=gs[1][:], op=ALU.mult)
                P.g("vector", "tensor_tensor", ["m1", "m2"], ["MT"], out=MT[:, oc, :], in0=m1[:], in1=m2[:], op=ALU.add)
            for t in range(4):
                b = xc % 2; xc += 1
                P.dma(xt[b][:], x_d[t0 + t * 128: t0 + (t + 1) * 128, :], writes=[f"xt{b}"])
                for hf in range(2):
                    for c in range(8):
                        P.g("tensor", "matmul", ["MT", "wout"], [f"pX{hf}"], pX[hf][:], lhsT=MT[:, c, t * 128:(t + 1) * 128], rhs=wout[:, c, hf * 512:(hf + 1) * 512],
                            start=(c == 0), stop=(c == 7), silent=(c != 7))
                    P.g("vector", "tensor_tensor", [f"pX{hf}"], ["xm0"], out=xm[b][:, hf * 512:(hf + 1) * 512], in0=pX[hf][:], in1=g1[:, hf * 512:(hf + 1) * 512], op=ALU.mult)
                P.g("vector", "tensor_tensor", ["xm0", f"xt{b}"], ["xm0"], out=xm[b][:], in0=xm[b][:], in1=xt[b][:], op=ALU.add)
                r = B * 512 + t * 128
                P.dma(xm_d[r:r + 128, :], xm[b][:], reads=["xm0"], writes=["xm_d"])
    if dbg and stop_after == 4:
        P.dma(dbg_d[:, 0:1024], xm_d[0:128, :], reads=[]); P.dma(dbg_d[:, 1024:2048], xm_d[1920:2048, :], reads=[])
        P.finish(P.all_tokens()); P.emit(); return nc

    out_tokens = []
    with P.scope():
        wq = P.sb("wq", [128, 8, 1024]); kbd = P.sb("kbd", [128, 8, 256])
        P.dma(wq[:], wq_d, writes=["wq"]); P.dma(kbd[:], kbd_d, writes=["kbd"])
        NBU = 12
        UV = [P.sb(f"UV{i}", [128, 2048], BF16) for i in range(NBU)]
        Dg = [P.sb(f"Dg{i}", [128, 128], BF16) for i in range(4)]
        xmt = [P.sb(f"xmt{i}", [128, 1024]) for i in range(3)]; h2 = [P.sb(f"h2_{i}", [128, 1024]) for i in range(3)]
        h2T = P.sb("h2T", [128, 8, 128]); QTs = P.sb("QTs", [128, 8, 128]); SCs = [P.sb("SCa", [128, 16, 128]), P.sb("SCb", [128, 16, 128])]; SC2 = P.sb("SC2", [128, 16, 128])
        V16 = P.sb("V16", [128, 16, 16]); I16 = P.sb("I16", [128, 16, 16], U32); I16f = P.sb("I16f", [128, 16, 16])
        cand = P.sb("cand", [128, 8, 256]); B16 = P.sb("B16", [128, 8, 16])
        cand2 = SC2[:].rearrange("p g k -> p (g k)").rearrange("p (h k) -> p h k", h=8)
        PI = P.sb("PI", [128, 8, 16], U32); PA = P.sb("PA", [128, 8, 16], U32); PB = P.sb("PB", [128, 8, 16], U32)
        paf = P.sb("paf", [128, 8, 16]); pbf = P.sb("pbf", [128, 8, 16]); OH = SC2[:].rearrange("p g k -> p (g k)").rearrange("p (h k) -> p h k", h=8)
        isel = P.sb("isel", [128, 128]); jsel = P.sb("jsel", [128, 128]); eif = P.sb("eif", [128, 128])
        EI = [P.sb("EI0", [128, 128], U32), P.sb("EI1", [128, 128], U32)]
        EG = P.sb("EG", [128, 8, 16]); EGs = [P.sb("EGs0", [128, 8, 16]), P.sb("EGs1", [128, 8, 16])]; gsm = P.sb("gsm", [128, 16]); GT = [P.sb("GT0", [128, 128]), P.sb("GT1", [128, 128])]
        Adot = [P.sb("Ad0", [128, 128]), P.sb("Ad1", [128, 128])]; junk = P.sb("junk", [128, 1024], BF16)
        gw = [P.sb("gwa", [128, 8]), P.sb("gwb", [128, 8])]; gw2 = [P.sb("gw2a", [128, 8]), P.sb("gw2b", [128, 8])]
        GA = [P.sb("GA0", [128, 128]), P.sb("GA1", [128, 128])]; acc0_ = P.sb("acc0", [128, 1024]); acc = [acc0_, acc0_]
        pT = P.ps("pT", [128, 1024]); pQ = P.ps("pQ", [128, 1024]); pSc = P.ps("pSc", [128, 1024]); pX = P.ps("pX", [128, 1024])
        cnt = {"u": 0, "v": 0, "d": 0}
        V16v = V16[:].rearrange("p (h t) k -> p h t k", t=2)
        I16v = I16f[:].rearrange("p (h t) k -> p h t k", t=2)
        NT = 1 if (dbg and stop_after == 5) else 16

        def front_pieces(n):
            b3 = n % 3; sb_ = n % 2; SC = SCs[sb_]

            def pa():
                P.dma(xmt[b3][:], xm_d[n * 128:(n + 1) * 128, :], reads=["xm_d"], writes=[f"xmt{b3}"])
                norm_tile(xmt[b3], 128, G2, SH2, h2[b3], f"xmt{b3}", f"h2{b3}", "p")

            def pb():
                for c in range(8):
                    P.g("tensor", "transpose", [f"h2{b3}", "cst"], ["pT"], out=pT[:, c * 128:(c + 1) * 128], in_=h2[b3][:, c * 128:(c + 1) * 128], identity=ident, silent=(c != 7))

            def pc():
                P.g("scalar", "activation", ["pT"], ["h2T"], out=h2T[:], in_=pT[:].rearrange("p (c n) -> p c n", c=8), func=AF.Copy)

            def pd(h0):
                def f():
                    for hd in range(h0, h0 + 4):
                        for c in range(8):
                            P.g("tensor", "matmul", ["wq", "h2T"], ["pQ"], pQ[:, hd * 128:(hd + 1) * 128], lhsT=wq[:, c, hd * 128:(hd + 1) * 128], rhs=h2T[:, c, :],
                                start=(c == 0), stop=(c == 7), silent=(c != 7))
                return f

            def pe():
                P.g("scalar", "activation", ["pQ"], ["QTs"], out=QTs[:], in_=pQ[:].rearrange("p (h n) -> p h n", h=8), func=AF.Copy)

            def pf(q):
                def f():
                    for h4 in range(4):
                        hd = q * 4 + h4
                        P.g("tensor", "matmul", ["QTs", "kbd"], ["pSc"], pSc[:, h4 * 256:(h4 + 1) * 256], lhsT=QTs[:, hd, :], rhs=kbd[:, hd, :], start=True, stop=True, silent=(h4 != 3))
                return f

            def pg(q):
                def f():
                    P.g("scalar", "activation", ["pSc"], [f"SC{sb_}_{q}"], out=SC[:, q * 8:(q + 1) * 8, :], in_=pSc[:].rearrange("p (g k) -> p g k", g=8), func=AF.Copy)
                return f
            return [pa, pb, pc, pd(0), pd(4), pe, pf(0), pg(0), pf(1), pg(1)]

        def front(n):
            for f in front_pieces(n):
                f()

        def routing_gen(n):
            b = n % 2; sb_ = n % 2; SC = SCs[sb_]
            for gI in range(16):
                P.g("vector", "max", [f"SC{sb_}_{gI // 8}"], [f"V16a{gI}"], out=V16[:, gI, 0:8], in_=SC[:, gI, :])
            yield
            for gI in range(16):
                P.g("vector", "match_replace", [f"SC{sb_}_{gI // 8}", f"V16a{gI}"], [f"SC2{gI}"], out=SC2[:, gI, :], in_to_replace=V16[:, gI, 0:8], in_values=SC[:, gI, :], imm_value=-1e30)
            yield
            for gI in range(16):
                P.g("vector", "max", [f"SC2{gI}"], [f"V16b{gI}"], out=V16[:, gI, 8:16], in_=SC2[:, gI, :])
            yield
            for gI in range(16):
                P.g("vector", "max_index", [f"SC{sb_}_{gI // 8}", f"V16a{gI}"], [f"I16a{gI}"], out=I16[:, gI, 0:8], in_max=V16[:, gI, 0:8], in_values=SC[:, gI, :])
            yield
            for gI in range(16):
                P.g("vector", "max_index", [f"SC{sb_}_{gI // 8}", f"V16b{gI}"], [f"I16b{gI}"], out=I16[:, gI, 8:16], in_max=V16[:, gI, 8:16], in_values=SC[:, gI, :])
            yield
            allV = [f"V16a{g_}" for g_ in range(16)] + [f"V16b{g_}" for g_ in range(16)]
            allI = [f"I16a{g_}" for g_ in range(16)] + [f"I16b{g_}" for g_ in range(16)]
            P.g("vector", "tensor_copy", allI, ["I16f"], out=I16f[:], in_=I16[:])
            P.g("vector", "tensor_tensor", allV + [f"cand2{h_}" for h_ in range(8)], ["cand"], out=cand[:].rearrange("p h (a c) -> p h a c", a=16),
                in0=V16v[:, :, 0, :].unsqueeze(3).to_broadcast([128, 8, 16, 16]), in1=V16v[:, :, 1, :].unsqueeze(2).to_broadcast([128, 8, 16, 16]), op=ALU.add)
            yield
            for hd in range(8):
                P.g("vector", "max", ["cand"], [f"B16a{hd}"], out=B16[:, hd, 0:8], in_=cand[:, hd, :])
            for hd in range(8):
                P.g("vector", "match_replace", ["cand", f"B16a{hd}"], [f"cand2{hd}"], out=cand2[:, hd, :], in_to_replace=B16[:, hd, 0:8], in_values=cand[:, hd, :], imm_value=-1e30)
            yield
            for hd in range(8):
                P.g("vector", "max", [f"cand2{hd}"], [f"B16b{hd}"], out=B16[:, hd, 8:16], in_=cand2[:, hd, :])
            for hd in range(8):
                P.g("vector", "max_index", ["cand", f"B16a{hd}"], [f"PIa{hd}"], out=PI[:, hd, 0:8], in_max=B16[:, hd, 0:8], in_values=cand[:, hd, :])
            yield
            for hd in range(8):
                P.g("vector", "max_index", ["cand", f"B16b{hd}"], [f"PIb{hd}"], out=PI[:, hd, 8:16], in_max=B16[:, hd, 8:16], in_values=cand[:, hd, :])
            allB = [f"B16a{h_}" for h_ in range(8)] + [f"B16b{h_}" for h_ in range(8)]
            allP = [f"PIa{h_}" for h_ in range(8)] + [f"PIb{h_}" for h_ in range(8)]
            P.g("vector", "tensor_single_scalar", allP, ["PA"], out=PA[:], in_=PI[:], scalar=4, op=ALU.logical_shift_right)
            P.g("vector", "tensor_single_scalar", allP, ["PB"], out=PB[:], in_=PI[:], scalar=15, op=ALU.bitwise_and)
            P.g("vector", "tensor_copy", ["PA"], ["paf"], out=paf[:], in_=PA[:])
            P.g("vector", "tensor_copy", ["PB"], ["pbf"], out=pbf[:], in_=PB[:])
            yield
            for (pf_, pk, tt, dst, dk) in ((paf, "paf", 0, isel, "isel"), (pbf, "pbf", 1, jsel, "jsel")):
                OHv = OH.rearrange("p h (k a) -> p h k a", k=16)
                P.g("vector", "tensor_tensor", [pk, "cst"] + [f"cand2{h_}" for h_ in range(8)], ["OH"], out=OHv, in0=iota16.unsqueeze(1).unsqueeze(1).to_broadcast([128, 8, 16, 16]),
                    in1=pf_[:].unsqueeze(3).to_broadcast([128, 8, 16, 16]), op=ALU.is_equal)
                yield
                P.g("vector", "tensor_tensor", ["OH", "I16f"], ["OH"], out=OHv, in0=OHv, in1=I16v[:, :, tt, :].unsqueeze(2).to_broadcast([128, 8, 16, 16]), op=ALU.mult)
                P.g("vector", "tensor_reduce", ["OH"], [dk], out=dst[:], in_=OH.rearrange("p h (k a) -> p (h k) a", k=16), axis=AX.X, op=ALU.add)
                yield
            P.g("vector", "scalar_tensor_tensor", ["isel", "jsel"], ["eif"], out=eif[:], in0=isel[:], scalar=128.0, in1=jsel[:], op0=ALU.mult, op1=ALU.add)
            P.g("vector", "tensor_copy", ["eif"], [f"EI{b}"], out=EI[b][:], in_=eif[:])
            P.g("vector", "tensor_tensor", allB, [f"EGs{b}"], out=EGs[b][:], in0=B16[:], in1=B16[:, :, 0:1].to_broadcast([128, 8, 16]), op=ALU.subtract)

        def routing(n):
            for _ in routing_gen(n):
                pass

        def routing_b(n):
            b = n % 2
            P.g("scalar", "activation", [f"EGs{b}"], ["EG"], out=EG[:], in_=EGs[b][:], func=AF.Exp)
            P.g("vector", "tensor_reduce", ["EG"], ["gsm"], out=gsm[:, 0:8], in_=EG[:], axis=AX.X, op=ALU.add)
            P.g("vector", "reciprocal", ["gsm"], ["gsm2"], out=gsm[:, 8:16], in_=gsm[:, 0:8])
            P.g("vector", "tensor_tensor", ["EG", "gsm2"], [f"GT{b}"], out=GT[b][:].rearrange("p (h k) -> p h k", h=8), in0=EG[:],
                in1=gsm[:, 8:16].unsqueeze(2).to_broadcast([128, 8, 16]), op=ALU.mult)

        def tile_body(n):
            b = n % 2; b3 = n % 3; SC = SCs[n % 2]
            fp = front_pieces(n + 2) if n + 2 < NT else None
            rg = routing_gen(n + 1) if n + 1 < NT else None
            for g in range(16):
                gb = g % 2
                ks = []
                for j in range(8):
                    s = g * 8 + j
                    k = cnt["u"] % NBU; cnt["u"] += 1; ks.append(k)
                    P.gather(UV[k][:], uv16_d, EI[b][:, s:s + 1], reads=[f"EI{b}", "p16"], writes=[f"UV{k}"])
                    P.g("vector", "scalar_tensor_tensor", [f"UV{k}", f"h2{b3}"], ["junk", f"Ad{b}_{s}"], out=junk[:], in0=UV[k][:, 0:1024], scalar=1.0, in1=h2[b3][:],
                        op0=ALU.mult, op1=ALU.mult, accum_out=Adot[b][:, s:s + 1])
                akeys = [f"Ad{b}_{g * 8 + j}" for j in range(8)]
                av = Adot[b][:, g * 8:(g + 1) * 8]
                P.g("vector", "tensor_tensor", akeys, [f"gw{gb}"], out=gw[gb][:], in0=av, in1=av, op=ALU.mult)
                P.g("vector", "tensor_scalar", [f"gw{gb}"], [f"gw{gb}"], out=gw[gb][:], in0=gw[gb][:], scalar1=0.044715, scalar2=1.0, op0=ALU.mult, op1=ALU.add)
                P.g("vector", "tensor_tensor", [f"gw{gb}"] + akeys, [f"gw{gb}"], out=gw[gb][:], in0=gw[gb][:], in1=av, op=ALU.mult)
                P.g("scalar", "activation", [f"gw{gb}"], [f"gw2{gb}"], out=gw2[gb][:], in_=gw[gb][:], func=AF.Sigmoid, scale=2.0 * math.sqrt(2.0 / math.pi))
                P.g("vector", "tensor_tensor", [f"gw2{gb}"] + akeys, [f"gw2{gb}"], out=gw2[gb][:], in0=gw2[gb][:], in1=av, op=ALU.mult)
                P.g("vector", "tensor_tensor", [f"gw2{gb}", f"GT{b}"], [f"GA{b}_{g}"], out=GA[b][:, g * 8:(g + 1) * 8], in0=gw2[gb][:], in1=GT[b][:, g * 8:(g + 1) * 8], op=ALU.mult)
                for j in range(8):
                    s = g * 8 + j; k = ks[j]
                    kd = cnt["d"] % 4; cnt["d"] += 1
                    P.g("scalar", "activation", ["cst", f"GA{b}_{g}"], [f"Dg{kd}"], out=Dg[kd][:], in_=ident, func=AF.Copy, scale=GA[b][:, s:s + 1])
                    for hf in range(2):
                        P.g("tensor", "matmul", [f"Dg{kd}", f"UV{k}"], ["pX"], pX[:, hf * 512:(hf + 1) * 512], lhsT=Dg[kd][:], rhs=UV[k][:, 1024 + hf * 512:1024 + (hf + 1) * 512],
                            start=(s == 0), stop=(s == 127), silent=(hf == 0))
                if fp is not None and g < len(fp):
                    fp[g]()
                if rg is not None:
                    if next(rg, "done") == "done":
                        rg = None
            if dbg and stop_after == 5:
                P.barrier()
                P.g("vector", "tensor_copy", [], ["acc0"], out=acc[0][:], in_=pX[:])
                P.barrier()
                P.dma(dbg_d[:, 0:2048], SC[:].rearrange("p g k -> p (g k)"), reads=[])
                P.dma(dbg_d[:, 2048:2304], V16[:].rearrange("p g k -> p (g k)"), reads=[])
                P.dma(dbg_d[:, 2304:2560], I16f[:].rearrange("p g k -> p (g k)"), reads=[])
                P.dma(dbg_d[:, 2560:2688], B16[:].rearrange("p g k -> p (g k)"), reads=[])
                P.dma(dbg_d[:, 2688:2816], eif[:], reads=[])
                P.dma(dbg_d[:, 2816:2944], GT[0][:], reads=[])
                P.dma(dbg_d[:, 2944:3072], Adot[0][:], reads=[])
                P.dma(dbg_d[:, 3072:4096], acc[0][:], reads=[])
                P.barrier()
            if rg is not None:
                for _ in rg:
                    pass
            if n + 1 < NT:
                routing_b(n + 1)
            P.g("vector", "tensor_tensor", ["pX"], ["acc0"], out=acc[b][:], in0=pX[:], in1=g2, op=ALU.mult)
            P.g("vector", "tensor_tensor", ["acc0", f"xmt{b3}"], ["acc0"], out=acc[b][:], in0=acc[b][:], in1=xmt[b3][:], op=ALU.add)
            norm_tile(acc[b], 128, FG, None, xmt[b3], "acc0", f"xmt{b3}", "f")
            out_tokens.append(P.dma(out_d[n * 128:(n + 1) * 128, :], xmt[b3][:], reads=[f"xmt{b3}"], writes=["out_d"]))

        front(0); routing(0); routing_b(0)
        if NT > 1:
            front(1)
        for n in range(NT):
            tile_body(n)
    P.finish(P.all_tokens())
    P.emit()
    return nc


_CONST_CACHE = {}


def _constants():
    if _CONST_CACHE:
        return _CONST_CACHE
    N = 2 * L
    base = np.cos(2 * np.pi * np.arange(N) / N)
    bases = np.sin(2 * np.pi * np.arange(N) / N)
    s = np.arange(L, dtype=np.int64)[:, None]
    k = np.arange(33 * 128, dtype=np.int64)[None, :]
    idx = (s * k) % N
    valid = (k <= L)
    C = np.where(valid, base[idx], 0.0).astype(np.float32)
    S = np.where(valid, bases[idx], 0.0).astype(np.float32)
    def fwd(T):
        return np.ascontiguousarray(T.reshape(32, 128, 33, 128).transpose(2, 1, 0, 3)).astype(ml_dtypes.bfloat16)
    _CONST_CACHE["dftc"] = fwd(C); _CONST_CACHE["dfts"] = fwd(S)
    kk = np.arange(33 * 128, dtype=np.int64)[:, None]
    t = np.arange(L, dtype=np.int64)[None, :]
    idx2 = (kk * t) % N
    wk = np.where((kk == 0) | (kk == L), 1.0, 2.0) / N
    wk = np.where(kk <= L, wk, 0.0)
    IC = (base[idx2] * wk).astype(np.float32).reshape(33, 128, L)
    IS = (bases[idx2] * wk).astype(np.float32).reshape(33, 128, L)
    _CONST_CACHE["idc"] = [np.ascontiguousarray(IC[:, :, h * 2048:(h + 1) * 2048]).astype(ml_dtypes.bfloat16) for h in range(2)]
    _CONST_CACHE["ids"] = [np.ascontiguousarray(IS[:, :, h * 2048:(h + 1) * 2048]).astype(ml_dtypes.bfloat16) for h in range(2)]
    f = 16
    inv = (10000.0 ** (-np.arange(f, dtype=np.float32) / f)).astype(np.float32)
    tok = np.arange(L)
    row = (tok // 64).astype(np.float32); col = (tok % 64).astype(np.float32)
    cosT = np.zeros((64, L), np.float32); sinT = np.zeros((64, L), np.float32)
    for d in range(64):
        pos = row if d < 32 else col
        j = d % 32
        ang = pos * inv[j % 16]
        cosT[d] = np.cos(ang)
        sinT[d] = -np.sin(ang) if j < 16 else np.sin(ang)
    _CONST_CACHE["rope"] = np.ascontiguousarray(np.stack([np.tile(cosT, (2, 1)), np.tile(sinT, (2, 1))])).astype(np.float32)
    def feats(tt):
        bands = np.arange(1, 17, dtype=np.float32)
        ang = (2.0 * np.pi * tt[:, None] * bands[None, :]).astype(np.float32)
        return np.concatenate([tt[:, None], np.cos(ang), np.sin(ang)], axis=-1).astype(np.float32)
    tt = (np.arange(L, dtype=np.float32) / L).astype(np.float32)
    tts = (np.maximum(np.arange(L) - 1, 0).astype(np.float32) / L).astype(np.float32)
    _CONST_CACHE["zf"] = np.ascontiguousarray(np.stack([feats(tt).T, feats(tts).T])).astype(np.float32)
    deltas = np.linspace(math.log(1e-2) / 1.5, math.log(1e-2) / 0.3, 512, dtype=np.float32)
    _CONST_CACHE["ndelta"] = np.ascontiguousarray(np.tile(-np.abs(deltas)[None, :], (128, 1))).astype(np.float32)
    tp = np.zeros((128, 64), np.float32)
    tp[:, 0:32] = tt.reshape(32, 128).T
    tp[:, 32:64] = tts.reshape(32, 128).T
    _CONST_CACHE["tpos"] = tp
    cst = np.zeros((128, 1024), np.float32)
    cst[:, 0:128] = np.eye(128, dtype=np.float32)
    i = np.arange(128)[:, None]; j = np.arange(384)[None, :]
    cst[:, 128:512] = np.where((j >= i) & (j <= i + 256), 0.0, NEG)
    cst[:, 512:528] = np.arange(16, dtype=np.float32)[None, :]
    cst[:, 528:656] = 1.0
    _CONST_CACHE["consts"] = cst
    _CONST_CACHE["constsb"] = np.eye(128, dtype=np.float32).astype(ml_dtypes.bfloat16)
    return _CONST_CACHE


def _chunk_rows(w, nk):
    return np.ascontiguousarray(w.reshape(nk, 128, -1).transpose(1, 0, 2))


def prepare_inputs(inp):
    cs = _constants()
    f32 = lambda a: np.ascontiguousarray(np.asarray(a, dtype=np.float32))
    w_in = f32(inp["w_in"])[0]
    swap64 = np.concatenate([np.arange(16, 32), np.arange(0, 16), np.arange(48, 64), np.arange(32, 48)])
    cols = []
    for c in range(4):
        cols.append(np.concatenate([c * 64 + np.arange(64), (4 + c) * 64 + np.arange(64)]))
    for c in range(4):
        cols.append(np.concatenate([c * 64 + swap64, (4 + c) * 64 + swap64]))
    cols.append(512 + np.arange(128))
    cols.append(512 + np.concatenate([swap64, 64 + swap64]))
    cols.append(640 + np.arange(128))
    for c in range(12):
        cols.append(768 + c * 128 + np.arange(128))
    for c in range(16):
        cols.append(2304 + c * 128 + np.arange(128))
    wch = np.stack([_chunk_rows(w_in[:, cc], 8) for cc in cols])
    w_mod = f32(inp["w_mod"])[0]
    wm = np.ascontiguousarray(w_mod.reshape(8, 128, 12, 512).transpose(2, 1, 0, 3))
    bc = lambda v: np.ascontiguousarray(np.tile(f32(v).reshape(1, -1), (128, 1)))
    ng = np.ascontiguousarray(np.stack([bc(inp["norm1_g"][0]), bc(inp["norm2_g"][0]), bc(inp["final_g"])], axis=1))
    convw = np.zeros((128, 12, 4), np.float32)
    cw = f32(inp["hy_conv_w"])[0]; cbias = f32(inp["hy_conv_b"])[0]
    for j in range(3):
        convw[:, :, j] = cw[j].reshape(12, 128).T
    convw[:, :, 3] = cbias.reshape(12, 128).T
    fvec = np.stack([f32(inp["hy_fb1"])[0], f32(inp["hy_fb2"])[0], f32(inp["hy_freq"])[0], np.zeros(64, np.float32)], axis=1)
    keys = f32(inp["peer_keys"])[0]
    kbd = np.zeros((128, 8, 256), np.float32)
    for h in range(8):
        for p in range(2):
            kbd[p * 64:(p + 1) * 64, h, p * 128:(p + 1) * 128] = keys[h, p].T
    shared = {
        "w_mod": wm, "b_mod": bc(inp["b_mod"][0]), "ng": ng, "w_in": wch, "rope": cs["rope"], "sink": bc(inp["attn_sink"][0]),
        "convw": convw, "fw1": f32(inp["hy_fw1"])[0], "fvec": np.ascontiguousarray(fvec), "fw2": f32(inp["hy_fw2"])[0],
        "fw3": f32(inp["hy_fw3"])[0], "fb3": bc(inp["hy_fb3"][0]), "skip": bc(inp["hy_skip"][0]), "zf": cs["zf"],
        "ndelta": cs["ndelta"], "tpos": cs["tpos"], "w_oa": _chunk_rows(f32(inp["w_o_attn"])[0], 4),
        "w_oh": _chunk_rows(f32(inp["w_o_hy"])[0], 4), "w_out": _chunk_rows(f32(inp["w_out"])[0], 8),
        "wq": _chunk_rows(f32(inp["peer_wq"])[0], 8), "kbd": kbd, "peer_u": f32(inp["peer_u"])[0], "peer_v": f32(inp["peer_v"])[0],
        "dftc": cs["dftc"], "dfts": cs["dfts"], "consts": cs["consts"], "constsb": cs["constsb"],
    }
    x = f32(inp["x"]); ctx = f32(inp["ctx"]); c = f32(inp["c"]); c_ctx = f32(inp["c_ctx"])
    maps = []
    for core in range(8):
        b, half = core // 2, core % 2
        cvec = np.concatenate([c[b].reshape(8, 128).T, c_ctx.reshape(8, 128).T], axis=1)
        m = dict(shared)
        cst = cs["consts"].copy()
        cst[:, 656] = 1.0 if half == 0 else -1.0
        xb = x[b] if half == 0 else np.ascontiguousarray(x[b][::-1])
        m.update({"x": xb, "ctx": ctx[b], "cvec": np.ascontiguousarray(cvec), "idftc": cs["idc"][0], "idfts": cs["ids"][0], "consts": cst})
        if half == 1:
            m["rope"] = np.ascontiguousarray(cs["rope"][:, :, ::-1])
            m["convw"] = np.ascontiguousarray(convw[:, :, [2, 1, 0, 3]])
        maps.append(m)
    return maps


_PROG_CACHE = {}


def kernel(**inputs):
    maps = prepare_inputs(inputs)
    nc = build_program(0)
    res = run_bass_kernel_spmd(nc, maps, core_ids=list(range(8)))
    out = np.zeros((4, L, D), np.float32)
    for core in range(8):
        b, half = core // 2, core % 2
        o = res.results[core]["out"]
        if half == 0:
            out[b, 0:2048] = o
        else:
            out[b, 2048:4096] = o[::-1]
    return out
```

```python
from contextlib import ExitStack, contextmanager
import math
import numpy as np
import ml_dtypes
import concourse.bass as bass
import concourse.mybir as mybir
from concourse.bass_utils import run_bass_kernel_spmd

F32 = mybir.dt.float32
BF16 = mybir.dt.bfloat16
U32 = mybir.dt.uint32
F32R = mybir.dt.float32r
AF = mybir.ActivationFunctionType
ALU = mybir.AluOpType
AX = mybir.AxisListType
ENGS = ["sync", "scalar", "vector", "gpsimd", "tensor"]

L = 4096
D = 1024
NEG = -30000.0


class Prog:
    def __init__(self, nc, n_dma_sync=24, n_dma_pool=32):
        self.nc = nc
        self.es = ExitStack()
        self.stacks = [self.es]
        self.ops = {e: [] for e in ENGS}
        self.cnt = {e: 0 for e in ENGS}
        self.sem = {e: self.es.enter_context(nc.semaphore("s_" + e)) for e in ENGS}
        self.dpool = {
            "sync": [self.es.enter_context(nc.semaphore(f"ds{i}")) for i in range(n_dma_sync)],
            "gpsimd": [self.es.enter_context(nc.semaphore(f"dg{i}")) for i in range(n_dma_pool)],
        }
        self.dval = {q: [0] * len(p) for q, p in self.dpool.items()}
        self.dnext = {q: 0 for q in self.dpool}
        self.waited = {e: {} for e in ENGS}
        self.lastw = {}
        self.readers = {}
        self.uid = 0

    def sb(self, name, shape, dtype=F32):
        self.uid += 1
        return self.stacks[-1].enter_context(self.nc.sbuf_tensor(f"{name}_{self.uid}", list(shape), dtype))

    def ps(self, name, shape, dtype=F32):
        self.uid += 1
        return self.stacks[-1].enter_context(self.nc.psum_tensor(f"{name}_{self.uid}", list(shape), dtype))

    @contextmanager
    def scope(self):
        st = ExitStack()
        self.stacks.append(st)
        try:
            yield
        finally:
            self.barrier()
            self.stacks.pop()
            st.close()

    def all_tokens(self):
        toks = [("c" + e, self.sem[e], self.cnt[e]) for e in ENGS if self.cnt[e] > 0]
        for q, pool in self.dpool.items():
            for j, sem in enumerate(pool):
                if self.dval[q][j] > 0:
                    toks.append((f"d{q}{j}", sem, self.dval[q][j]))
        return toks

    def barrier(self):
        toks = self.all_tokens()
        for e in ENGS:
            waits = []
            for sid, sem, val in toks:
                if self.waited[e].get(sid, 0) < val:
                    waits.append((sem, val))
                    self.waited[e][sid] = val
            self.ops[e].append((waits, None, None))
        self.lastw = {}
        self.readers = {}

    def _deps(self, eng, reads, writes):
        toks = []
        for k in reads:
            t = self.lastw.get(k)
            if t is not None:
                toks.append(t)
        for k in writes:
            t = self.lastw.get(k)
            if t is not None:
                toks.append(t)
            toks.extend(self.readers.get(k, ()))
        wd = self.waited[eng]
        best = {}
        for (sid, sem, val) in toks:
            if sid == "c" + eng and val > self.cnt[eng]:
                continue
            if wd.get(sid, 0) < val and best.get(sid, (None, 0))[1] < val:
                best[sid] = (sem, val)
        waits = []
        for sid, (sem, val) in best.items():
            wd[sid] = val
            waits.append((sem, val))
        return waits

    def _commit(self, tok, reads, writes):
        for k in writes:
            self.lastw[k] = tok
            self.readers[k] = []
        for k in reads:
            self.readers.setdefault(k, []).append(tok)

    def op(self, eng, fn, reads=(), writes=(), silent=False):
        reads, writes = list(reads), list(writes)
        waits = self._deps(eng, reads, writes)
        if silent:
            tok = ("c" + eng, self.sem[eng], self.cnt[eng] + 1)
            self.ops[eng].append((waits, fn, None))
        else:
            self.cnt[eng] += 1
            tok = ("c" + eng, self.sem[eng], self.cnt[eng])
            self.ops[eng].append((waits, fn, (self.sem[eng], 1)))
        self._commit(tok, reads, writes)
        return tok

    def g(self, eng, name, reads, writes, *args, silent=False, **kw):
        return self.op(eng, lambda e: getattr(e, name)(*args, **kw), reads, writes, silent=silent)

    def dma(self, out, in_, reads=(), writes=(), q="sync", fn=None):
        reads, writes = list(reads), list(writes)
        waits = self._deps(q, reads, writes)
        j = self.dnext[q]
        self.dnext[q] = (j + 1) % len(self.dpool[q])
        sem = self.dpool[q][j]
        sid = f"d{q}{j}"
        prev = self.dval[q][j]
        if prev > 0 and self.waited[q].get(sid, 0) < prev:
            waits.append((sem, prev))
            self.waited[q][sid] = prev
        self.dval[q][j] = prev + 16
        tok = (sid, sem, prev + 16)
        if fn is None:
            fn = lambda e: e.dma_start(out=out, in_=in_)
        self.ops[q].append((waits, fn, (sem, 16)))
        self._commit(tok, reads, writes)
        return tok

    def gather(self, out, table, idx_ap, reads, writes):
        fn = lambda e: e.indirect_dma_start(out=out, out_offset=None, in_=table,
                                            in_offset=bass.IndirectOffsetOnAxis(ap=idx_ap, axis=0))
        return self.dma(None, None, reads, writes, q="gpsimd", fn=fn)

    def finish(self, tokens, eng="sync"):
        waits = []
        for (sid, sem, val) in tokens:
            if self.waited[eng].get(sid, 0) < val:
                waits.append((sem, val))
                self.waited[eng][sid] = val
        self.ops[eng].append((waits, None, None))

    def emit(self):
        with self.nc.Block() as block:
            def mk(engname):
                def body(e):
                    for waits, fn, inc in self.ops[engname]:
                        for sem, val in waits:
                            e.wait_ge(sem, val)
                        if fn is not None:
                            ins = fn(e)
                            if inc is not None:
                                ins.then_inc(inc[0], inc[1])
                return body
            block.sync(mk("sync"))
            block.scalar(mk("scalar"))
            block.vector(mk("vector"))
            block.gpsimd(mk("gpsimd"))
            block.tensor(mk("tensor"))
        for st in reversed(self.stacks):
            st.close()


CQ, CQS, CK, CKS, CV, CHY, CG = 0, 4, 8, 9, 10, 11, 23
NWCH = 39


def build_program(half, stop_after=None, dbg=False):
    nc = bass.Bass("TRN2", target_bir_lowering=False)

    def din(name, shape, dt=F32):
        return nc.dram_tensor(name, list(shape), dt, kind="ExternalInput").ap()

    x_d = din("x", [L, D]); ctx_d = din("ctx", [256, D]); cvec_d = din("cvec", [128, 16])
    wmod_d = din("w_mod", [12, 128, 8, 512]); bmod_d = din("b_mod", [128, 6144]); ng_d = din("ng", [128, 3, 1024])
    win_d = din("w_in", [NWCH, 128, 8, 128]); rope_d = din("rope", [2, 128, L]); sink_d = din("sink", [128, 8])
    convw_d = din("convw", [128, 12, 4]); fw1_d = din("fw1", [33, 64]); fvec_d = din("fvec", [64, 4])
    fw2_d = din("fw2", [64, 64]); fw3_d = din("fw3", [64, 1024]); fb3_d = din("fb3", [128, 1024])
    skip_d = din("skip", [128, 512]); zf_d = din("zf", [2, 33, L]); ndelta_d = din("ndelta", [128, 512])
    tpos_d = din("tpos", [128, 64]); woa_d = din("w_oa", [128, 4, 1024]); woh_d = din("w_oh", [128, 4, 1024])
    wout_d = din("w_out", [128, 8, 1024]); wq_d = din("wq", [128, 8, 1024]); kbd_d = din("kbd", [128, 8, 256])
    pu_d = din("peer_u", [16384, D]); pv_d = din("peer_v", [16384, D])
    dftc_d = din("dftc", [33, 128, 32, 128], BF16); dfts_d = din("dfts", [33, 128, 32, 128], BF16)
    idc_d = din("idftc", [33, 128, 2048], BF16); ids_d = din("idfts", [33, 128, 2048], BF16)
    cst_d = din("consts", [128, 1024]); cstb_d = din("constsb", [128, 128], BF16)
    out_d = nc.dram_tensor("out", [2048, D], F32, kind="ExternalOutput").ap()
    ya_d = nc.dram_tensor("ya_s", [4, 128, 2048], F32, kind="Internal").ap()
    yh_d = nc.dram_tensor("yh_s", [4, 128, 2048], F32, kind="Internal").ap()
    x0c_d = nc.dram_tensor("x0c_s", [4, 128, 2048], F32, kind="Internal").ap()
    xm_d = nc.dram_tensor("xm_s", [2048, D], F32, kind="Internal").ap()
    dbg_d = nc.dram_tensor("dbg", [128, 4096], F32, kind="ExternalOutput").ap() if dbg else None
    uv16_d = nc.dram_tensor("uv16_s", [16384, 2 * D], BF16, kind="Internal").ap()

    P = Prog(nc)
    own0 = 2048 * half
    oblk0 = 4 * half
    kblk0 = oblk0 - 1

    cst = P.sb("cst", [128, 1024])
    cstb = P.sb("cstb", [128, 128], BF16)
    P.dma(cst[:], cst_d, writes=["cst"]); P.dma(cstb[:], cstb_d, writes=["cstb"])
    ident = cst[:, 0:128]; bmask = cst[:, 128:512]; iota16 = cst[:, 512:528]; ones = cst[:, 528:656]
    MODX = P.sb("MODX", [128, 6144])
    FG = P.sb("FG", [128, 1024])
    P.dma(FG[:], ng_d[:, 2, :], writes=["FG"])
    outer = P.scope(); outer.__enter__()
    MODC = P.sb("MODC", [128, 2048])
    uT = P.sb("uT", [128, 4, L], BF16)

    def norm_tile(xt, np_, G, SH, out, key_x, key_out, tag):
        ss = P.sb("ss" + tag, [128, 4]) if tag not in norm_tile.cache else norm_tile.cache[tag]
        norm_tile.cache[tag] = ss
        k = "ss" + tag
        P.g("vector", "scalar_tensor_tensor", [key_x], [key_out, k], out=out[0:np_, :], in0=xt[0:np_, :], scalar=1.0, in1=xt[0:np_, :],
            op0=ALU.mult, op1=ALU.mult, accum_out=ss[0:np_, 0:1])
        P.g("vector", "tensor_scalar", [k], [k + "b"], out=ss[0:np_, 1:2], in0=ss[0:np_, 0:1], scalar1=1.0 / D, scalar2=1e-6,
            op0=ALU.mult, op1=ALU.add)
        P.g("scalar", "activation", [k + "b"], [k + "c"], out=ss[0:np_, 2:3], in_=ss[0:np_, 1:2], func=AF.Sqrt)
        P.g("vector", "reciprocal", [k + "c"], [k + "d"], out=ss[0:np_, 3:4], in_=ss[0:np_, 2:3])
        if SH is None:
            P.g("vector", "scalar_tensor_tensor", [key_x, k + "d"], [key_out], out=out[0:np_, :], in0=xt[0:np_, :],
                scalar=ss[0:np_, 3:4], in1=G[0:np_, :], op0=ALU.mult, op1=ALU.mult)
        else:
            P.g("vector", "scalar_tensor_tensor", [key_x, k + "d"], [key_out], out=out[0:np_, :], in0=xt[0:np_, :],
                scalar=ss[0:np_, 3:4], in1=G[0:np_, :], op0=ALU.mult, op1=ALU.mult)
            P.g("vector", "tensor_tensor", [key_out], [key_out], out=out[0:np_, :], in0=out[0:np_, :], in1=SH[0:np_, :], op=ALU.add)
    norm_tile.cache = {}

    with P.scope():
        cv = P.sb("cv", [128, 16]); cs = P.sb("cs", [128, 16]); CB = P.sb("CB", [128, 2, 8, 128])
        ngt = P.sb("ngt", [128, 2, 1024])
        wm = [P.sb("wm0", [128, 8, 512]), P.sb("wm1", [128, 8, 512])]
        bm = [P.sb("bm0", [128, 512]), P.sb("bm1", [128, 512])]
        pm = [P.ps("pm0", [128, 512]), P.ps("pm1", [128, 512])]
        P.dma(cv[:], cvec_d, writes=["cv"]); P.dma(ngt[:], ng_d[:, 0:2, :], writes=["ngt"])
        P.g("scalar", "activation", ["cv"], ["cs"], out=cs[:], in_=cv[:], func=AF.Silu)
        for v in range(2):
            P.g("vector", "tensor_copy", ["cs"], ["CB"], out=CB[:, v, :, :],
                in_=cs[:, v * 8:(v + 1) * 8].unsqueeze(2).to_broadcast([128, 8, 128]))
        it = 0
        for v, nblk, dst in ((0, 12, MODX), (1, 4, MODC)):
            for j in range(nblk):
                b = it % 2; it += 1
                P.dma(wm[b][:], wmod_d[j], writes=[f"wm{b}"])
                P.dma(bm[b][:], bmod_d[:, j * 512:(j + 1) * 512], writes=[f"bm{b}"])
                for c in range(8):
                    P.g("tensor", "matmul", ["CB", f"wm{b}"], [f"pm{b}"], pm[b][:], lhsT=CB[:, v, c, :], rhs=wm[b][:, c, :],
                        start=(c == 0), stop=(c == 7), silent=(c != 7))
                P.g("vector", "tensor_tensor", [f"pm{b}", f"bm{b}"], ["MOD"], out=dst[:, j * 512:(j + 1) * 512], in0=pm[b][:],
                    in1=bm[b][:], op=ALU.add)
        for dst, lo, gi in ((MODX, 1024, 0), (MODX, 4096, 1), (MODC, 1024, 0)):
            P.g("vector", "scalar_tensor_tensor", ["MOD", "ngt"], ["MOD"], out=dst[:, lo:lo + 1024], in0=dst[:, lo:lo + 1024],
                scalar=1.0, in1=ngt[:, gi, :], op0=ALU.add, op1=ALU.mult)
    SH1 = MODX[:, 0:1024]; G1 = MODX[:, 1024:2048]; g1 = MODX[:, 2048:3072]
    SH2 = MODX[:, 3072:4096]; G2 = MODX[:, 4096:5120]; g2 = MODX[:, 5120:6144]
    if dbg and stop_after == 0:
        t = P.dma(dbg_d[:, 0:4096], MODX[:, 0:4096], reads=[])
        P.finish(P.all_tokens()); P.emit(); return nc

    with P.scope():
        QT = P.sb("QT", [128, 4, 2048]); KT = P.sb("KT", [128, 6 * 512]); V = P.sb("V", [128, 24, 128], F32R)
        CKT = P.sb("CKT", [128, 256]); CVt = P.sb("CV", [128, 2, 128], F32R)
        sinkt = P.sb("sinkt", [128, 8]); convw = P.sb("convw", [128, 12, 4])
        P.dma(sinkt[:], sink_d, writes=["sink"]); P.dma(convw[:], convw_d, writes=["convw"])
        with P.scope():
            xt = [P.sb("xt0", [128, 1024]), P.sb("xt1", [128, 1024])]
            ht = [P.sb("ht0", [128, 1024]), P.sb("ht1", [128, 1024])]
            hT = P.sb("hT", [128, 8, 514], F32R)
            Wf = [P.sb(f"Wf{i}", [128, 8, 128]) for i in range(3)]
            W = [P.sb(f"W{i}", [128, 8, 128], F32R) for i in range(3)]
            rp = P.sb("rp", [128, 2, 512])
            ZT = [P.sb(f"ZT{i}", [128, 514]) for i in range(2)]
            ca = P.sb("ca", [128, 512]); cb = P.sb("cb", [128, 512]); x0s_ = P.sb("x0s0", [128, 512]); x0s = [x0s_, x0s_]
            tq = P.sb("tq", [128, 512]); tq2 = P.sb("tq2", [128, 512])
            pT = P.ps("pT", [128, 1024]); pA = [P.ps("pA0", [128, 1024]), P.ps("pA1", [128, 1024])]
            pB0_ = P.ps("pB0", [128, 512]); pB = [pB0_, pB0_]
            wcnt = [0]; zcnt = [0]; acnt = [0]; xcnt = [0]

            def load_w(ci):
                i = wcnt[0] % 3; wcnt[0] += 1
                P.dma(Wf[i][:], win_d[ci], writes=[f"Wf{i}"])
                if wcnt[0] % 3 == 0:
                    P.g("gpsimd", "tensor_copy", [f"Wf{i}"], [f"W{i}"], out=W[i][:], in_=Wf[i][:])
                else:
                    P.g("scalar", "activation", [f"Wf{i}"], [f"W{i}"], out=W[i][:], in_=Wf[i][:], func=AF.Copy)
                return W[i], f"W{i}"

            def make_hT(src_d, row0, ntile, G, SH, col0):
                for t in range(ntile):
                    b = xcnt[0] % 2; xcnt[0] += 1
                    P.dma(xt[b][:], src_d[row0 + t * 128: row0 + (t + 1) * 128, :], writes=[f"xt{b}"])
                    norm_tile(xt[b], 128, G, SH, ht[b], f"xt{b}", f"ht{b}", "m")
                    for c in range(8):
                        P.g("tensor", "transpose", [f"ht{b}", "cst"], ["pT"], out=pT[:, c * 128:(c + 1) * 128],
                            in_=ht[b][:, c * 128:(c + 1) * 128], identity=ident, silent=(c != 7))
                    P.g("scalar", "activation", ["pT"], ["hT"], out=hT[:, :, col0 + t * 128: col0 + (t + 1) * 128],
                        in_=pT[:].rearrange("p (c n) -> p c n", c=8), func=AF.Copy)

            def proj_fm(ci, n0, n1, pdst, pkey):
                w, wk = load_w(ci)
                for c in range(8):
                    P.g("tensor", "matmul", [wk, "hT"], [pkey], pdst, lhsT=w[:, c, :], rhs=hT[:, c, n0:n1],
                        start=(c == 0), stop=(c == 7), silent=(c != 7))

            def rope_fm(ci, cis, dst, dkey, tcol):
                a = acnt[0] % 2; acnt[0] += 1
                proj_fm(ci, 1, 513, pA[a][:, 0:512], f"pA{a}")
                proj_fm(cis, 1, 513, pB[a][:], "pB0")
                P.g("scalar", "activation", ["pB0"], ["tq"], out=tq[:], in_=pB[a][:], func=AF.Copy)
                P.g("vector", "tensor_tensor", ["tq", "rp"], ["tq"], out=tq[:], in0=tq[:], in1=rp[:, 1, :], op=ALU.mult)
                P.g("vector", "tensor_tensor", [f"pA{a}", "rp"], ["tq2"], out=tq2[:], in0=pA[a][:, 0:512], in1=rp[:, 0, :], op=ALU.mult)
                P.g("vector", "tensor_tensor", ["tq", "tq2"], [dkey], out=dst, in0=tq[:], in1=tq2[:], op=ALU.add)

            def conv3(z, zk, ci, dst, dkey):
                cw = convw[:, ci - CHY, :]
                P.g("vector", "tensor_scalar", [zk, "convw"], [dkey], out=dst, in0=z[:, 1:513], scalar1=cw[:, 1:2], scalar2=cw[:, 3:4],
                    op0=ALU.mult, op1=ALU.add)
                P.g("vector", "scalar_tensor_tensor", [zk, "convw", dkey], [dkey], out=dst, in0=z[:, 0:512], scalar=cw[:, 0:1], in1=dst,
                    op0=ALU.mult, op1=ALU.add)
                P.g("vector", "scalar_tensor_tensor", [zk, "convw", dkey], [dkey], out=dst, in0=z[:, 2:514], scalar=cw[:, 2:3], in1=dst,
                    op0=ALU.mult, op1=ALU.add)

            def proj_z(ci):
                a = acnt[0] % 2; acnt[0] += 1
                zi = zcnt[0] % 2; zcnt[0] += 1
                w, wk = load_w(ci)
                for (c0, c1, o0) in ((0, 258, 0), (258, 514, 512)):
                    for c in range(8):
                        P.g("tensor", "matmul", [wk, "hT"], [f"pA{a}"], pA[a][:, o0:o0 + (c1 - c0)], lhsT=w[:, c, :],
                            rhs=hT[:, c, c0:c1], start=(c == 0), stop=(c == 7), silent=(c != 7))
                P.g("scalar", "activation", [f"pA{a}"], [f"ZT{zi}"], out=ZT[zi][:, 0:258], in_=pA[a][:, 0:258], func=AF.Copy)
                P.g("scalar", "activation", [f"pA{a}"], [f"ZT{zi}"], out=ZT[zi][:, 258:514], in_=pA[a][:, 512:768], func=AF.Copy)
                return ZT[zi], f"ZT{zi}"

            make_hT(ctx_d, 0, 2, MODC[:, 1024:2048], MODC[:, 0:1024], 1)
            proj_fm(CK, 1, 257, pA[0][:, 0:256], "pA0")
            P.g("scalar", "activation", ["pA0"], ["CKT"], out=CKT[:], in_=pA[0][:, 0:256], func=AF.Copy)
            w, wk = load_w(CV)
            for t in range(2):
                for c in range(8):
                    P.g("tensor", "matmul", [wk, "hT"], ["pB0"], pB[0][:, t * 128:(t + 1) * 128], lhsT=hT[:, c, 1 + t * 128: 1 + (t + 1) * 128],
                        rhs=w[:, c, :], start=(c == 0), stop=(c == 7), silent=(c != 7))
            P.g("scalar", "activation", ["pB0"], ["CV"], out=CVt[:], in_=pB[0][:, 0:256].rearrange("p (t n) -> p t n", t=2), func=AF.Copy)

            pTh = P.ps("pTh", [128, 16])
            for B in range(8):
                t0 = B * 512
                own = oblk0 <= B < oblk0 + 4
                kv = kblk0 <= B < kblk0 + 6
                make_hT(x_d, t0, 4, G1, SH1, 1)
                r0 = max(t0 - 1, 0); r1 = min(t0 + 512, L - 1)
                hb = xcnt[0] % 2; xcnt[0] += 1
                hx = xt[hb]; hh = ht[hb]
                P.dma(hx[0:1, :], x_d[r0:r0 + 1, :], writes=[f"xt{hb}"]); P.dma(hx[1:2, :], x_d[r1:r1 + 1, :], writes=[f"xt{hb}"])
                norm_tile(hx, 2, G1, SH1, hh, f"xt{hb}", f"ht{hb}", "h")
                for c in range(8):
                    P.g("tensor", "transpose", [f"ht{hb}", "cst"], ["pTh"], out=pTh[:, c * 2:(c + 1) * 2], in_=hh[0:2, c * 128:(c + 1) * 128],
                        identity=cst[0:2, 0:2], silent=(c != 7))
                pv = pTh[:].rearrange("p (c n) -> p c n", c=8)
                P.g("vector", "tensor_copy", ["pTh"], ["hT"], out=hT[:, :, 0:1], in_=pv[:, :, 0:1])
                P.g("vector", "tensor_copy", ["pTh"], ["hT"], out=hT[:, :, 513:514], in_=pv[:, :, 1:2])
                if B == 0:
                    P.g("vector", "tensor_copy", ["cst"], ["hT"], out=hT[:, :, 0:1], in_=cst[:, 664:672].unsqueeze(2))
                if B == 7:
                    P.g("vector", "tensor_copy", ["cst"], ["hT"], out=hT[:, :, 513:514], in_=cst[:, 664:672].unsqueeze(2))
                if own or kv:
                    P.dma(rp[:], rope_d[:, :, t0:t0 + 512].rearrange("a p n -> p a n"), writes=["rp"])
                if kv:
                    kc0 = (B - kblk0) * 512
                    rope_fm(CK, CKS, KT[:, kc0:kc0 + 512], "KT", t0)
                    w, wk = load_w(CV)
                    for t in range(4):
                        for c in range(8):
                            P.g("tensor", "matmul", [wk, "hT"], ["pB0"], pB[0][:, t * 128:(t + 1) * 128],
                                lhsT=hT[:, c, 1 + t * 128: 1 + (t + 1) * 128], rhs=w[:, c, :], start=(c == 0), stop=(c == 7), silent=(c != 7))
                    vt0 = (B - kblk0) * 4
                    P.g("scalar", "activation", ["pB0"], ["V"], out=V[:, vt0:vt0 + 4, :], in_=pB[0][:].rearrange("p (t n) -> p t n", t=4),
                        func=AF.Copy)
                if own:
                    oc0 = t0 - own0
                    for c in range(4):
                        rope_fm(CQ + c, CQS + c, QT[:, c, oc0:oc0 + 512], "QT", t0)
                    for c in range(4):
                        z, zk = proj_z(CHY + c)
                        b = 0
                        conv3(z, zk, CHY + c, x0s[b][:], f"x0s{b}")
                        P.dma(x0c_d[c, :, oc0:oc0 + 512], x0s[b][:], reads=[f"x0s{b}"], writes=["x0c_d"])
                for c in range(4):
                    z1, z1k = proj_z(CHY + 4 + c)
                    z2, z2k = proj_z(CHY + 8 + c)
                    conv3(z1, z1k, CHY + 4 + c, ca[:], "ca")
                    conv3(z2, z2k, CHY + 8 + c, cb[:], "cb")
                    P.g("vector", "tensor_tensor", ["ca", "cb"], ["uT"], out=uT[:, c, t0:t0 + 512], in0=ca[:], in1=cb[:], op=ALU.mult)
        if dbg and stop_after == 1:
            P.dma(dbg_d[:, 0:2048], QT[:, 0, :], reads=["QT"]); P.dma(dbg_d[:, 2048:4096], KT[:, 512:2560], reads=["KT"])
            P.finish(P.all_tokens()); P.emit(); return nc

        with P.scope():
            Sm = [P.sb("Sm0", [128, 648]), P.sb("Sm1", [128, 648])]
            Pm = [P.sb("Pm0", [128, 648]), P.sb("Pm1", [128, 648])]
            PTs = [P.sb("PTs0", [128, 640], F32R), P.sb("PTs1", [128, 640], F32R)]
            st = [P.sb("st0", [128, 4]), P.sb("st1", [128, 4])]
            Ysb = P.sb("Ysb", [128, 512]); YAs = [P.sb("YAs0", [128, 4, 128]), P.sb("YAs1", [128, 4, 128])]
            pS = [P.ps("pS0", [128, 1024]), P.ps("pS1", [128, 1024])]
            pPT0_ = P.ps("pPT0", [128, 1024]); pPT = [pPT0_, pPT0_]
            pY = P.ps("pY", [128, 512]); pYT = P.ps("pYT", [128, 512])
            for b in range(2):
                P.g("vector", "memset", [], [f"Sm{b}"], Sm[b][:], NEG)
            P.g("vector", "tensor_copy", ["cst"], ["KT"], out=KT[:, 0:512], in_=cst[:, 664:665].to_broadcast([128, 512]))
            it = 0
            cin = [P.sb(f"cin{i}", [128, 2, 1024]) for i in range(3)]
            cou = [P.sb(f"cou{i}", [128, 2, 1024], BF16) for i in range(3)]
            cast_steps = [(src, dst, r) for (src, dst) in ((pu_d, uv16_d[:, 0:D]), (pv_d, uv16_d[:, D:2 * D])) for r in range(64)]

            def cast_step(k):
                src, dst, r = cast_steps[k]
                i = k % 3
                P.dma(cin[i][:], src[r * 256:(r + 1) * 256, :].rearrange("(a p) d -> p a d", p=128), writes=[f"cin{i}"])
                P.g("scalar", "activation", [f"cin{i}"], [f"cou{i}"], out=cou[i][:], in_=cin[i][:], func=AF.Copy)
                P.dma(dst[r * 256:(r + 1) * 256, :].rearrange("(a p) d -> p a d", p=128), cou[i][:], reads=[f"cou{i}"], writes=["p16"])
            def geom(n):
                gt = 16 * half + n
                lo = 128 if gt == 0 else 0
                hi = 256 if gt == 31 else 384
                kc0 = (gt - 1) * 128 - kblk0 * 512
                vt0 = (gt - 1) - kblk0 * 4
                return lo, hi, kc0, vt0

            def stage_a(i):
                n, hd = divmod(i, 8)
                lo, hi, kc0, vt0 = geom(n)
                cast_step(i)
                b = i % 2
                c = hd % 4; po = (hd // 4) * 64
                qv = QT[po:po + 64, c, n * 128:(n + 1) * 128]
                P.g("tensor", "matmul", ["QT", "KT"], [f"pS{b}"], pS[b][:, 0:hi], lhsT=qv, rhs=KT[po:po + 64, kc0: kc0 + hi],
                    start=True, stop=True, silent=True)
                P.g("tensor", "matmul", ["QT", "CKT"], [f"pS{b}"], pS[b][:, 512:768], lhsT=qv, rhs=CKT[po:po + 64, :],
                    start=True, stop=True)
                if lo > 0:
                    P.g("vector", "memset", [], [f"Sm{b}"], Sm[b][:, 0:lo], NEG)
                if hi < 384:
                    P.g("vector", "memset", [], [f"Sm{b}"], Sm[b][:, hi:384], NEG)
                P.g("vector", "scalar_tensor_tensor", [f"pS{b}", "cst"], [f"Sm{b}"], out=Sm[b][:, lo:hi], in0=pS[b][:, lo:hi],
                    scalar=0.125, in1=bmask[:, lo:hi], op0=ALU.mult, op1=ALU.add)
                P.g("scalar", "activation", [f"pS{b}"], [f"Sm{b}"], out=Sm[b][:, 384:640], in_=pS[b][:, 512:768], func=AF.Copy, scale=0.125)
                P.g("vector", "tensor_copy", ["sink"], [f"Sm{b}"], out=Sm[b][:, 640:641], in_=sinkt[:, hd:hd + 1])
                P.g("vector", "tensor_reduce", [f"Sm{b}"], [f"st{b}"], out=st[b][:, 0:1], in_=Sm[b][:, 0:641], axis=AX.X, op=ALU.max, negate=True)

            def stage_b(i):
                n, hd = divmod(i, 8)
                lo, hi, kc0, vt0 = geom(n)
                b = i % 2
                po = (hd // 4) * 64
                P.g("scalar", "activation", [f"Sm{b}", f"st{b}"], [f"Pm{b}", f"st{b}b"], out=Pm[b][:, 0:641], in_=Sm[b][:, 0:641], func=AF.Exp,
                    bias=st[b][:, 0:1], scale=1.0, accum_out=st[b][:, 1:2])
                P.g("vector", "reciprocal", [f"st{b}b"], [f"st{b}c"], out=st[b][:, 2:3], in_=st[b][:, 1:2])
                P.g("vector", "tensor_scalar", [f"Pm{b}", f"st{b}c"], [f"Pm{b}"], out=Pm[b][:, 0:640], in0=Pm[b][:, 0:640], scalar1=st[b][:, 2:3],
                    scalar2=None, op0=ALU.mult)
                for kt in range(5):
                    P.g("tensor", "transpose", [f"Pm{b}", "cst"], ["pPT0"], out=pPT[b][:, kt * 128:(kt + 1) * 128],
                        in_=Pm[b][:, kt * 128:(kt + 1) * 128], identity=ident, silent=(kt != 4))
                P.g("scalar", "activation", ["pPT0"], [f"PTs{b}"], out=PTs[b][:], in_=pPT[b][:, 0:640], func=AF.Copy)
                mms = []
                for kt in range(3):
                    if lo <= kt * 128 < hi:
                        mms.append((kt, V[:, vt0 + kt, po:po + 64], "V"))
                mms.append((3, CVt[:, 0, po:po + 64], "CV")); mms.append((4, CVt[:, 1, po:po + 64], "CV"))
                for j, (kt, rv, rk) in enumerate(mms):
                    P.g("tensor", "matmul", [f"PTs{b}", rk], ["pY"], pY[:, hd * 64:(hd + 1) * 64], lhsT=PTs[b][:, kt * 128:(kt + 1) * 128], rhs=rv,
                        start=(j == 0), stop=(j == len(mms) - 1), silent=(j != len(mms) - 1))
                if hd == 7:
                    P.g("vector", "tensor_copy", ["pY"], ["Ysb"], out=Ysb[:], in_=pY[:])
                    for c in range(4):
                        P.g("tensor", "transpose", ["Ysb", "cst"], ["pYT"], out=pYT[:, c * 128:(c + 1) * 128], in_=Ysb[:, c * 128:(c + 1) * 128], identity=ident, silent=(c != 3))
                    yb = n % 2
                    P.g("scalar", "activation", ["pYT"], [f"YAs{yb}"], out=YAs[yb][:], in_=pYT[:].rearrange("p (c n) -> p c n", c=4), func=AF.Copy)
                    P.dma(ya_d[:, :, n * 128:(n + 1) * 128].rearrange("c p n -> p c n"), YAs[yb][:], reads=[f"YAs{yb}"], writes=["ya_d"])

            stage_a(0)
            for i in range(128):
                if i + 1 < 128:
                    stage_a(i + 1)
                stage_b(i)
    if dbg and stop_after == 2:
        P.dma(dbg_d[:, 0:2048], ya_d[0], reads=[]); P.dma(dbg_d[:, 2048:4096], ya_d[3], reads=[])
        P.finish(P.all_tokens()); P.emit(); return nc

    with P.scope():
        fw1 = P.sb("fw1", [33, 64]); fvec = P.sb("fvec", [64, 8]); fw2 = P.sb("fw2", [64, 64]); fw3 = P.sb("fw3", [64, 1024])
        tpos = P.sb("tpos", [128, 64])
        P.dma(fw1[:], fw1_d, writes=["fw"]); P.dma(fvec[:, 0:4], fvec_d, writes=["fvec"]); P.dma(fw2[:], fw2_d, writes=["fw"])
        P.dma(fw3[:], fw3_d, writes=["fw"]); P.dma(tpos[:], tpos_d, writes=["tpos"])
        P.g("vector", "tensor_tensor", ["fvec"], ["fvec2"], out=fvec[:, 4:5], in0=fvec[:, 0:1], in1=fvec[:, 2:3], op=ALU.mult)
        P.g("vector", "tensor_tensor", ["fvec"], ["fvec2"], out=fvec[:, 5:6], in0=fvec[:, 1:2], in1=fvec[:, 2:3], op=ALU.mult)
        RH = P.sb("RH", [128, 32, 768], BF16)
        YFr = P.sb("YFr", [128, 33, 256], BF16); YFi = P.sb("YFi", [128, 33, 256], BF16)
        rinv = P.sb("rinv", [128, 256])
        for ps_ in range(2):
            ch0 = ps_ * 256
            with P.scope():
                pU = [P.ps("pU0", [128, 256], BF16), P.ps("pU1", [128, 256], BF16)]
                for s in range(32):
                    b = s % 2
                    for c2 in range(2):
                        P.g("tensor", "transpose", ["uT", "cstb"], [f"pU{b}"], out=pU[b][:, c2 * 128:(c2 + 1) * 128],
                            in_=uT[:, ps_ * 2 + c2, s * 128:(s + 1) * 128], identity=cstb[:], silent=(c2 != 1))
                    P.g("scalar", "activation", [f"pU{b}"], ["RHu"], out=RH[:, s, 256:512], in_=pU[b][:], func=AF.Copy)
            with P.scope():
                zf = [P.sb("zf0", [33, 512]), P.sb("zf1", [33, 512])]
                h1 = [P.sb("h1a", [64, 512]), P.sb("h1b", [64, 512])]; h2 = [[P.sb(f"h2f{i}", [64, 512]), P.sb(f"h2b{i}", [64, 512])] for i in range(2)]
                wa = [P.sb("wa0", [64, 512]), P.sb("wa1", [64, 512])]; wb = [P.sb("wb0", [64, 512]), P.sb("wb1", [64, 512])]
                fb3p = P.sb("fb3p", [128, 512]); ndl = P.sb("ndl", [128, 256])
                dec = [P.sb("dec0", [128, 512]), P.sb("dec1", [128, 512])]
                hfd = [P.sb("hfd0", [128, 512]), P.sb("hfd1", [128, 512])]; ab = [P.sb("ab0", [128, 512]), P.sb("ab1", [128, 512])]
                pF = [P.ps("pF0", [128, 512]), P.ps("pF1", [128, 512])]; pH = [P.ps("pH0", [128, 512]), P.ps("pH1", [128, 512])]; pN = P.ps("pN", [128, 256])
                P.dma(fb3p[:, 0:256], fb3_d[:, ch0:ch0 + 256], writes=["fb3p"]); P.dma(fb3p[:, 256:512], fb3_d[:, 512 + ch0:512 + ch0 + 256], writes=["fb3p"])
                P.dma(ndl[:], ndelta_d[:, ch0:ch0 + 256], writes=["ndl"])

                def sin_layer(v, bias_col, dst, dkey):
                    A_, B_ = wa[v], wb[v]
                    P.g("vector", "tensor_scalar", [f"pF{v}", "fvec", "fvec2"], [f"wa{v}"], out=A_[:], in0=pF[v][0:64, :], scalar1=fvec[:, 2:3], scalar2=fvec[:, bias_col:bias_col + 1],
                        op0=ALU.mult, op1=ALU.add)
                    P.g("vector", "tensor_scalar", [f"wa{v}"], [f"wb{v}"], out=B_[:], in0=A_[:], scalar1=-math.pi, scalar2=2 * math.pi, op0=ALU.is_lt, op1=ALU.mult)
                    P.g("vector", "tensor_tensor", [f"wa{v}", f"wb{v}"], [f"wb{v}"], out=B_[:], in0=A_[:], in1=B_[:], op=ALU.add)
                    P.g("vector", "tensor_scalar", [f"wa{v}"], [f"wa{v}"], out=A_[:], in0=A_[:], scalar1=math.pi, scalar2=-2 * math.pi, op0=ALU.is_gt, op1=ALU.mult)
                    P.g("vector", "tensor_tensor", [f"wa{v}", f"wb{v}"], [f"wb{v}"], out=B_[:], in0=A_[:], in1=B_[:], op=ALU.add)
                    P.g("scalar", "activation", [f"wb{v}"], [dkey], out=dst, in_=B_[:], func=AF.Sin)

                def sin_stage(nb):
                    hb = nb % 2
                    for v in range(2):
                        P.dma(zf[v][:], zf_d[v, :, nb * 512:(nb + 1) * 512], writes=[f"zf{v}"])
                        P.g("tensor", "matmul", ["fw", f"zf{v}"], [f"pF{v}"], pF[v][0:64, :], lhsT=fw1[:], rhs=zf[v][:], start=True, stop=True)
                    for v in range(2):
                        sin_layer(v, 4, h1[v][:], f"h1{v}")
                    for v in range(2):
                        P.g("tensor", "matmul", ["fw", f"h1{v}"], [f"pF{v}"], pF[v][0:64, :], lhsT=fw2[:], rhs=h1[v][:], start=True, stop=True)
                    for v in range(2):
                        sin_layer(v, 5, h2[hb][v][:], f"h2{hb}{v}")

                def chunk_a(s):
                    nb, q = divmod(s, 4); hb = nb % 2; b = s % 2
                    P.g("tensor", "matmul", ["fw", f"h2{hb}0"], [f"pH{b}"], pH[b][:, 0:256], lhsT=h2[hb][0][:, q * 128:(q + 1) * 128], rhs=fw3[:, ch0:ch0 + 256], start=True, stop=True, silent=True)
                    P.g("tensor", "matmul", ["fw", f"h2{hb}1"], [f"pH{b}"], pH[b][:, 256:512], lhsT=h2[hb][1][:, q * 128:(q + 1) * 128], rhs=fw3[:, 512 + ch0:512 + ch0 + 256],
                        start=True, stop=True)
                    P.g("scalar", "activation", ["ndl", "tpos"], [f"dec{b}"], out=dec[b][:, 0:256], in_=ndl[:], func=AF.Exp, scale=tpos[:, s:s + 1])
                    P.g("scalar", "activation", ["ndl", "tpos"], [f"dec{b}"], out=dec[b][:, 256:512], in_=ndl[:], func=AF.Exp, scale=tpos[:, 32 + s:33 + s])
                    P.g("vector", "tensor_tensor", [f"pH{b}", "fb3p"], [f"hfd{b}"], out=hfd[b][:], in0=pH[b][:], in1=fb3p[:], op=ALU.add)
                    P.g("vector", "tensor_tensor", [f"hfd{b}", f"dec{b}"], [f"hfd{b}"], out=hfd[b][:], in0=hfd[b][:], in1=dec[b][:], op=ALU.mult)
                    if s == 0:
                        P.g("vector", "memset", [], [f"hfd{b}"], hfd[b][0:1, 256:512], 0.0)
                    P.g("scalar", "activation", [f"hfd{b}"], [f"ab{b}"], out=ab[b][:], in_=hfd[b][:], func=AF.Abs)
                    P.g("vector", "tensor_tensor", [f"hfd{b}"], ["RHke"], out=RH[:, s, 0:256], in0=hfd[b][:, 0:256], in1=hfd[b][:, 256:512], op=ALU.add)
                    P.g("gpsimd", "tensor_tensor", [f"hfd{b}"], ["RHko"], out=RH[:, s, 512:768], in0=hfd[b][:, 0:256], in1=hfd[b][:, 256:512], op=ALU.subtract)

                def chunk_b(s):
                    b = s % 2
                    P.g("tensor", "matmul", [f"ab{b}", "cst"], ["pN"], pN[:], lhsT=ones, rhs=ab[b][:, 0:256], start=(s == 0), stop=False, silent=True)
                    P.g("tensor", "matmul", [f"ab{b}", "cst"], ["pN"], pN[:], lhsT=ones, rhs=ab[b][:, 256:512], start=False, stop=(s == 31))

                sin_stage(0)
                for nb in range(8):
                    if nb + 1 < 8:
                        sin_stage(nb + 1)
                    for q in range(4):
                        s = nb * 4 + q
                        chunk_a(s)
                        if s > 0:
                            chunk_b(s - 1)
                chunk_b(31)
                P.g("vector", "reciprocal", ["pN"], ["rinv"], out=rinv[:], in_=pN[:])
            with P.scope():
                TC = [P.sb("TC0", [128, 32, 128], BF16), P.sb("TC1", [128, 32, 128], BF16)]
                TS = [P.sb("TS0", [128, 32, 128], BF16), P.sb("TS1", [128, 32, 128], BF16)]
                skp = P.sb("skp", [128, 256]); Ap = P.sb("Ap", [128, 256]); Bp = P.sb("Bp", [128, 256])
                t1 = P.sb("t1", [128, 256]); t2 = P.sb("t2", [128, 256])
                pC = [P.ps("pC0", [128, 512]), P.ps("pC1", [128, 512])]; pSn = [P.ps("pSn0", [128, 512]), P.ps("pSn1", [128, 512])]
                P.dma(skp[:], skip_d[:, ch0:ch0 + 256], writes=["skp"])
                for j in range(33):
                    b = j % 2
                    P.dma(TC[b][:], dftc_d[j], writes=[f"TC{b}"]); P.dma(TS[b][:], dfts_d[j], writes=[f"TS{b}"])
                    for s in range(32):
                        P.g("tensor", "matmul", [f"TC{b}", "RHu", "RHke"], [f"pC{b}"], pC[b][:], lhsT=TC[b][:, s, :], rhs=RH[:, s, 0:512], start=(s == 0), stop=(s == 31), silent=(s != 31))
                    for s in range(32):
                        P.g("tensor", "matmul", [f"TS{b}", "RHu", "RHko"], [f"pSn{b}"], pSn[b][:], lhsT=TS[b][:, s, :], rhs=RH[:, s, 256:768], start=(s == 0), stop=(s == 31), silent=(s != 31))
                    P.g("vector", "tensor_tensor", [f"pC{b}", "rinv"], ["Ap"], out=Ap[:], in0=pC[b][:, 0:256], in1=rinv[:], op=ALU.mult)
                    P.g("vector", "tensor_tensor", ["Ap", "skp"], ["Ap"], out=Ap[:], in0=Ap[:], in1=skp[:], op=ALU.add)
                    P.g("vector", "tensor_tensor", [f"pSn{b}", "rinv"], ["Bp"], out=Bp[:], in0=pSn[b][:, 256:512], in1=rinv[:], op=ALU.mult)
                    P.g("vector", "tensor_scalar", ["Bp", "cst"], ["Bp"], out=Bp[:], in0=Bp[:], scalar1=cst[:, 656:657], scalar2=None, op0=ALU.mult)
                    P.g("vector", "tensor_tensor", [f"pC{b}", "Ap"], ["t1"], out=t1[:], in0=pC[b][:, 256:512], in1=Ap[:], op=ALU.mult)
                    P.g("vector", "tensor_tensor", [f"pSn{b}", "Bp"], ["t2"], out=t2[:], in0=pSn[b][:, 0:256], in1=Bp[:], op=ALU.mult)
                    P.g("vector", "tensor_tensor", ["t1", "t2"], ["YF"], out=YFr[:, j, :], in0=t1[:], in1=t2[:], op=ALU.subtract)
                    P.g("vector", "tensor_tensor", [f"pC{b}", "Bp"], ["t1"], out=t1[:], in0=pC[b][:, 256:512], in1=Bp[:], op=ALU.mult)
                    P.g("vector", "tensor_tensor", [f"pSn{b}", "Ap"], ["t2"], out=t2[:], in0=pSn[b][:, 0:256], in1=Ap[:], op=ALU.mult)
                    P.g("vector", "tensor_tensor", ["t1", "t2"], ["YF"], out=YFi[:, j, :], in0=t1[:], in1=t2[:], op=ALU.add)
            with P.scope():
                IC = [P.sb("IC0", [128, 2048], BF16), P.sb("IC1", [128, 2048], BF16)]
                IS = [P.sb("IS0", [128, 2048], BF16), P.sb("IS1", [128, 2048], BF16)]
                x0c = P.sb("x0c", [128, 2048]); yst = [P.sb("yst0", [128, 512]), P.sb("yst1", [128, 512])]
                pO = [[P.ps(f"pO{cc}{tb}", [128, 512]) for tb in range(4)] for cc in range(2)]
                for kc in range(33):
                    b = kc % 2
                    P.dma(IC[b][:], idc_d[kc], writes=[f"IC{b}"]); P.dma(IS[b][:], ids_d[kc], writes=[f"IS{b}"])
                    for cc in range(2):
                        for tb in range(4):
                            P.g("tensor", "matmul", ["YF", f"IC{b}"], [f"pO{cc}{tb}"], pO[cc][tb][:], lhsT=YFr[:, kc, cc * 128:(cc + 1) * 128],
                                rhs=IC[b][:, tb * 512:(tb + 1) * 512], start=(kc == 0), stop=False, silent=True)
                            P.g("tensor", "matmul", ["YF", f"IS{b}"], [f"pO{cc}{tb}"], pO[cc][tb][:], lhsT=YFi[:, kc, cc * 128:(cc + 1) * 128],
                                rhs=IS[b][:, tb * 512:(tb + 1) * 512], start=False, stop=(kc == 32), silent=not (cc == 1 and tb == 3))
                i = 0
                for cc in range(2):
                    cg = ps_ * 2 + cc
                    P.dma(x0c[:], x0c_d[cg], reads=["x0c_d"], writes=["x0c"])
                    for tb in range(4):
                        b = i % 2; i += 1
                        P.g("vector", "tensor_tensor", [f"pO{cc}{tb}", "x0c"], [f"yst{b}"], out=yst[b][:], in0=pO[cc][tb][:], in1=x0c[:, tb * 512:(tb + 1) * 512], op=ALU.mult)
                        P.dma(yh_d[cg, :, tb * 512:(tb + 1) * 512], yst[b][:], reads=[f"yst{b}"], writes=["yh_d"])
    if dbg and stop_after == 3:
        P.dma(dbg_d[:, 0:2048], yh_d[0], reads=[]); P.dma(dbg_d[:, 2048:4096], yh_d[3], reads=[])
        P.finish(P.all_tokens()); P.emit(); return nc

    outer.__exit__(None, None, None)
    with P.scope():
        woa = P.sb("woa", [128, 4, 1024], F32R); woh = P.sb("woh", [128, 4, 1024], F32R); wout = P.sb("wout", [128, 8, 1024], F32R)
        Wf = [P.sb(f"Wf{i}", [128, 8, 128]) for i in range(2)]
        W = [P.sb(f"W{i}", [128, 8, 128], F32R) for i in range(2)]
        k_ = 0
        for (wt_, wd_, wk_, nk_) in ((woa, woa_d, "woa", 4), (woh, woh_d, "woh", 4), (wout, wout_d, "wout", 8)):
            for c in range(nk_):
                i = k_ % 2; k_ += 1
                P.dma(Wf[i][:].rearrange("p a b -> p (a b)"), wd_[:, c, :], writes=[f"Wf{i}"])
                P.g("gpsimd", "tensor_copy", [f"Wf{i}"], [wk_], out=wt_[:, c, :], in_=Wf[i][:].rearrange("p a b -> p (a b)"))
        xt = [P.sb("xt0", [128, 1024]), P.sb("xt1", [128, 1024])]; ht = [P.sb("ht0", [128, 1024]), P.sb("ht1", [128, 1024])]
        hT = P.sb("hT", [128, 8, 512], F32R)
        YA = P.sb("YA", [128, 4, 512]); YH = P.sb("YH", [128, 4, 512]); MT = P.sb("MT", [128, 8, 512], F32R)
        YAr = P.sb("YAr", [128, 4, 512], F32R); YHr = P.sb("YHr", [128, 4, 512], F32R)
        gs = [P.sb("gs0", [128, 512]), P.sb("gs1", [128, 512])]; m1 = P.sb("m1", [128, 512]); m2 = P.sb("m2", [128, 512])
        xm0_ = P.sb("xm0", [128, 1024]); xm = [xm0_, xm0_]
        pT = P.ps("pT", [128, 1024]); pG = [P.ps("pG0", [128, 512]), P.ps("pG1", [128, 512])]
        pM = [P.ps("pM0", [128, 512]), P.ps("pM1", [128, 512])]; pX = [P.ps("pX0", [128, 512]), P.ps("pX1", [128, 512])]
        wc = 0; xc = 0
        for B in range(4):
            t0 = own0 + B * 512
            xts = []
            for t in range(4):
                b = xc % 2; xc += 1
                P.dma(xt[b][:], x_d[t0 + t * 128: t0 + (t + 1) * 128, :], writes=[f"xt{b}"])
                norm_tile(xt[b], 128, G1, SH1, ht[b], f"xt{b}", f"ht{b}", "m4")
                for c in range(8):
                    P.g("tensor", "transpose", [f"ht{b}", "cst"], ["pT"], out=pT[:, c * 128:(c + 1) * 128], in_=ht[b][:, c * 128:(c + 1) * 128], identity=ident, silent=(c != 7))
                P.g("scalar", "activation", ["pT"], ["hT"], out=hT[:, :, t * 128:(t + 1) * 128], in_=pT[:].rearrange("p (c n) -> p c n", c=8), func=AF.Copy)
            P.dma(YA[:], ya_d[:, :, B * 512:(B + 1) * 512].rearrange("c p n -> p c n"), reads=["ya_d"], writes=["YA"])
            P.dma(YH[:], yh_d[:, :, B * 512:(B + 1) * 512].rearrange("c p n -> p c n"), reads=["yh_d"], writes=["YH"])
            P.g("gpsimd", "tensor_copy", ["YA"], ["YAr"], out=YAr[:], in_=YA[:])
            P.g("gpsimd", "tensor_copy", ["YH"], ["YHr"], out=YHr[:], in_=YH[:])
            for oc in range(8):
                for gi in range(2):
                    i = wc % 2; j_ = wc % 2; wc += 1
                    P.dma(Wf[j_][:], win_d[CG + gi * 8 + oc], writes=[f"Wf{j_}"])
                    if wc % 3 == 0:
                        P.g("gpsimd", "tensor_copy", [f"Wf{j_}"], [f"W{i}"], out=W[i][:], in_=Wf[j_][:])
                    else:
                        P.g("scalar", "activation", [f"Wf{j_}"], [f"W{i}"], out=W[i][:], in_=Wf[j_][:], func=AF.Copy)
                    for c in range(8):
                        P.g("tensor", "matmul", [f"W{i}", "hT"], [f"pG{gi}"], pG[gi][:], lhsT=W[i][:, c, :], rhs=hT[:, c, :], start=(c == 0), stop=(c == 7), silent=(c != 7))
                    P.g("scalar", "activation", [f"pG{gi}"], [f"gs{gi}"], out=gs[gi][:], in_=pG[gi][:], func=AF.Sigmoid)
                for c in range(4):
                    P.g("tensor", "matmul", ["woa", "YAr"], ["pM0"], pM[0][:], lhsT=woa[:, c, oc * 128:(oc + 1) * 128], rhs=YAr[:, c, :], start=(c == 0), stop=(c == 3), silent=(c != 3))
                for c in range(4):
                    P.g("tensor", "matmul", ["woh", "YHr"], ["pM1"], pM[1][:], lhsT=woh[:, c, oc * 128:(oc + 1) * 128], rhs=YHr[:, c, :], start=(c == 0), stop=(c == 3), silent=(c != 3))
                P.g("vector", "tensor_tensor", ["pM0", "gs0"], ["m1"], out=m1[:], in0=pM[0][:], in1=gs[0][:], op=ALU.mult)
                P.g("vector", "tensor_tensor", ["pM1", "gs1"], ["m2"], out=m2[:], in0=pM[1][:], in1=gs[1][:], op=ALU.mult)
                P.g("vector", "tensor_tensor", ["m1", "m2"], ["MT"], out=MT[:, oc, :], in0=m1[:], in1=m2[:], op=ALU.add)
            for t in range(4):
                b = xc % 2; xc += 1
                P.dma(xt[b][:], x_d[t0 + t * 128: t0 + (t + 1) * 128, :], writes=[f"xt{b}"])
                for hf in range(2):
                    for c in range(8):
                        P.g("tensor", "matmul", ["MT", "wout"], [f"pX{hf}"], pX[hf][:], lhsT=MT[:, c, t * 128:(t + 1) * 128], rhs=wout[:, c, hf * 512:(hf + 1) * 512],
                            start=(c == 0), stop=(c == 7), silent=(c != 7))
                    P.g("vector", "tensor_tensor", [f"pX{hf}"], ["xm0"], out=xm[b][:, hf * 512:(hf + 1) * 512], in0=pX[hf][:], in1=g1[:, hf * 512:(hf + 1) * 512], op=ALU.mult)
                P.g("vector", "tensor_tensor", ["xm0", f"xt{b}"], ["xm0"], out=xm[b][:], in0=xm[b][:], in1=xt[b][:], op=ALU.add)
                r = B * 512 + t * 128
                P.dma(xm_d[r:r + 128, :], xm[b][:], reads=["xm0"], writes=["xm_d"])
    if dbg and stop_after == 4:
        P.dma(dbg_d[:, 0:1024], xm_d[0:128, :], reads=[]); P.dma(dbg_d[:, 1024:2048], xm_d[1920:2048, :], reads=[])
        P.finish(P.all_tokens()); P.emit(); return nc

    out_tokens = []
    with P.scope():
        wq = P.sb("wq", [128, 8, 1024]); kbd = P.sb("kbd", [128, 8, 256])
        P.dma(wq[:], wq_d, writes=["wq"]); P.dma(kbd[:], kbd_d, writes=["kbd"])
        NBU = 12
        UV = [P.sb(f"UV{i}", [128, 2048], BF16) for i in range(NBU)]
        Dg = [P.sb(f"Dg{i}", [128, 128], BF16) for i in range(4)]
        xmt = [P.sb(f"xmt{i}", [128, 1024]) for i in range(3)]; h2 = [P.sb(f"h2_{i}", [128, 1024]) for i in range(3)]
        h2T = P.sb("h2T", [128, 8, 128]); QTs = P.sb("QTs", [128, 8, 128]); SCs = [P.sb("SCa", [128, 16, 128]), P.sb("SCb", [128, 16, 128])]; SC2 = P.sb("SC2", [128, 16, 128])
        V16 = P.sb("V16", [128, 16, 16]); I16 = P.sb("I16", [128, 16, 16], U32); I16f = P.sb("I16f", [128, 16, 16])
        cand = P.sb("cand", [128, 8, 256]); B16 = P.sb("B16", [128, 8, 16])
        cand2 = SC2[:].rearrange("p g k -> p (g k)").rearrange("p (h k) -> p h k", h=8)
        PI = P.sb("PI", [128, 8, 16], U32); PA = P.sb("PA", [128, 8, 16], U32); PB = P.sb("PB", [128, 8, 16], U32)
        paf = P.sb("paf", [128, 8, 16]); pbf = P.sb("pbf", [128, 8, 16]); OH = SC2[:].rearrange("p g k -> p (g k)").rearrange("p (h k) -> p h k", h=8)
        isel = P.sb("isel", [128, 128]); jsel = P.sb("jsel", [128, 128]); eif = P.sb("eif", [128, 128])
        EI = [P.sb("EI0", [128, 128], U32), P.sb("EI1", [128, 128], U32)]
        EG = P.sb("EG", [128, 8, 16]); EGs = [P.sb("EGs0", [128, 8, 16]), P.sb("EGs1", [128, 8, 16])]; gsm = P.sb("gsm", [128, 16]); GT = [P.sb("GT0", [128, 128]), P.sb("GT1", [128, 128])]
        Adot = [P.sb("Ad0", [128, 128]), P.sb("Ad1", [128, 128])]; junk = P.sb("junk", [128, 1024], BF16)
        gw = [P.sb("gwa", [128, 8]), P.sb("gwb", [128, 8])]; gw2 = [P.sb("gw2a", [128, 8]), P.sb("gw2b", [128, 8])]
        GA = [P.sb("GA0", [128, 128]), P.sb("GA1", [128, 128])]; acc0_ = P.sb("acc0", [128, 1024]); acc = [acc0_, acc0_]
        pT = P.ps("pT", [128, 1024]); pQ = pT
        pSc = P.ps("pSc", [128, 1024]); pXs = [P.ps("pXa", [128, 1024]), P.ps("pXb", [128, 1024])]
        cnt = {"u": 0, "v": 0, "d": 0}
        V16v = V16[:].rearrange("p (h t) k -> p h t k", t=2)
        I16v = I16f[:].rearrange("p (h t) k -> p h t k", t=2)
        NT = 1 if (dbg and stop_after == 5) else 16

        def front_pieces(n):
            b3 = n % 3; sb_ = n % 2; SC = SCs[sb_]

            def pa():
                P.dma(xmt[b3][:], xm_d[n * 128:(n + 1) * 128, :], reads=["xm_d"], writes=[f"xmt{b3}"])
                norm_tile(xmt[b3], 128, G2, SH2, h2[b3], f"xmt{b3}", f"h2{b3}", "p")

            def pb():
                for c in range(8):
                    P.g("tensor", "transpose", [f"h2{b3}", "cst"], ["pT"], out=pT[:, c * 128:(c + 1) * 128], in_=h2[b3][:, c * 128:(c + 1) * 128], identity=ident, silent=(c != 7))

            def pc():
                P.g("scalar", "activation", ["pT"], ["h2T"], out=h2T[:], in_=pT[:].rearrange("p (c n) -> p c n", c=8), func=AF.Copy)

            def pd(h0):
                def f():
                    for hd in range(h0, h0 + 4):
                        for c in range(8):
                            P.g("tensor", "matmul", ["wq", "h2T"], ["pT"], pQ[:, hd * 128:(hd + 1) * 128], lhsT=wq[:, c, hd * 128:(hd + 1) * 128], rhs=h2T[:, c, :],
                                start=(c == 0), stop=(c == 7), silent=(c != 7))
                return f

            def pe():
                P.g("scalar", "activation", ["pT"], ["QTs"], out=QTs[:], in_=pQ[:].rearrange("p (h n) -> p h n", h=8), func=AF.Copy)

            def pf(q):
                def f():
                    for h4 in range(4):
                        hd = q * 4 + h4
                        P.g("tensor", "matmul", ["QTs", "kbd"], ["pSc"], pSc[:, h4 * 256:(h4 + 1) * 256], lhsT=QTs[:, hd, :], rhs=kbd[:, hd, :], start=True, stop=True, silent=(h4 != 3))
                return f

            def pg(q):
                def f():
                    P.g("scalar", "activation", ["pSc"], [f"SC{sb_}_{q}"], out=SC[:, q * 8:(q + 1) * 8, :], in_=pSc[:].rearrange("p (g k) -> p g k", g=8), func=AF.Copy)
                return f
            return [pa, pb, pc, pd(0), pd(4), pe, pf(0), pg(0), pf(1), pg(1)]

        def front(n):
            for f in front_pieces(n):
                f()

        def routing_gen(n):
            b = n % 2; sb_ = n % 2; SC = SCs[sb_]
            for gI in range(16):
                P.g("vector", "max", [f"SC{sb_}_{gI // 8}"], [f"V16a{gI}"], out=V16[:, gI, 0:8], in_=SC[:, gI, :])
            yield
            for gI in range(16):
                P.g("vector", "match_replace", [f"SC{sb_}_{gI // 8}", f"V16a{gI}"], [f"SC2{gI}"], out=SC2[:, gI, :], in_to_replace=V16[:, gI, 0:8], in_values=SC[:, gI, :], imm_value=-1e30)
            yield
            for gI in range(16):
                P.g("vector", "max", [f"SC2{gI}"], [f"V16b{gI}"], out=V16[:, gI, 8:16], in_=SC2[:, gI, :])
            yield
            for gI in range(16):
                P.g("vector", "max_index", [f"SC{sb_}_{gI // 8}", f"V16a{gI}"], [f"I16a{gI}"], out=I16[:, gI, 0:8], in_max=V16[:, gI, 0:8], in_values=SC[:, gI, :])
            yield
            for gI in range(16):
                P.g("vector", "max_index", [f"SC{sb_}_{gI // 8}", f"V16b{gI}"], [f"I16b{gI}"], out=I16[:, gI, 8:16], in_max=V16[:, gI, 8:16], in_values=SC[:, gI, :])
            yield
            allV = [f"V16a{g_}" for g_ in range(16)] + [f"V16b{g_}" for g_ in range(16)]
            allI = [f"I16a{g_}" for g_ in range(16)] + [f"I16b{g_}" for g_ in range(16)]
            P.g("vector", "tensor_copy", allI, ["I16f"], out=I16f[:], in_=I16[:])
            P.g("vector", "tensor_tensor", allV + [f"cand2{h_}" for h_ in range(8)], ["cand"], out=cand[:].rearrange("p h (a c) -> p h a c", a=16),
                in0=V16v[:, :, 0, :].unsqueeze(3).to_broadcast([128, 8, 16, 16]), in1=V16v[:, :, 1, :].unsqueeze(2).to_broadcast([128, 8, 16, 16]), op=ALU.add)
            yield
            for hd in range(8):
                P.g("vector", "max", ["cand"], [f"B16a{hd}"], out=B16[:, hd, 0:8], in_=cand[:, hd, :])
            for hd in range(8):
                P.g("vector", "match_replace", ["cand", f"B16a{hd}"], [f"cand2{hd}"], out=cand2[:, hd, :], in_to_replace=B16[:, hd, 0:8], in_values=cand[:, hd, :], imm_value=-1e30)
            yield
            for hd in range(8):
                P.g("vector", "max", [f"cand2{hd}"], [f"B16b{hd}"], out=B16[:, hd, 8:16], in_=cand2[:, hd, :])
            for hd in range(8):
                P.g("vector", "max_index", ["cand", f"B16a{hd}"], [f"PIa{hd}"], out=PI[:, hd, 0:8], in_max=B16[:, hd, 0:8], in_values=cand[:, hd, :])
            yield
            for hd in range(8):
                P.g("vector", "max_index", ["cand", f"B16b{hd}"], [f"PIb{hd}"], out=PI[:, hd, 8:16], in_max=B16[:, hd, 8:16], in_values=cand[:, hd, :])
            allB = [f"B16a{h_}" for h_ in range(8)] + [f"B16b{h_}" for h_ in range(8)]
            allP = [f"PIa{h_}" for h_ in range(8)] + [f"PIb{h_}" for h_ in range(8)]
            P.g("vector", "tensor_single_scalar", allP, ["PA"], out=PA[:], in_=PI[:], scalar=4, op=ALU.logical_shift_right)
            P.g("vector", "tensor_single_scalar", allP, ["PB"], out=PB[:], in_=PI[:], scalar=15, op=ALU.bitwise_and)
            P.g("vector", "tensor_copy", ["PA"], ["paf"], out=paf[:], in_=PA[:])
            P.g("vector", "tensor_copy", ["PB"], ["pbf"], out=pbf[:], in_=PB[:])
            yield
            for (pf_, pk, tt, dst, dk) in ((paf, "paf", 0, isel, "isel"), (pbf, "pbf", 1, jsel, "jsel")):
                OHv = OH.rearrange("p h (k a) -> p h k a", k=16)
                P.g("vector", "tensor_tensor", [pk, "cst"] + [f"cand2{h_}" for h_ in range(8)], ["OH"], out=OHv, in0=iota16.unsqueeze(1).unsqueeze(1).to_broadcast([128, 8, 16, 16]),
                    in1=pf_[:].unsqueeze(3).to_broadcast([128, 8, 16, 16]), op=ALU.is_equal)
                yield
                P.g("vector", "tensor_tensor", ["OH", "I16f"], ["OH"], out=OHv, in0=OHv, in1=I16v[:, :, tt, :].unsqueeze(2).to_broadcast([128, 8, 16, 16]), op=ALU.mult)
                P.g("vector", "tensor_reduce", ["OH"], [dk], out=dst[:], in_=OH.rearrange("p h (k a) -> p (h k) a", k=16), axis=AX.X, op=ALU.add)
                yield
            P.g("vector", "scalar_tensor_tensor", ["isel", "jsel"], ["eif"], out=eif[:], in0=isel[:], scalar=128.0, in1=jsel[:], op0=ALU.mult, op1=ALU.add)
            P.g("vector", "tensor_copy", ["eif"], [f"EI{b}"], out=EI[b][:], in_=eif[:])
            P.g("vector", "tensor_tensor", allB, [f"EGs{b}"], out=EGs[b][:], in0=B16[:], in1=B16[:, :, 0:1].to_broadcast([128, 8, 16]), op=ALU.subtract)

        def routing(n):
            for _ in routing_gen(n):
                pass

        def routing_b(n):
            b = n % 2
            P.g("scalar", "activation", [f"EGs{b}"], ["EG"], out=EG[:], in_=EGs[b][:], func=AF.Exp)
            P.g("vector", "tensor_reduce", ["EG"], ["gsm"], out=gsm[:, 0:8], in_=EG[:], axis=AX.X, op=ALU.add)
            P.g("vector", "reciprocal", ["gsm"], ["gsm2"], out=gsm[:, 8:16], in_=gsm[:, 0:8])
            P.g("vector", "tensor_tensor", ["EG", "gsm2"], [f"GT{b}"], out=GT[b][:].rearrange("p (h k) -> p h k", h=8), in0=EG[:],
                in1=gsm[:, 8:16].unsqueeze(2).to_broadcast([128, 8, 16]), op=ALU.mult)

        def epilogue(n):
            b = n % 2; b3 = n % 3; pX = pXs[n % 2]
            P.g("vector", "tensor_tensor", [f"pX{n % 2}"], ["acc0"], out=acc[b][:], in0=pX[:], in1=g2, op=ALU.mult)
            P.g("vector", "tensor_tensor", ["acc0", f"xmt{b3}"], ["acc0"], out=acc[b][:], in0=acc[b][:], in1=xmt[b3][:], op=ALU.add)
            norm_tile(acc[b], 128, FG, None, xmt[b3], "acc0", f"xmt{b3}", "f")
            out_tokens.append(P.dma(out_d[n * 128:(n + 1) * 128, :], xmt[b3][:], reads=[f"xmt{b3}"], writes=["out_d"]))

        def tile_body(n):
            b = n % 2; b3 = n % 3; SC = SCs[n % 2]; pX = pXs[n % 2]
            fp = front_pieces(n + 2) if n + 2 < NT else None
            rg = routing_gen(n + 1) if n + 1 < NT else None
            for g in range(16):
                gb = g % 2
                ks = []
                for j in range(8):
                    s = g * 8 + j
                    k = cnt["u"] % NBU; cnt["u"] += 1; ks.append(k)
                    P.gather(UV[k][:], uv16_d, EI[b][:, s:s + 1], reads=[f"EI{b}", "p16"], writes=[f"UV{k}"])
                    P.g("vector", "scalar_tensor_tensor", [f"UV{k}", f"h2{b3}"], ["junk", f"Ad{b}_{s}"], out=junk[:], in0=UV[k][:, 0:1024], scalar=1.0, in1=h2[b3][:],
                        op0=ALU.mult, op1=ALU.mult, accum_out=Adot[b][:, s:s + 1])
                akeys = [f"Ad{b}_{g * 8 + j}" for j in range(8)]
                av = Adot[b][:, g * 8:(g + 1) * 8]
                P.g("vector", "tensor_tensor", akeys, [f"gw{gb}"], out=gw[gb][:], in0=av, in1=av, op=ALU.mult)
                P.g("vector", "tensor_scalar", [f"gw{gb}"], [f"gw{gb}"], out=gw[gb][:], in0=gw[gb][:], scalar1=0.044715, scalar2=1.0, op0=ALU.mult, op1=ALU.add)
                P.g("vector", "tensor_tensor", [f"gw{gb}"] + akeys, [f"gw{gb}"], out=gw[gb][:], in0=gw[gb][:], in1=av, op=ALU.mult)
                P.g("scalar", "activation", [f"gw{gb}"], [f"gw2{gb}"], out=gw2[gb][:], in_=gw[gb][:], func=AF.Sigmoid, scale=2.0 * math.sqrt(2.0 / math.pi))
                P.g("vector", "tensor_tensor", [f"gw2{gb}"] + akeys, [f"gw2{gb}"], out=gw2[gb][:], in0=gw2[gb][:], in1=av, op=ALU.mult)
                P.g("vector", "tensor_tensor", [f"gw2{gb}", f"GT{b}"], [f"GA{b}_{g}"], out=GA[b][:, g * 8:(g + 1) * 8], in0=gw2[gb][:], in1=GT[b][:, g * 8:(g + 1) * 8], op=ALU.mult)
                for j in range(8):
                    s = g * 8 + j; k = ks[j]
                    kd = cnt["d"] % 4; cnt["d"] += 1
                    P.g("scalar", "activation", ["cst", f"GA{b}_{g}"], [f"Dg{kd}"], out=Dg[kd][:], in_=ident, func=AF.Copy, scale=GA[b][:, s:s + 1])
                    for hf in range(2):
                        P.g("tensor", "matmul", [f"Dg{kd}", f"UV{k}"], [f"pX{n % 2}"], pX[:, hf * 512:(hf + 1) * 512], lhsT=Dg[kd][:], rhs=UV[k][:, 1024 + hf * 512:1024 + (hf + 1) * 512],
                            start=(s == 0), stop=(s == 127), silent=(hf == 0))
                if g == 1 and n > 0:
                    epilogue(n - 1)
                if fp is not None and 2 <= g < len(fp) + 2:
                    fp[g - 2]()
                if rg is not None:
                    if next(rg, "done") == "done":
                        rg = None
            if dbg and stop_after == 5:
                P.barrier()
                P.g("vector", "tensor_copy", [], ["acc0"], out=acc[0][:], in_=pX[:])
                P.barrier()
                P.dma(dbg_d[:, 0:2048], SC[:].rearrange("p g k -> p (g k)"), reads=[])
                P.dma(dbg_d[:, 2048:2304], V16[:].rearrange("p g k -> p (g k)"), reads=[])
                P.dma(dbg_d[:, 2304:2560], I16f[:].rearrange("p g k -> p (g k)"), reads=[])
                P.dma(dbg_d[:, 2560:2688], B16[:].rearrange("p g k -> p (g k)"), reads=[])
                P.dma(dbg_d[:, 2688:2816], eif[:], reads=[])
                P.dma(dbg_d[:, 2816:2944], GT[0][:], reads=[])
                P.dma(dbg_d[:, 2944:3072], Adot[0][:], reads=[])
                P.dma(dbg_d[:, 3072:4096], acc[0][:], reads=[])
                P.barrier()
            if rg is not None:
                for _ in rg:
                    pass
            if n + 1 < NT:
                routing_b(n + 1)
            if n == NT - 1:
                epilogue(n)

        front(0); routing(0); routing_b(0)
        if NT > 1:
            front(1)
        for n in range(NT):
            tile_body(n)
    P.finish(P.all_tokens())
    P.emit()
    return nc


_CONST_CACHE = {}


def _constants():
    if _CONST_CACHE:
        return _CONST_CACHE
    N = 2 * L
    base = np.cos(2 * np.pi * np.arange(N) / N)
    bases = np.sin(2 * np.pi * np.arange(N) / N)
    s = np.arange(L, dtype=np.int64)[:, None]
    k = np.arange(33 * 128, dtype=np.int64)[None, :]
    idx = (s * k) % N
    valid = (k <= L)
    C = np.where(valid, base[idx], 0.0).astype(np.float32)
    S = np.where(valid, bases[idx], 0.0).astype(np.float32)
    def fwd(T):
        return np.ascontiguousarray(T.reshape(32, 128, 33, 128).transpose(2, 1, 0, 3)).astype(ml_dtypes.bfloat16)
    _CONST_CACHE["dftc"] = fwd(C); _CONST_CACHE["dfts"] = fwd(S)
    kk = np.arange(33 * 128, dtype=np.int64)[:, None]
    t = np.arange(L, dtype=np.int64)[None, :]
    idx2 = (kk * t) % N
    wk = np.where((kk == 0) | (kk == L), 1.0, 2.0) / N
    wk = np.where(kk <= L, wk, 0.0)
    IC = (base[idx2] * wk).astype(np.float32).reshape(33, 128, L)
    IS = (bases[idx2] * wk).astype(np.float32).reshape(33, 128, L)
    _CONST_CACHE["idc"] = [np.ascontiguousarray(IC[:, :, h * 2048:(h + 1) * 2048]).astype(ml_dtypes.bfloat16) for h in range(2)]
    _CONST_CACHE["ids"] = [np.ascontiguousarray(IS[:, :, h * 2048:(h + 1) * 2048]).astype(ml_dtypes.bfloat16) for h in range(2)]
    f = 16
    inv = (10000.0 ** (-np.arange(f, dtype=np.float32) / f)).astype(np.float32)
    tok = np.arange(L)
    row = (tok // 64).astype(np.float32); col = (tok % 64).astype(np.float32)
    cosT = np.zeros((64, L), np.float32); sinT = np.zeros((64, L), np.float32)
    for d in range(64):
        pos = row if d < 32 else col
        j = d % 32
        ang = pos * inv[j % 16]
        cosT[d] = np.cos(ang)
        sinT[d] = -np.sin(ang) if j < 16 else np.sin(ang)
    _CONST_CACHE["rope"] = np.ascontiguousarray(np.stack([np.tile(cosT, (2, 1)), np.tile(sinT, (2, 1))])).astype(np.float32)
    def feats(tt):
        bands = np.arange(1, 17, dtype=np.float32)
        ang = (2.0 * np.pi * tt[:, None] * bands[None, :]).astype(np.float32)
        return np.concatenate([tt[:, None], np.cos(ang), np.sin(ang)], axis=-1).astype(np.float32)
    tt = (np.arange(L, dtype=np.float32) / L).astype(np.float32)
    tts = (np.maximum(np.arange(L) - 1, 0).astype(np.float32) / L).astype(np.float32)
    _CONST_CACHE["zf"] = np.ascontiguousarray(np.stack([feats(tt).T, feats(tts).T])).astype(np.float32)
    deltas = np.linspace(math.log(1e-2) / 1.5, math.log(1e-2) / 0.3, 512, dtype=np.float32)
    _CONST_CACHE["ndelta"] = np.ascontiguousarray(np.tile(-np.abs(deltas)[None, :], (128, 1))).astype(np.float32)
    tp = np.zeros((128, 64), np.float32)
    tp[:, 0:32] = tt.reshape(32, 128).T
    tp[:, 32:64] = tts.reshape(32, 128).T
    _CONST_CACHE["tpos"] = tp
    cst = np.zeros((128, 1024), np.float32)
    cst[:, 0:128] = np.eye(128, dtype=np.float32)
    i = np.arange(128)[:, None]; j = np.arange(384)[None, :]
    cst[:, 128:512] = np.where((j >= i) & (j <= i + 256), 0.0, NEG)
    cst[:, 512:528] = np.arange(16, dtype=np.float32)[None, :]
    cst[:, 528:656] = 1.0
    _CONST_CACHE["consts"] = cst
    _CONST_CACHE["constsb"] = np.eye(128, dtype=np.float32).astype(ml_dtypes.bfloat16)
    return _CONST_CACHE


def _chunk_rows(w, nk):
    return np.ascontiguousarray(w.reshape(nk, 128, -1).transpose(1, 0, 2))


def prepare_inputs(inp):
    cs = _constants()
    f32 = lambda a: np.ascontiguousarray(np.asarray(a, dtype=np.float32))
    w_in = f32(inp["w_in"])[0]
    swap64 = np.concatenate([np.arange(16, 32), np.arange(0, 16), np.arange(48, 64), np.arange(32, 48)])
    cols = []
    for c in range(4):
        cols.append(np.concatenate([c * 64 + np.arange(64), (4 + c) * 64 + np.arange(64)]))
    for c in range(4):
        cols.append(np.concatenate([c * 64 + swap64, (4 + c) * 64 + swap64]))
    cols.append(512 + np.arange(128))
    cols.append(512 + np.concatenate([swap64, 64 + swap64]))
    cols.append(640 + np.arange(128))
    for c in range(12):
        cols.append(768 + c * 128 + np.arange(128))
    for c in range(16):
        cols.append(2304 + c * 128 + np.arange(128))
    wch = np.stack([_chunk_rows(w_in[:, cc], 8) for cc in cols])
    w_mod = f32(inp["w_mod"])[0]
    wm = np.ascontiguousarray(w_mod.reshape(8, 128, 12, 512).transpose(2, 1, 0, 3))
    bc = lambda v: np.ascontiguousarray(np.tile(f32(v).reshape(1, -1), (128, 1)))
    ng = np.ascontiguousarray(np.stack([bc(inp["norm1_g"][0]), bc(inp["norm2_g"][0]), bc(inp["final_g"])], axis=1))
    convw = np.zeros((128, 12, 4), np.float32)
    cw = f32(inp["hy_conv_w"])[0]; cbias = f32(inp["hy_conv_b"])[0]
    for j in range(3):
        convw[:, :, j] = cw[j].reshape(12, 128).T
    convw[:, :, 3] = cbias.reshape(12, 128).T
    fvec = np.stack([f32(inp["hy_fb1"])[0], f32(inp["hy_fb2"])[0], f32(inp["hy_freq"])[0], np.zeros(64, np.float32)], axis=1)
    keys = f32(inp["peer_keys"])[0]
    kbd = np.zeros((128, 8, 256), np.float32)
    for h in range(8):
        for p in range(2):
            kbd[p * 64:(p + 1) * 64, h, p * 128:(p + 1) * 128] = keys[h, p].T
    shared = {
        "w_mod": wm, "b_mod": bc(inp["b_mod"][0]), "ng": ng, "w_in": wch, "rope": cs["rope"], "sink": bc(inp["attn_sink"][0]),
        "convw": convw, "fw1": f32(inp["hy_fw1"])[0], "fvec": np.ascontiguousarray(fvec), "fw2": f32(inp["hy_fw2"])[0],
        "fw3": f32(inp["hy_fw3"])[0], "fb3": bc(inp["hy_fb3"][0]), "skip": bc(inp["hy_skip"][0]), "zf": cs["zf"],
        "ndelta": cs["ndelta"], "tpos": cs["tpos"], "w_oa": _chunk_rows(f32(inp["w_o_attn"])[0], 4),
        "w_oh": _chunk_rows(f32(inp["w_o_hy"])[0], 4), "w_out": _chunk_rows(f32(inp["w_out"])[0], 8),
        "wq": _chunk_rows(f32(inp["peer_wq"])[0], 8), "kbd": kbd, "peer_u": f32(inp["peer_u"])[0], "peer_v": f32(inp["peer_v"])[0],
        "dftc": cs["dftc"], "dfts": cs["dfts"], "consts": cs["consts"], "constsb": cs["constsb"],
    }
    x = f32(inp["x"]); ctx = f32(inp["ctx"]); c = f32(inp["c"]); c_ctx = f32(inp["c_ctx"])
    maps = []
    for core in range(8):
        b, half = core // 2, core % 2
        cvec = np.concatenate([c[b].reshape(8, 128).T, c_ctx.reshape(8, 128).T], axis=1)
        m = dict(shared)
        cst = cs["consts"].copy()
        cst[:, 656] = 1.0 if half == 0 else -1.0
        xb = x[b] if half == 0 else np.ascontiguousarray(x[b][::-1])
        m.update({"x": xb, "ctx": ctx[b], "cvec": np.ascontiguousarray(cvec), "idftc": cs["idc"][0], "idfts": cs["ids"][0], "consts": cst})
        if half == 1:
            m["rope"] = np.ascontiguousarray(cs["rope"][:, :, ::-1])
            m["convw"] = np.ascontiguousarray(convw[:, :, [2, 1, 0, 3]])
        maps.append(m)
    return maps


_PROG_CACHE = {}


def kernel(**inputs):
    maps = prepare_inputs(inputs)
    nc = build_program(0)
    res = run_bass_kernel_spmd(nc, maps, core_ids=list(range(8)))
    out = np.zeros((4, L, D), np.float32)
    for core in range(8):
        b, half = core // 2, core % 2
        o = res.results[core]["out"]
        if half == 0:
            out[b, 0:2048] = o
        else:
            out[b, 2048:4096] = o[::-1]
    return out
```

```python
from contextlib import ExitStack, contextmanager
import math
import numpy as np
import ml_dtypes
import concourse.bass as bass
import concourse.mybir as mybir
from concourse.bass_utils import run_bass_kernel_spmd

F32 = mybir.dt.float32
BF16 = mybir.dt.bfloat16
U32 = mybir.dt.uint32
F32R = mybir.dt.float32r
AF = mybir.ActivationFunctionType
ALU = mybir.AluOpType
AX = mybir.AxisListType
ENGS = ["sync", "scalar", "vector", "gpsimd", "tensor"]

L = 4096
D = 1024
NEG = -30000.0


class Prog:
    def __init__(self, nc, n_dma_sync=24, n_dma_pool=32):
        self.nc = nc
        self.es = ExitStack()
        self.stacks = [self.es]
        self.ops = {e: [] for e in ENGS}
        self.cnt = {e: 0 for e in ENGS}
        self.sem = {e: self.es.enter_context(nc.semaphore("s_" + e)) for e in ENGS}
        self.dpool = {
            "sync": [self.es.enter_context(nc.semaphore(f"ds{i}")) for i in range(n_dma_sync)],
            "gpsimd": [self.es.enter_context(nc.semaphore(f"dg{i}")) for i in range(n_dma_pool)],
        }
        self.dval = {q: [0] * len(p) for q, p in self.dpool.items()}
        self.dnext = {q: 0 for q in self.dpool}
        self.waited = {e: {} for e in ENGS}
        self.lastw = {}
        self.readers = {}
        self.uid = 0

    def sb(self, name, shape, dtype=F32):
        self.uid += 1
        return self.stacks[-1].enter_context(self.nc.sbuf_tensor(f"{name}_{self.uid}", list(shape), dtype))

    def ps(self, name, shape, dtype=F32):
        self.uid += 1
        return self.stacks[-1].enter_context(self.nc.psum_tensor(f"{name}_{self.uid}", list(shape), dtype))

    @contextmanager
    def scope(self):
        st = ExitStack()
        self.stacks.append(st)
        try:
            yield
        finally:
            self.barrier()
            self.stacks.pop()
            st.close()

    def all_tokens(self):
        toks = [("c" + e, self.sem[e], self.cnt[e]) for e in ENGS if self.cnt[e] > 0]
        for q, pool in self.dpool.items():
            for j, sem in enumerate(pool):
                if self.dval[q][j] > 0:
                    toks.append((f"d{q}{j}", sem, self.dval[q][j]))
        return toks

    def barrier(self):
        toks = self.all_tokens()
        for e in ENGS:
            waits = []
            for sid, sem, val in toks:
                if self.waited[e].get(sid, 0) < val:
                    waits.append((sem, val))
                    self.waited[e][sid] = val
            self.ops[e].append((waits, None, None))
        self.lastw = {}
        self.readers = {}

    def _deps(self, eng, reads, writes):
        toks = []
        for k in reads:
            t = self.lastw.get(k)
            if t is not None:
                toks.append(t)
        for k in writes:
            t = self.lastw.get(k)
            if t is not None:
                toks.append(t)
            toks.extend(self.readers.get(k, ()))
        wd = self.waited[eng]
        best = {}
        for (sid, sem, val) in toks:
            if sid == "c" + eng and val > self.cnt[eng]:
                continue
            if wd.get(sid, 0) < val and best.get(sid, (None, 0))[1] < val:
                best[sid] = (sem, val)
        waits = []
        for sid, (sem, val) in best.items():
            wd[sid] = val
            waits.append((sem, val))
        return waits

    def _commit(self, tok, reads, writes):
        for k in writes:
            self.lastw[k] = tok
            self.readers[k] = []
        for k in reads:
            self.readers.setdefault(k, []).append(tok)

    def op(self, eng, fn, reads=(), writes=(), silent=False):
        reads, writes = list(reads), list(writes)
        waits = self._deps(eng, reads, writes)
        if silent:
            tok = ("c" + eng, self.sem[eng], self.cnt[eng] + 1)
            self.ops[eng].append((waits, fn, None))
        else:
            self.cnt[eng] += 1
            tok = ("c" + eng, self.sem[eng], self.cnt[eng])
            self.ops[eng].append((waits, fn, (self.sem[eng], 1)))
        self._commit(tok, reads, writes)
        return tok

    def g(self, eng, name, reads, writes, *args, silent=False, **kw):
        return self.op(eng, lambda e: getattr(e, name)(*args, **kw), reads, writes, silent=silent)

    def dma(self, out, in_, reads=(), writes=(), q="sync", fn=None):
        reads, writes = list(reads), list(writes)
        waits = self._deps(q, reads, writes)
        j = self.dnext[q]
        self.dnext[q] = (j + 1) % len(self.dpool[q])
        sem = self.dpool[q][j]
        sid = f"d{q}{j}"
        prev = self.dval[q][j]
        if prev > 0 and self.waited[q].get(sid, 0) < prev:
            waits.append((sem, prev))
            self.waited[q][sid] = prev
        self.dval[q][j] = prev + 16
        tok = (sid, sem, prev + 16)
        if fn is None:
            fn = lambda e: e.dma_start(out=out, in_=in_)
        self.ops[q].append((waits, fn, (sem, 16)))
        self._commit(tok, reads, writes)
        return tok

    def gather(self, out, table, idx_ap, reads, writes):
        fn = lambda e: e.indirect_dma_start(out=out, out_offset=None, in_=table,
                                            in_offset=bass.IndirectOffsetOnAxis(ap=idx_ap, axis=0))
        return self.dma(None, None, reads, writes, q="gpsimd", fn=fn)

    def finish(self, tokens, eng="sync"):
        waits = []
        for (sid, sem, val) in tokens:
            if self.waited[eng].get(sid, 0) < val:
                waits.append((sem, val))
                self.waited[eng][sid] = val
        self.ops[eng].append((waits, None, None))

    def emit(self):
        with self.nc.Block() as block:
            def mk(engname):
                def body(e):
                    for waits, fn, inc in self.ops[engname]:
                        for sem, val in waits:
                            e.wait_ge(sem, val)
                        if fn is not None:
                            ins = fn(e)
                            if inc is not None:
                                ins.then_inc(inc[0], inc[1])
                return body
            block.sync(mk("sync"))
            block.scalar(mk("scalar"))
            block.vector(mk("vector"))
            block.gpsimd(mk("gpsimd"))
            block.tensor(mk("tensor"))
        for st in reversed(self.stacks):
            st.close()


CQ, CQS, CK, CKS, CV, CHY, CG = 0, 4, 8, 9, 10, 11, 23
NWCH = 39


def build_program(half, stop_after=None, dbg=False):
    nc = bass.Bass("TRN2", target_bir_lowering=False)

    def din(name, shape, dt=F32):
        return nc.dram_tensor(name, list(shape), dt, kind="ExternalInput").ap()

    x_d = din("x", [L, D]); ctx_d = din("ctx", [256, D]); cvec_d = din("cvec", [128, 16])
    wmod_d = din("w_mod", [12, 128, 8, 512]); bmod_d = din("b_mod", [128, 6144]); ng_d = din("ng", [128, 3, 1024])
    win_d = din("w_in", [NWCH, 128, 8, 128]); rope_d = din("rope", [2, 128, L]); sink_d = din("sink", [128, 8])
    convw_d = din("convw", [128, 12, 4]); fw1_d = din("fw1", [33, 64]); fvec_d = din("fvec", [64, 4])
    fw2_d = din("fw2", [64, 64]); fw3_d = din("fw3", [64, 1024]); fb3_d = din("fb3", [128, 1024])
    skip_d = din("skip", [128, 512]); zf_d = din("zf", [2, 33, L]); ndelta_d = din("ndelta", [128, 512])
    tpos_d = din("tpos", [128, 64]); woa_d = din("w_oa", [128, 4, 1024]); woh_d = din("w_oh", [128, 4, 1024])
    wout_d = din("w_out", [128, 8, 1024]); wq_d = din("wq", [128, 8, 1024]); kbd_d = din("kbd", [128, 8, 256])
    pu_d = din("peer_u", [16384, D]); pv_d = din("peer_v", [16384, D])
    dftc_d = din("dftc", [33, 128, 32, 128], BF16); dfts_d = din("dfts", [33, 128, 32, 128], BF16)
    idc_d = din("idftc", [33, 128, 2048], BF16); ids_d = din("idfts", [33, 128, 2048], BF16)
    cst_d = din("consts", [128, 1024]); cstb_d = din("constsb", [128, 128], BF16)
    out_d = nc.dram_tensor("out", [2048, D], F32, kind="ExternalOutput").ap()
    ya_d = nc.dram_tensor("ya_s", [4, 128, 2048], F32, kind="Internal").ap()
    yh_d = nc.dram_tensor("yh_s", [4, 128, 2048], F32, kind="Internal").ap()
    x0c_d = nc.dram_tensor("x0c_s", [4, 128, 2048], F32, kind="Internal").ap()
    xm_d = nc.dram_tensor("xm_s", [2048, D], F32, kind="Internal").ap()
    dbg_d = nc.dram_tensor("dbg", [128, 4096], F32, kind="ExternalOutput").ap() if dbg else None
    uv16_d = nc.dram_tensor("uv16_s", [16384, 2 * D], BF16, kind="Internal").ap()

    P = Prog(nc)
    own0 = 2048 * half
    oblk0 = 4 * half
    kblk0 = oblk0 - 1

    cst = P.sb("cst", [128, 1024])
    cstb = P.sb("cstb", [128, 128], BF16)
    P.dma(cst[:], cst_d, writes=["cst"]); P.dma(cstb[:], cstb_d, writes=["cstb"])
    ident = cst[:, 0:128]; bmask = cst[:, 128:512]; iota16 = cst[:, 512:528]; ones = cst[:, 528:656]
    MODX = P.sb("MODX", [128, 6144])
    FG = P.sb("FG", [128, 1024])
    P.dma(FG[:], ng_d[:, 2, :], writes=["FG"])
    outer = P.scope(); outer.__enter__()
    MODC = P.sb("MODC", [128, 2048])
    uT = P.sb("uT", [128, 4, L], BF16)

    def norm_tile(xt, np_, G, SH, out, key_x, key_out, tag):
        ss = P.sb("ss" + tag, [128, 4]) if tag not in norm_tile.cache else norm_tile.cache[tag]
        norm_tile.cache[tag] = ss
        k = "ss" + tag
        P.g("vector", "scalar_tensor_tensor", [key_x], [key_out, k], out=out[0:np_, :], in0=xt[0:np_, :], scalar=1.0, in1=xt[0:np_, :],
            op0=ALU.mult, op1=ALU.mult, accum_out=ss[0:np_, 0:1])
        P.g("vector", "tensor_scalar", [k], [k + "b"], out=ss[0:np_, 1:2], in0=ss[0:np_, 0:1], scalar1=1.0 / D, scalar2=1e-6,
            op0=ALU.mult, op1=ALU.add)
        P.g("scalar", "activation", [k + "b"], [k + "c"], out=ss[0:np_, 2:3], in_=ss[0:np_, 1:2], func=AF.Sqrt)
        P.g("vector", "reciprocal", [k + "c"], [k + "d"], out=ss[0:np_, 3:4], in_=ss[0:np_, 2:3])
        if SH is None:
            P.g("vector", "scalar_tensor_tensor", [key_x, k + "d"], [key_out], out=out[0:np_, :], in0=xt[0:np_, :],
                scalar=ss[0:np_, 3:4], in1=G[0:np_, :], op0=ALU.mult, op1=ALU.mult)
        else:
            P.g("vector", "scalar_tensor_tensor", [key_x, k + "d"], [key_out], out=out[0:np_, :], in0=xt[0:np_, :],
                scalar=ss[0:np_, 3:4], in1=G[0:np_, :], op0=ALU.mult, op1=ALU.mult)
            P.g("vector", "tensor_tensor", [key_out], [key_out], out=out[0:np_, :], in0=out[0:np_, :], in1=SH[0:np_, :], op=ALU.add)
    norm_tile.cache = {}

    with P.scope():
        cv = P.sb("cv", [128, 16]); cs = P.sb("cs", [128, 16]); CB = P.sb("CB", [128, 2, 8, 128])
        ngt = P.sb("ngt", [128, 2, 1024])
        wm = [P.sb("wm0", [128, 8, 512]), P.sb("wm1", [128, 8, 512])]
        bm = [P.sb("bm0", [128, 512]), P.sb("bm1", [128, 512])]
        pm = [P.ps("pm0", [128, 512]), P.ps("pm1", [128, 512])]
        P.dma(cv[:], cvec_d, writes=["cv"]); P.dma(ngt[:], ng_d[:, 0:2, :], writes=["ngt"])
        P.g("scalar", "activation", ["cv"], ["cs"], out=cs[:], in_=cv[:], func=AF.Silu)
        for v in range(2):
            P.g("vector", "tensor_copy", ["cs"], ["CB"], out=CB[:, v, :, :],
                in_=cs[:, v * 8:(v + 1) * 8].unsqueeze(2).to_broadcast([128, 8, 128]))
        it = 0
        for v, nblk, dst in ((0, 12, MODX), (1, 4, MODC)):
            for j in range(nblk):
                b = it % 2; it += 1
                P.dma(wm[b][:], wmod_d[j], writes=[f"wm{b}"])
                P.dma(bm[b][:], bmod_d[:, j * 512:(j + 1) * 512], writes=[f"bm{b}"])
                for c in range(8):
                    P.g("tensor", "matmul", ["CB", f"wm{b}"], [f"pm{b}"], pm[b][:], lhsT=CB[:, v, c, :], rhs=wm[b][:, c, :],
                        start=(c == 0), stop=(c == 7), silent=(c != 7))
                P.g("vector", "tensor_tensor", [f"pm{b}", f"bm{b}"], ["MOD"], out=dst[:, j * 512:(j + 1) * 512], in0=pm[b][:],
                    in1=bm[b][:], op=ALU.add)
        for dst, lo, gi in ((MODX, 1024, 0), (MODX, 4096, 1), (MODC, 1024, 0)):
            P.g("vector", "scalar_tensor_tensor", ["MOD", "ngt"], ["MOD"], out=dst[:, lo:lo + 1024], in0=dst[:, lo:lo + 1024],
                scalar=1.0, in1=ngt[:, gi, :], op0=ALU.add, op1=ALU.mult)
    SH1 = MODX[:, 0:1024]; G1 = MODX[:, 1024:2048]; g1 = MODX[:, 2048:3072]
    SH2 = MODX[:, 3072:4096]; G2 = MODX[:, 4096:5120]; g2 = MODX[:, 5120:6144]
    if dbg and stop_after == 0:
        t = P.dma(dbg_d[:, 0:4096], MODX[:, 0:4096], reads=[])
        P.finish(P.all_tokens()); P.emit(); return nc

    with P.scope():
        QT = P.sb("QT", [128, 4, 2048]); KT = P.sb("KT", [128, 6 * 512]); V = P.sb("V", [128, 24, 128], F32R)
        CKT = P.sb("CKT", [128, 256]); CVt = P.sb("CV", [128, 2, 128], F32R)
        sinkt = P.sb("sinkt", [128, 8]); convw = P.sb("convw", [128, 12, 4])
        P.dma(sinkt[:], sink_d, writes=["sink"]); P.dma(convw[:], convw_d, writes=["convw"])
        with P.scope():
            xt = [P.sb("xt0", [128, 1024]), P.sb("xt1", [128, 1024])]
            ht = [P.sb("ht0", [128, 1024]), P.sb("ht1", [128, 1024])]
            hT = P.sb("hT", [128, 8, 514], F32R)
            Wf = [P.sb(f"Wf{i}", [128, 8, 128]) for i in range(3)]
            W = [P.sb(f"W{i}", [128, 8, 128], F32R) for i in range(3)]
            rp = P.sb("rp", [128, 2, 512])
            ZT = [P.sb(f"ZT{i}", [128, 514]) for i in range(2)]
            ca = P.sb("ca", [128, 512]); cb = P.sb("cb", [128, 512]); x0s_ = P.sb("x0s0", [128, 512]); x0s = [x0s_, x0s_]
            tq = P.sb("tq", [128, 512]); tq2 = P.sb("tq2", [128, 512])
            pT = P.ps("pT", [128, 1024]); pA = [P.ps("pA0", [128, 1024]), P.ps("pA1", [128, 1024])]
            pB0_ = P.ps("pB0", [128, 512]); pB = [pB0_, pB0_]
            wcnt = [0]; zcnt = [0]; acnt = [0]; xcnt = [0]

            def load_w(ci):
                i = wcnt[0] % 3; wcnt[0] += 1
                P.dma(Wf[i][:], win_d[ci], writes=[f"Wf{i}"])
                if wcnt[0] % 3 == 0:
                    P.g("gpsimd", "tensor_copy", [f"Wf{i}"], [f"W{i}"], out=W[i][:], in_=Wf[i][:])
                else:
                    P.g("scalar", "activation", [f"Wf{i}"], [f"W{i}"], out=W[i][:], in_=Wf[i][:], func=AF.Copy)
                return W[i], f"W{i}"

            def make_hT(src_d, row0, ntile, G, SH, col0):
                for t in range(ntile):
                    b = xcnt[0] % 2; xcnt[0] += 1
                    P.dma(xt[b][:], src_d[row0 + t * 128: row0 + (t + 1) * 128, :], writes=[f"xt{b}"])
                    norm_tile(xt[b], 128, G, SH, ht[b], f"xt{b}", f"ht{b}", "m")
                    for c in range(8):
                        P.g("tensor", "transpose", [f"ht{b}", "cst"], ["pT"], out=pT[:, c * 128:(c + 1) * 128],
                            in_=ht[b][:, c * 128:(c + 1) * 128], identity=ident, silent=(c != 7))
                    P.g("scalar", "activation", ["pT"], ["hT"], out=hT[:, :, col0 + t * 128: col0 + (t + 1) * 128],
                        in_=pT[:].rearrange("p (c n) -> p c n", c=8), func=AF.Copy)

            def proj_fm(ci, n0, n1, pdst, pkey):
                w, wk = load_w(ci)
                for c in range(8):
                    P.g("tensor", "matmul", [wk, "hT"], [pkey], pdst, lhsT=w[:, c, :], rhs=hT[:, c, n0:n1],
                        start=(c == 0), stop=(c == 7), silent=(c != 7))

            def rope_fm(ci, cis, dst, dkey, tcol):
                a = acnt[0] % 2; acnt[0] += 1
                proj_fm(ci, 1, 513, pA[a][:, 0:512], f"pA{a}")
                proj_fm(cis, 1, 513, pB[a][:], "pB0")
                P.g("scalar", "activation", ["pB0"], ["tq"], out=tq[:], in_=pB[a][:], func=AF.Copy)
                P.g("vector", "tensor_tensor", ["tq", "rp"], ["tq"], out=tq[:], in0=tq[:], in1=rp[:, 1, :], op=ALU.mult)
                P.g("vector", "tensor_tensor", [f"pA{a}", "rp"], ["tq2"], out=tq2[:], in0=pA[a][:, 0:512], in1=rp[:, 0, :], op=ALU.mult)
                P.g("vector", "tensor_tensor", ["tq", "tq2"], [dkey], out=dst, in0=tq[:], in1=tq2[:], op=ALU.add)

            def conv3(z, zk, ci, dst, dkey):
                cw = convw[:, ci - CHY, :]
                P.g("vector", "tensor_scalar", [zk, "convw"], [dkey], out=dst, in0=z[:, 1:513], scalar1=cw[:, 1:2], scalar2=cw[:, 3:4],
                    op0=ALU.mult, op1=ALU.add)
                P.g("vector", "scalar_tensor_tensor", [zk, "convw", dkey], [dkey], out=dst, in0=z[:, 0:512], scalar=cw[:, 0:1], in1=dst,
                    op0=ALU.mult, op1=ALU.add)
                P.g("vector", "scalar_tensor_tensor", [zk, "convw", dkey], [dkey], out=dst, in0=z[:, 2:514], scalar=cw[:, 2:3], in1=dst,
                    op0=ALU.mult, op1=ALU.add)

            def proj_z(ci):
                a = acnt[0] % 2; acnt[0] += 1
                zi = zcnt[0] % 2; zcnt[0] += 1
                w, wk = load_w(ci)
                for (c0, c1, o0) in ((0, 258, 0), (258, 514, 512)):
                    for c in range(8):
                        P.g("tensor", "matmul", [wk, "hT"], [f"pA{a}"], pA[a][:, o0:o0 + (c1 - c0)], lhsT=w[:, c, :],
                            rhs=hT[:, c, c0:c1], start=(c == 0), stop=(c == 7), silent=(c != 7))
                P.g("scalar", "activation", [f"pA{a}"], [f"ZT{zi}"], out=ZT[zi][:, 0:258], in_=pA[a][:, 0:258], func=AF.Copy)
                P.g("scalar", "activation", [f"pA{a}"], [f"ZT{zi}"], out=ZT[zi][:, 258:514], in_=pA[a][:, 512:768], func=AF.Copy)
                return ZT[zi], f"ZT{zi}"

            make_hT(ctx_d, 0, 2, MODC[:, 1024:2048], MODC[:, 0:1024], 1)
            proj_fm(CK, 1, 257, pA[0][:, 0:256], "pA0")
            P.g("scalar", "activation", ["pA0"], ["CKT"], out=CKT[:], in_=pA[0][:, 0:256], func=AF.Copy)
            w, wk = load_w(CV)
            for t in range(2):
                for c in range(8):
                    P.g("tensor", "matmul", [wk, "hT"], ["pB0"], pB[0][:, t * 128:(t + 1) * 128], lhsT=hT[:, c, 1 + t * 128: 1 + (t + 1) * 128],
                        rhs=w[:, c, :], start=(c == 0), stop=(c == 7), silent=(c != 7))
            P.g("scalar", "activation", ["pB0"], ["CV"], out=CVt[:], in_=pB[0][:, 0:256].rearrange("p (t n) -> p t n", t=2), func=AF.Copy)

            pTh = P.ps("pTh", [128, 16])
            for B in range(8):
                t0 = B * 512
                own = oblk0 <= B < oblk0 + 4
                kv = kblk0 <= B < kblk0 + 6
                make_hT(x_d, t0, 4, G1, SH1, 1)
                r0 = max(t0 - 1, 0); r1 = min(t0 + 512, L - 1)
                hb = xcnt[0] % 2; xcnt[0] += 1
                hx = xt[hb]; hh = ht[hb]
                P.dma(hx[0:1, :], x_d[r0:r0 + 1, :], writes=[f"xt{hb}"]); P.dma(hx[1:2, :], x_d[r1:r1 + 1, :], writes=[f"xt{hb}"])
                norm_tile(hx, 2, G1, SH1, hh, f"xt{hb}", f"ht{hb}", "h")
                for c in range(8):
                    P.g("tensor", "transpose", [f"ht{hb}", "cst"], ["pTh"], out=pTh[:, c * 2:(c + 1) * 2], in_=hh[0:2, c * 128:(c + 1) * 128],
                        identity=cst[0:2, 0:2], silent=(c != 7))
                pv = pTh[:].rearrange("p (c n) -> p c n", c=8)
                P.g("vector", "tensor_copy", ["pTh"], ["hT"], out=hT[:, :, 0:1], in_=pv[:, :, 0:1])
                P.g("vector", "tensor_copy", ["pTh"], ["hT"], out=hT[:, :, 513:514], in_=pv[:, :, 1:2])
                if B == 0:
                    P.g("vector", "tensor_copy", ["cst"], ["hT"], out=hT[:, :, 0:1], in_=cst[:, 664:672].unsqueeze(2))
                if B == 7:
                    P.g("vector", "tensor_copy", ["cst"], ["hT"], out=hT[:, :, 513:514], in_=cst[:, 664:672].unsqueeze(2))
                if own or kv:
                    P.dma(rp[:], rope_d[:, :, t0:t0 + 512].rearrange("a p n -> p a n"), writes=["rp"])
                if kv:
                    kc0 = (B - kblk0) * 512
                    rope_fm(CK, CKS, KT[:, kc0:kc0 + 512], "KT", t0)
                    w, wk = load_w(CV)
                    for t in range(4):
                        for c in range(8):
                            P.g("tensor", "matmul", [wk, "hT"], ["pB0"], pB[0][:, t * 128:(t + 1) * 128],
                                lhsT=hT[:, c, 1 + t * 128: 1 + (t + 1) * 128], rhs=w[:, c, :], start=(c == 0), stop=(c == 7), silent=(c != 7))
                    vt0 = (B - kblk0) * 4
                    P.g("scalar", "activation", ["pB0"], ["V"], out=V[:, vt0:vt0 + 4, :], in_=pB[0][:].rearrange("p (t n) -> p t n", t=4),
                        func=AF.Copy)
                if own:
                    oc0 = t0 - own0
                    for c in range(4):
                        rope_fm(CQ + c, CQS + c, QT[:, c, oc0:oc0 + 512], "QT", t0)
                    for c in range(4):
                        z, zk = proj_z(CHY + c)
                        b = 0
                        conv3(z, zk, CHY + c, x0s[b][:], f"x0s{b}")
                        P.dma(x0c_d[c, :, oc0:oc0 + 512], x0s[b][:], reads=[f"x0s{b}"], writes=["x0c_d"])
                for c in range(4):
                    z1, z1k = proj_z(CHY + 4 + c)
                    z2, z2k = proj_z(CHY + 8 + c)
                    conv3(z1, z1k, CHY + 4 + c, ca[:], "ca")
                    conv3(z2, z2k, CHY + 8 + c, cb[:], "cb")
                    P.g("vector", "tensor_tensor", ["ca", "cb"], ["uT"], out=uT[:, c, t0:t0 + 512], in0=ca[:], in1=cb[:], op=ALU.mult)
        if dbg and stop_after == 1:
            P.dma(dbg_d[:, 0:2048], QT[:, 0, :], reads=["QT"]); P.dma(dbg_d[:, 2048:4096], KT[:, 512:2560], reads=["KT"])
            P.finish(P.all_tokens()); P.emit(); return nc

        with P.scope():
            Sm = [P.sb("Sm0", [128, 648]), P.sb("Sm1", [128, 648])]
            Pm = [P.sb("Pm0", [128, 648]), P.sb("Pm1", [128, 648])]
            PTs = [P.sb("PTs0", [128, 640], F32R), P.sb("PTs1", [128, 640], F32R)]
            st = [P.sb("st0", [128, 4]), P.sb("st1", [128, 4])]
            Ysb = P.sb("Ysb", [128, 512]); YAs = [P.sb("YAs0", [128, 4, 128]), P.sb("YAs1", [128, 4, 128])]
            pS0_ = P.ps("pS0", [128, 1024]); pS = [pS0_, pS0_]
            pPT = [P.ps("pPT0", [128, 1024]), P.ps("pPT1", [128, 1024])]
            pY = P.ps("pY", [128, 512]); pYT = P.ps("pYT", [128, 512])
            for b in range(2):
                P.g("vector", "memset", [], [f"Sm{b}"], Sm[b][:], NEG)
            P.g("vector", "tensor_copy", ["cst"], ["KT"], out=KT[:, 0:512], in_=cst[:, 664:665].to_broadcast([128, 512]))
            it = 0
            cin = [P.sb(f"cin{i}", [128, 2, 1024]) for i in range(3)]
            cou = [P.sb(f"cou{i}", [128, 2, 1024], BF16) for i in range(3)]
            cast_steps = [(src, dst, r) for (src, dst) in ((pu_d, uv16_d[:, 0:D]), (pv_d, uv16_d[:, D:2 * D])) for r in range(64)]

            def cast_step(k):
                src, dst, r = cast_steps[k]
                i = k % 3
                P.dma(cin[i][:], src[r * 256:(r + 1) * 256, :].rearrange("(a p) d -> p a d", p=128), writes=[f"cin{i}"])
                P.g("scalar", "activation", [f"cin{i}"], [f"cou{i}"], out=cou[i][:], in_=cin[i][:], func=AF.Copy)
                P.dma(dst[r * 256:(r + 1) * 256, :].rearrange("(a p) d -> p a d", p=128), cou[i][:], reads=[f"cou{i}"], writes=["p16"])
            def geom(n):
                gt = 16 * half + n
                lo = 128 if gt == 0 else 0
                hi = 256 if gt == 31 else 384
                kc0 = (gt - 1) * 128 - kblk0 * 512
                vt0 = (gt - 1) - kblk0 * 4
                return lo, hi, kc0, vt0

            def stage_a(i):
                n, hd = divmod(i, 8)
                lo, hi, kc0, vt0 = geom(n)
                cast_step(i)
                b = i % 2
                c = hd % 4; po = (hd // 4) * 64
                qv = QT[po:po + 64, c, n * 128:(n + 1) * 128]
                P.g("tensor", "matmul", ["QT", "KT"], ["pS0"], pS[b][:, 0:hi], lhsT=qv, rhs=KT[po:po + 64, kc0: kc0 + hi],
                    start=True, stop=True, silent=True)
                P.g("tensor", "matmul", ["QT", "CKT"], ["pS0"], pS[b][:, 512:768], lhsT=qv, rhs=CKT[po:po + 64, :],
                    start=True, stop=True)
                if lo > 0:
                    P.g("vector", "memset", [], [f"Sm{b}"], Sm[b][:, 0:lo], NEG)
                if hi < 384:
                    P.g("vector", "memset", [], [f"Sm{b}"], Sm[b][:, hi:384], NEG)
                P.g("vector", "scalar_tensor_tensor", ["pS0", "cst"], [f"Sm{b}"], out=Sm[b][:, lo:hi], in0=pS[b][:, lo:hi],
                    scalar=0.125, in1=bmask[:, lo:hi], op0=ALU.mult, op1=ALU.add)
                P.g("scalar", "activation", ["pS0"], [f"Sm{b}"], out=Sm[b][:, 384:640], in_=pS[b][:, 512:768], func=AF.Copy, scale=0.125)
                P.g("vector", "tensor_copy", ["sink"], [f"Sm{b}"], out=Sm[b][:, 640:641], in_=sinkt[:, hd:hd + 1])
                P.g("vector", "tensor_reduce", [f"Sm{b}"], [f"st{b}"], out=st[b][:, 0:1], in_=Sm[b][:, 0:641], axis=AX.X, op=ALU.max, negate=True)

            def stage_b(i):
                n, hd = divmod(i, 8)
                lo, hi, kc0, vt0 = geom(n)
                b = i % 2
                po = (hd // 4) * 64
                P.g("scalar", "activation", [f"Sm{b}", f"st{b}"], [f"Pm{b}", f"st{b}b"], out=Pm[b][:, 0:641], in_=Sm[b][:, 0:641], func=AF.Exp,
                    bias=st[b][:, 0:1], scale=1.0, accum_out=st[b][:, 1:2])
                P.g("vector", "reciprocal", [f"st{b}b"], [f"st{b}c"], out=st[b][:, 2:3], in_=st[b][:, 1:2])
                P.g("vector", "tensor_scalar", [f"Pm{b}", f"st{b}c"], [f"Pm{b}"], out=Pm[b][:, 0:640], in0=Pm[b][:, 0:640], scalar1=st[b][:, 2:3],
                    scalar2=None, op0=ALU.mult)
                for kt in range(5):
                    P.g("tensor", "transpose", [f"Pm{b}", "cst"], [f"pPT{b}"], out=pPT[b][:, kt * 128:(kt + 1) * 128],
                        in_=Pm[b][:, kt * 128:(kt + 1) * 128], identity=ident, silent=(kt != 4))

            def stage_c(i):
                n, hd = divmod(i, 8)
                lo, hi, kc0, vt0 = geom(n)
                b = i % 2
                po = (hd // 4) * 64
                P.g("scalar", "activation", [f"pPT{b}"], [f"PTs{b}"], out=PTs[b][:], in_=pPT[b][:, 0:640], func=AF.Copy)
                mms = []
                for kt in range(3):
                    if lo <= kt * 128 < hi:
                        mms.append((kt, V[:, vt0 + kt, po:po + 64], "V"))
                mms.append((3, CVt[:, 0, po:po + 64], "CV")); mms.append((4, CVt[:, 1, po:po + 64], "CV"))
                for j, (kt, rv, rk) in enumerate(mms):
                    P.g("tensor", "matmul", [f"PTs{b}", rk], ["pY"], pY[:, hd * 64:(hd + 1) * 64], lhsT=PTs[b][:, kt * 128:(kt + 1) * 128], rhs=rv,
                        start=(j == 0), stop=(j == len(mms) - 1), silent=(j != len(mms) - 1))
                if hd == 7:
                    P.g("vector", "tensor_copy", ["pY"], ["Ysb"], out=Ysb[:], in_=pY[:])
                    for c in range(4):
                        P.g("tensor", "transpose", ["Ysb", "cst"], ["pYT"], out=pYT[:, c * 128:(c + 1) * 128], in_=Ysb[:, c * 128:(c + 1) * 128], identity=ident, silent=(c != 3))
                    yb = n % 2
                    P.g("scalar", "activation", ["pYT"], [f"YAs{yb}"], out=YAs[yb][:], in_=pYT[:].rearrange("p (c n) -> p c n", c=4), func=AF.Copy)
                    P.dma(ya_d[:, :, n * 128:(n + 1) * 128].rearrange("c p n -> p c n"), YAs[yb][:], reads=[f"YAs{yb}"], writes=["ya_d"])

            stage_a(0); stage_a(1); stage_b(0)
            for i in range(128):
                if i + 2 < 128:
                    stage_a(i + 2)
                if i + 1 < 128:
                    stage_b(i + 1)
                stage_c(i)
    if dbg and stop_after == 2:
        P.dma(dbg_d[:, 0:2048], ya_d[0], reads=[]); P.dma(dbg_d[:, 2048:4096], ya_d[3], reads=[])
        P.finish(P.all_tokens()); P.emit(); return nc

    with P.scope():
        fw1 = P.sb("fw1", [33, 64]); fvec = P.sb("fvec", [64, 8]); fw2 = P.sb("fw2", [64, 64]); fw3 = P.sb("fw3", [64, 1024])
        tpos = P.sb("tpos", [128, 64])
        P.dma(fw1[:], fw1_d, writes=["fw"]); P.dma(fvec[:, 0:4], fvec_d, writes=["fvec"]); P.dma(fw2[:], fw2_d, writes=["fw"])
        P.dma(fw3[:], fw3_d, writes=["fw"]); P.dma(tpos[:], tpos_d, writes=["tpos"])
        P.g("vector", "tensor_tensor", ["fvec"], ["fvec2"], out=fvec[:, 4:5], in0=fvec[:, 0:1], in1=fvec[:, 2:3], op=ALU.mult)
        P.g("vector", "tensor_tensor", ["fvec"], ["fvec2"], out=fvec[:, 5:6], in0=fvec[:, 1:2], in1=fvec[:, 2:3], op=ALU.mult)
        RH = P.sb("RH", [128, 32, 768], BF16)
        YFr = P.sb("YFr", [128, 33, 256], BF16); YFi = P.sb("YFi", [128, 33, 256], BF16)
        rinv = P.sb("rinv", [128, 256])
        for ps_ in range(2):
            ch0 = ps_ * 256
            with P.scope():
                pU = [P.ps("pU0", [128, 256], BF16), P.ps("pU1", [128, 256], BF16)]
                for s in range(32):
                    b = s % 2
                    for c2 in range(2):
                        P.g("tensor", "transpose", ["uT", "cstb"], [f"pU{b}"], out=pU[b][:, c2 * 128:(c2 + 1) * 128],
                            in_=uT[:, ps_ * 2 + c2, s * 128:(s + 1) * 128], identity=cstb[:], silent=(c2 != 1))
                    P.g("scalar", "activation", [f"pU{b}"], ["RHu"], out=RH[:, s, 256:512], in_=pU[b][:], func=AF.Copy)
            with P.scope():
                zf = [P.sb("zf0", [33, 512]), P.sb("zf1", [33, 512])]
                h1 = [P.sb("h1a", [64, 512]), P.sb("h1b", [64, 512])]; h2 = [[P.sb(f"h2f{i}", [64, 512]), P.sb(f"h2b{i}", [64, 512])] for i in range(2)]
                wa = [P.sb("wa0", [64, 512]), P.sb("wa1", [64, 512])]; wb = [P.sb("wb0", [64, 512]), P.sb("wb1", [64, 512])]
                fb3p = P.sb("fb3p", [128, 512]); ndl = P.sb("ndl", [128, 256])
                dec = [P.sb("dec0", [128, 512]), P.sb("dec1", [128, 512])]
                hfd = [P.sb("hfd0", [128, 512]), P.sb("hfd1", [128, 512])]; ab = [P.sb("ab0", [128, 512]), P.sb("ab1", [128, 512])]
                pF = [P.ps("pF0", [128, 512]), P.ps("pF1", [128, 512])]; pH = [P.ps("pH0", [128, 512]), P.ps("pH1", [128, 512])]; pN = P.ps("pN", [128, 256])
                P.dma(fb3p[:, 0:256], fb3_d[:, ch0:ch0 + 256], writes=["fb3p"]); P.dma(fb3p[:, 256:512], fb3_d[:, 512 + ch0:512 + ch0 + 256], writes=["fb3p"])
                P.dma(ndl[:], ndelta_d[:, ch0:ch0 + 256], writes=["ndl"])

                def sin_layer(v, bias_col, dst, dkey):
                    A_, B_ = wa[v], wb[v]
                    P.g("vector", "tensor_scalar", [f"pF{v}", "fvec", "fvec2"], [f"wa{v}"], out=A_[:], in0=pF[v][0:64, :], scalar1=fvec[:, 2:3], scalar2=fvec[:, bias_col:bias_col + 1],
                        op0=ALU.mult, op1=ALU.add)
                    P.g("vector", "tensor_scalar", [f"wa{v}"], [f"wb{v}"], out=B_[:], in0=A_[:], scalar1=-math.pi, scalar2=2 * math.pi, op0=ALU.is_lt, op1=ALU.mult)
                    P.g("vector", "tensor_tensor", [f"wa{v}", f"wb{v}"], [f"wb{v}"], out=B_[:], in0=A_[:], in1=B_[:], op=ALU.add)
                    P.g("vector", "tensor_scalar", [f"wa{v}"], [f"wa{v}"], out=A_[:], in0=A_[:], scalar1=math.pi, scalar2=-2 * math.pi, op0=ALU.is_gt, op1=ALU.mult)
                    P.g("vector", "tensor_tensor", [f"wa{v}", f"wb{v}"], [f"wb{v}"], out=B_[:], in0=A_[:], in1=B_[:], op=ALU.add)
                    P.g("scalar", "activation", [f"wb{v}"], [dkey], out=dst, in_=B_[:], func=AF.Sin)

                def sin_stage(nb):
                    hb = nb % 2
                    for v in range(2):
                        P.dma(zf[v][:], zf_d[v, :, nb * 512:(nb + 1) * 512], writes=[f"zf{v}"])
                        P.g("tensor", "matmul", ["fw", f"zf{v}"], [f"pF{v}"], pF[v][0:64, :], lhsT=fw1[:], rhs=zf[v][:], start=True, stop=True)
                    for v in range(2):
                        sin_layer(v, 4, h1[v][:], f"h1{v}")
                    for v in range(2):
                        P.g("tensor", "matmul", ["fw", f"h1{v}"], [f"pF{v}"], pF[v][0:64, :], lhsT=fw2[:], rhs=h1[v][:], start=True, stop=True)
                    for v in range(2):
                        sin_layer(v, 5, h2[hb][v][:], f"h2{hb}{v}")

                def chunk_a(s):
                    nb, q = divmod(s, 4); hb = nb % 2; b = s % 2
                    P.g("tensor", "matmul", ["fw", f"h2{hb}0"], [f"pH{b}"], pH[b][:, 0:256], lhsT=h2[hb][0][:, q * 128:(q + 1) * 128], rhs=fw3[:, ch0:ch0 + 256], start=True, stop=True, silent=True)
                    P.g("tensor", "matmul", ["fw", f"h2{hb}1"], [f"pH{b}"], pH[b][:, 256:512], lhsT=h2[hb][1][:, q * 128:(q + 1) * 128], rhs=fw3[:, 512 + ch0:512 + ch0 + 256],
                        start=True, stop=True)
                    P.g("scalar", "activation", ["ndl", "tpos"], [f"dec{b}"], out=dec[b][:, 0:256], in_=ndl[:], func=AF.Exp, scale=tpos[:, s:s + 1])
                    P.g("scalar", "activation", ["ndl", "tpos"], [f"dec{b}"], out=dec[b][:, 256:512], in_=ndl[:], func=AF.Exp, scale=tpos[:, 32 + s:33 + s])
                    P.g("vector", "tensor_tensor", [f"pH{b}", "fb3p"], [f"hfd{b}"], out=hfd[b][:], in0=pH[b][:], in1=fb3p[:], op=ALU.add)
                    P.g("vector", "tensor_tensor", [f"hfd{b}", f"dec{b}"], [f"hfd{b}"], out=hfd[b][:], in0=hfd[b][:], in1=dec[b][:], op=ALU.mult)
                    if s == 0:
                        P.g("vector", "memset", [], [f"hfd{b}"], hfd[b][0:1, 256:512], 0.0)
                    P.g("scalar", "activation", [f"hfd{b}"], [f"ab{b}"], out=ab[b][:], in_=hfd[b][:], func=AF.Abs)
                    P.g("vector", "tensor_tensor", [f"hfd{b}"], ["RHke"], out=RH[:, s, 0:256], in0=hfd[b][:, 0:256], in1=hfd[b][:, 256:512], op=ALU.add)
                    P.g("gpsimd", "tensor_tensor", [f"hfd{b}"], ["RHko"], out=RH[:, s, 512:768], in0=hfd[b][:, 0:256], in1=hfd[b][:, 256:512], op=ALU.subtract)

                def chunk_b(s):
                    b = s % 2
                    P.g("tensor", "matmul", [f"ab{b}", "cst"], ["pN"], pN[:], lhsT=ones, rhs=ab[b][:, 0:256], start=(s == 0), stop=False, silent=True)
                    P.g("tensor", "matmul", [f"ab{b}", "cst"], ["pN"], pN[:], lhsT=ones, rhs=ab[b][:, 256:512], start=False, stop=(s == 31))

                sin_stage(0)
                for nb in range(8):
                    if nb + 1 < 8:
                        sin_stage(nb + 1)
                    for q in range(4):
                        s = nb * 4 + q
                        chunk_a(s)
                        if s > 0:
                            chunk_b(s - 1)
                chunk_b(31)
                P.g("vector", "reciprocal", ["pN"], ["rinv"], out=rinv[:], in_=pN[:])
            with P.scope():
                TC = [P.sb("TC0", [128, 32, 128], BF16), P.sb("TC1", [128, 32, 128], BF16)]
                TS = [P.sb("TS0", [128, 32, 128], BF16), P.sb("TS1", [128, 32, 128], BF16)]
                skp = P.sb("skp", [128, 256]); Ap = P.sb("Ap", [128, 256]); Bp = P.sb("Bp", [128, 256])
                t1 = P.sb("t1", [128, 256]); t2 = P.sb("t2", [128, 256])
                pC = [P.ps("pC0", [128, 512]), P.ps("pC1", [128, 512])]; pSn = [P.ps("pSn0", [128, 512]), P.ps("pSn1", [128, 512])]
                P.dma(skp[:], skip_d[:, ch0:ch0 + 256], writes=["skp"])
                for j in range(33):
                    b = j % 2
                    P.dma(TC[b][:], dftc_d[j], writes=[f"TC{b}"]); P.dma(TS[b][:], dfts_d[j], writes=[f"TS{b}"])
                    for s in range(32):
                        P.g("tensor", "matmul", [f"TC{b}", "RHu", "RHke"], [f"pC{b}"], pC[b][:], lhsT=TC[b][:, s, :], rhs=RH[:, s, 0:512], start=(s == 0), stop=(s == 31), silent=(s != 31))
                    for s in range(32):
                        P.g("tensor", "matmul", [f"TS{b}", "RHu", "RHko"], [f"pSn{b}"], pSn[b][:], lhsT=TS[b][:, s, :], rhs=RH[:, s, 256:768], start=(s == 0), stop=(s == 31), silent=(s != 31))
                    P.g("vector", "tensor_tensor", [f"pC{b}", "rinv"], ["Ap"], out=Ap[:], in0=pC[b][:, 0:256], in1=rinv[:], op=ALU.mult)
                    P.g("vector", "tensor_tensor", ["Ap", "skp"], ["Ap"], out=Ap[:], in0=Ap[:], in1=skp[:], op=ALU.add)
                    P.g("vector", "tensor_tensor", [f"pSn{b}", "rinv"], ["Bp"], out=Bp[:], in0=pSn[b][:, 256:512], in1=rinv[:], op=ALU.mult)
                    P.g("vector", "tensor_scalar", ["Bp", "cst"], ["Bp"], out=Bp[:], in0=Bp[:], scalar1=cst[:, 656:657], scalar2=None, op0=ALU.mult)
                    P.g("vector", "tensor_tensor", [f"pC{b}", "Ap"], ["t1"], out=t1[:], in0=pC[b][:, 256:512], in1=Ap[:], op=ALU.mult)
                    P.g("vector", "tensor_tensor", [f"pSn{b}", "Bp"], ["t2"], out=t2[:], in0=pSn[b][:, 0:256], in1=Bp[:], op=ALU.mult)
                    P.g("vector", "tensor_tensor", ["t1", "t2"], ["YF"], out=YFr[:, j, :], in0=t1[:], in1=t2[:], op=ALU.subtract)
                    P.g("vector", "tensor_tensor", [f"pC{b}", "Bp"], ["t1"], out=t1[:], in0=pC[b][:, 256:512], in1=Bp[:], op=ALU.mult)
                    P.g("vector", "tensor_tensor", [f"pSn{b}", "Ap"], ["t2"], out=t2[:], in0=pSn[b][:, 0:256], in1=Ap[:], op=ALU.mult)
                    P.g("vector", "tensor_tensor", ["t1", "t2"], ["YF"], out=YFi[:, j, :], in0=t1[:], in1=t2[:], op=ALU.add)
            with P.scope():
                IC = [P.sb("IC0", [128, 2048], BF16), P.sb("IC1", [128, 2048], BF16)]
                IS = [P.sb("IS0", [128, 2048], BF16), P.sb("IS1", [128, 2048], BF16)]
                x0c = P.sb("x0c", [128, 2048]); yst = [P.sb("yst0", [128, 512]), P.sb("yst1", [128, 512])]
                pO = [[P.ps(f"pO{cc}{tb}", [128, 512]) for tb in range(4)] for cc in range(2)]
                for kc in range(33):
                    b = kc % 2
                    P.dma(IC[b][:], idc_d[kc], writes=[f"IC{b}"]); P.dma(IS[b][:], ids_d[kc], writes=[f"IS{b}"])
                    for cc in range(2):
                        for tb in range(4):
                            P.g("tensor", "matmul", ["YF", f"IC{b}"], [f"pO{cc}{tb}"], pO[cc][tb][:], lhsT=YFr[:, kc, cc * 128:(cc + 1) * 128],
                                rhs=IC[b][:, tb * 512:(tb + 1) * 512], start=(kc == 0), stop=False, silent=True)
                            P.g("tensor", "matmul", ["YF", f"IS{b}"], [f"pO{cc}{tb}"], pO[cc][tb][:], lhsT=YFi[:, kc, cc * 128:(cc + 1) * 128],
                                rhs=IS[b][:, tb * 512:(tb + 1) * 512], start=False, stop=(kc == 32), silent=not (cc == 1 and tb == 3))
                i = 0
                for cc in range(2):
                    cg = ps_ * 2 + cc
                    P.dma(x0c[:], x0c_d[cg], reads=["x0c_d"], writes=["x0c"])
                    for tb in range(4):
                        b = i % 2; i += 1
                        P.g("vector", "tensor_tensor", [f"pO{cc}{tb}", "x0c"], [f"yst{b}"], out=yst[b][:], in0=pO[cc][tb][:], in1=x0c[:, tb * 512:(tb + 1) * 512], op=ALU.mult)
                        P.dma(yh_d[cg, :, tb * 512:(tb + 1) * 512], yst[b][:], reads=[f"yst{b}"], writes=["yh_d"])
    if dbg and stop_after == 3:
        P.dma(dbg_d[:, 0:2048], yh_d[0], reads=[]); P.dma(dbg_d[:, 2048:4096], yh_d[3], reads=[])
        P.finish(P.all_tokens()); P.emit(); return nc

    outer.__exit__(None, None, None)
    with P.scope():
        woa = P.sb("woa", [128, 4, 1024], F32R); woh = P.sb("woh", [128, 4, 1024], F32R); wout = P.sb("wout", [128, 8, 1024], F32R)
        Wf = [P.sb(f"Wf{i}", [128, 8, 128]) for i in range(2)]
        W = [P.sb(f"W{i}", [128, 8, 128], F32R) for i in range(2)]
        k_ = 0
        for (wt_, wd_, wk_, nk_) in ((woa, woa_d, "woa", 4), (woh, woh_d, "woh", 4), (wout, wout_d, "wout", 8)):
            for c in range(nk_):
                i = k_ % 2; k_ += 1
                P.dma(Wf[i][:].rearrange("p a b -> p (a b)"), wd_[:, c, :], writes=[f"Wf{i}"])
                P.g("gpsimd", "tensor_copy", [f"Wf{i}"], [wk_], out=wt_[:, c, :], in_=Wf[i][:].rearrange("p a b -> p (a b)"))
        xt = [P.sb("xt0", [128, 1024]), P.sb("xt1", [128, 1024])]; ht = [P.sb("ht0", [128, 1024]), P.sb("ht1", [128, 1024])]
        hT = P.sb("hT", [128, 8, 512], F32R)
        YA = P.sb("YA", [128, 4, 512]); YH = P.sb("YH", [128, 4, 512]); MT = P.sb("MT", [128, 8, 512], F32R)
        YAr = P.sb("YAr", [128, 4, 512], F32R); YHr = P.sb("YHr", [128, 4, 512], F32R)
        gs = [P.sb("gs0", [128, 512]), P.sb("gs1", [128, 512])]; m1 = P.sb("m1", [128, 512]); m2 = P.sb("m2", [128, 512])
        xm0_ = P.sb("xm0", [128, 1024]); xm = [xm0_, xm0_]
        pT = P.ps("pT", [128, 1024]); pG = [P.ps("pG0", [128, 512]), P.ps("pG1", [128, 512])]
        pM = [P.ps("pM0", [128, 512]), P.ps("pM1", [128, 512])]; pX = [P.ps("pX0", [128, 512]), P.ps("pX1", [128, 512])]
        wc = 0; xc = 0
        for B in range(4):
            t0 = own0 + B * 512
            xts = []
            for t in range(4):
                b = xc % 2; xc += 1
                P.dma(xt[b][:], x_d[t0 + t * 128: t0 + (t + 1) * 128, :], writes=[f"xt{b}"])
                norm_tile(xt[b], 128, G1, SH1, ht[b], f"xt{b}", f"ht{b}", "m4")
                for c in range(8):
                    P.g("tensor", "transpose", [f"ht{b}", "cst"], ["pT"], out=pT[:, c * 128:(c + 1) * 128], in_=ht[b][:, c * 128:(c + 1) * 128], identity=ident, silent=(c != 7))
                P.g("scalar", "activation", ["pT"], ["hT"], out=hT[:, :, t * 128:(t + 1) * 128], in_=pT[:].rearrange("p (c n) -> p c n", c=8), func=AF.Copy)
            P.dma(YA[:], ya_d[:, :, B * 512:(B + 1) * 512].rearrange("c p n -> p c n"), reads=["ya_d"], writes=["YA"])
            P.dma(YH[:], yh_d[:, :, B * 512:(B + 1) * 512].rearrange("c p n -> p c n"), reads=["yh_d"], writes=["YH"])
            P.g("gpsimd", "tensor_copy", ["YA"], ["YAr"], out=YAr[:], in_=YA[:])
            P.g("gpsimd", "tensor_copy", ["YH"], ["YHr"], out=YHr[:], in_=YH[:])
            for oc in range(8):
                for gi in range(2):
                    i = wc % 2; j_ = wc % 2; wc += 1
                    P.dma(Wf[j_][:], win_d[CG + gi * 8 + oc], writes=[f"Wf{j_}"])
                    if wc % 3 == 0:
                        P.g("gpsimd", "tensor_copy", [f"Wf{j_}"], [f"W{i}"], out=W[i][:], in_=Wf[j_][:])
                    else:
                        P.g("scalar", "activation", [f"Wf{j_}"], [f"W{i}"], out=W[i][:], in_=Wf[j_][:], func=AF.Copy)
                    for c in range(8):
                        P.g("tensor", "matmul", [f"W{i}", "hT"], [f"pG{gi}"], pG[gi][:], lhsT=W[i][:, c, :], rhs=hT[:, c, :], start=(c == 0), stop=(c == 7), silent=(c != 7))
                    P.g("scalar", "activation", [f"pG{gi}"], [f"gs{gi}"], out=gs[gi][:], in_=pG[gi][:], func=AF.Sigmoid)
                for c in range(4):
                    P.g("tensor", "matmul", ["woa", "YAr"], ["pM0"], pM[0][:], lhsT=woa[:, c, oc * 128:(oc + 1) * 128], rhs=YAr[:, c, :], start=(c == 0), stop=(c == 3), silent=(c != 3))
                for c in range(4):
                    P.g("tensor", "matmul", ["woh", "YHr"], ["pM1"], pM[1][:], lhsT=woh[:, c, oc * 128:(oc + 1) * 128], rhs=YHr[:, c, :], start=(c == 0), stop=(c == 3), silent=(c != 3))
                P.g("vector", "tensor_tensor", ["pM0", "gs0"], ["m1"], out=m1[:], in0=pM[0][:], in1=gs[0][:], op=ALU.mult)
                P.g("vector", "tensor_tensor", ["pM1", "gs1"], ["m2"], out=m2[:], in0=pM[1][:], in1=gs[1][:], op=ALU.mult)
                P.g("vector", "tensor_tensor", ["m1", "m2"], ["MT"], out=MT[:, oc, :], in0=m1[:], in1=m2[:], op=ALU.add)
            for t in range(4):
                b = xc % 2; xc += 1
                P.dma(xt[b][:], x_d[t0 + t * 128: t0 + (t + 1) * 128, :], writes=[f"xt{b}"])
                for hf in range(2):
                    for c in range(8):
                        P.g("tensor", "matmul", ["MT", "wout"], [f"pX{hf}"], pX[hf][:], lhsT=MT[:, c, t * 128:(t + 1) * 128], rhs=wout[:, c, hf * 512:(hf + 1) * 512],
                            start=(c == 0), stop=(c == 7), silent=(c != 7))
                    P.g("vector", "tensor_tensor", [f"pX{hf}"], ["xm0"], out=xm[b][:, hf * 512:(hf + 1) * 512], in0=pX[hf][:], in1=g1[:, hf * 512:(hf + 1) * 512], op=ALU.mult)
                P.g("vector", "tensor_tensor", ["xm0", f"xt{b}"], ["xm0"], out=xm[b][:], in0=xm[b][:], in1=xt[b][:], op=ALU.add)
                r = B * 512 + t * 128
                P.dma(xm_d[r:r + 128, :], xm[b][:], reads=["xm0"], writes=["xm_d"])
    if dbg and stop_after == 4:
        P.dma(dbg_d[:, 0:1024], xm_d[0:128, :], reads=[]); P.dma(dbg_d[:, 1024:2048], xm_d[1920:2048, :], reads=[])
        P.finish(P.all_tokens()); P.emit(); return nc

    out_tokens = []
    with P.scope():
        wq = P.sb("wq", [128, 8, 1024]); kbd = P.sb("kbd", [128, 8, 256])
        P.dma(wq[:], wq_d, writes=["wq"]); P.dma(kbd[:], kbd_d, writes=["kbd"])
        NBU = 12
        UV = [P.sb(f"UV{i}", [128, 2048], BF16) for i in range(NBU)]
        Dg = [P.sb(f"Dg{i}", [128, 128], BF16) for i in range(4)]
        xmt = [P.sb(f"xmt{i}", [128, 1024]) for i in range(3)]; h2 = [P.sb(f"h2_{i}", [128, 1024]) for i in range(3)]
        h2T = P.sb("h2T", [128, 8, 128]); QTs = P.sb("QTs", [128, 8, 128]); SCs = [P.sb("SCa", [128, 16, 128]), P.sb("SCb", [128, 16, 128])]; SC2 = P.sb("SC2", [128, 16, 128])
        V16 = P.sb("V16", [128, 16, 16]); I16 = P.sb("I16", [128, 16, 16], U32); I16f = P.sb("I16f", [128, 16, 16])
        cand = P.sb("cand", [128, 8, 256]); B16 = P.sb("B16", [128, 8, 16])
        cand2 = SC2[:].rearrange("p g k -> p (g k)").rearrange("p (h k) -> p h k", h=8)
        PI = P.sb("PI", [128, 8, 16], U32); PA = P.sb("PA", [128, 8, 16], U32); PB = P.sb("PB", [128, 8, 16], U32)
        paf = P.sb("paf", [128, 8, 16]); pbf = P.sb("pbf", [128, 8, 16]); OH = SC2[:].rearrange("p g k -> p (g k)").rearrange("p (h k) -> p h k", h=8)
        isel = P.sb("isel", [128, 128]); jsel = P.sb("jsel", [128, 128]); eif = P.sb("eif", [128, 128])
        EI = [P.sb("EI0", [128, 128], U32), P.sb("EI1", [128, 128], U32)]
        EG = P.sb("EG", [128, 8, 16]); EGs = [P.sb("EGs0", [128, 8, 16]), P.sb("EGs1", [128, 8, 16])]; gsm = P.sb("gsm", [128, 16]); GT = [P.sb("GT0", [128, 128]), P.sb("GT1", [128, 128])]
        Adot = [P.sb("Ad0", [128, 128]), P.sb("Ad1", [128, 128])]; junk = P.sb("junk", [128, 1024], BF16)
        gw = [P.sb("gwa", [128, 8]), P.sb("gwb", [128, 8])]; gw2 = [P.sb("gw2a", [128, 8]), P.sb("gw2b", [128, 8])]
        GA = [P.sb("GA0", [128, 128]), P.sb("GA1", [128, 128])]; acc0_ = P.sb("acc0", [128, 1024]); acc = [acc0_, acc0_]
        pT = P.ps("pT", [128, 1024]); pQ = pT
        pSc = P.ps("pSc", [128, 1024]); pXs = [P.ps("pXa", [128, 1024]), P.ps("pXb", [128, 1024])]
        cnt = {"u": 0, "v": 0, "d": 0}
        V16v = V16[:].rearrange("p (h t) k -> p h t k", t=2)
        I16v = I16f[:].rearrange("p (h t) k -> p h t k", t=2)
        NT = 1 if (dbg and stop_after == 5) else 16

        def front_pieces(n):
            b3 = n % 3; sb_ = n % 2; SC = SCs[sb_]

            def pa():
                P.dma(xmt[b3][:], xm_d[n * 128:(n + 1) * 128, :], reads=["xm_d"], writes=[f"xmt{b3}"])
                norm_tile(xmt[b3], 128, G2, SH2, h2[b3], f"xmt{b3}", f"h2{b3}", "p")

            def pb():
                for c in range(8):
                    P.g("tensor", "transpose", [f"h2{b3}", "cst"], ["pT"], out=pT[:, c * 128:(c + 1) * 128], in_=h2[b3][:, c * 128:(c + 1) * 128], identity=ident, silent=(c != 7))

            def pc():
                P.g("scalar", "activation", ["pT"], ["h2T"], out=h2T[:], in_=pT[:].rearrange("p (c n) -> p c n", c=8), func=AF.Copy)

            def pd(h0):
                def f():
                    for hd in range(h0, h0 + 4):
                        for c in range(8):
                            P.g("tensor", "matmul", ["wq", "h2T"], ["pT"], pQ[:, hd * 128:(hd + 1) * 128], lhsT=wq[:, c, hd * 128:(hd + 1) * 128], rhs=h2T[:, c, :],
                                start=(c == 0), stop=(c == 7), silent=(c != 7))
                return f

            def pe():
                P.g("scalar", "activation", ["pT"], ["QTs"], out=QTs[:], in_=pQ[:].rearrange("p (h n) -> p h n", h=8), func=AF.Copy)

            def pf(q):
                def f():
                    for h4 in range(4):
                        hd = q * 4 + h4
                        P.g("tensor", "matmul", ["QTs", "kbd"], ["pSc"], pSc[:, h4 * 256:(h4 + 1) * 256], lhsT=QTs[:, hd, :], rhs=kbd[:, hd, :], start=True, stop=True, silent=(h4 != 3))
                return f

            def pg(q):
                def f():
                    P.g("scalar", "activation", ["pSc"], [f"SC{sb_}_{q}"], out=SC[:, q * 8:(q + 1) * 8, :], in_=pSc[:].rearrange("p (g k) -> p g k", g=8), func=AF.Copy)
                return f
            return [pa, pb, pc, pd(0), pd(4), pe, pf(0), pg(0), pf(1), pg(1)]

        def front(n):
            for f in front_pieces(n):
                f()

        def routing_gen(n):
            b = n % 2; sb_ = n % 2; SC = SCs[sb_]
            for gI in range(16):
                P.g("vector", "max", [f"SC{sb_}_{gI // 8}"], [f"V16a{gI}"], out=V16[:, gI, 0:8], in_=SC[:, gI, :])
            yield
            for gI in range(16):
                P.g("vector", "match_replace", [f"SC{sb_}_{gI // 8}", f"V16a{gI}"], [f"SC2{gI}"], out=SC2[:, gI, :], in_to_replace=V16[:, gI, 0:8], in_values=SC[:, gI, :], imm_value=-1e30)
            yield
            for gI in range(16):
                P.g("vector", "max", [f"SC2{gI}"], [f"V16b{gI}"], out=V16[:, gI, 8:16], in_=SC2[:, gI, :])
            yield
            for gI in range(16):
                P.g("vector", "max_index", [f"SC{sb_}_{gI // 8}", f"V16a{gI}"], [f"I16a{gI}"], out=I16[:, gI, 0:8], in_max=V16[:, gI, 0:8], in_values=SC[:, gI, :])
            yield
            for gI in range(16):
                P.g("vector", "max_index", [f"SC{sb_}_{gI // 8}", f"V16b{gI}"], [f"I16b{gI}"], out=I16[:, gI, 8:16], in_max=V16[:, gI, 8:16], in_values=SC[:, gI, :])
            yield
            allV = [f"V16a{g_}" for g_ in range(16)] + [f"V16b{g_}" for g_ in range(16)]
            allI = [f"I16a{g_}" for g_ in range(16)] + [f"I16b{g_}" for g_ in range(16)]
            P.g("vector", "tensor_copy", allI, ["I16f"], out=I16f[:], in_=I16[:])
            P.g("vector", "tensor_tensor", allV + [f"cand2{h_}" for h_ in range(8)], ["cand"], out=cand[:].rearrange("p h (a c) -> p h a c", a=16),
                in0=V16v[:, :, 0, :].unsqueeze(3).to_broadcast([128, 8, 16, 16]), in1=V16v[:, :, 1, :].unsqueeze(2).to_broadcast([128, 8, 16, 16]), op=ALU.add)
            yield
            for hd in range(8):
                P.g("vector", "max", ["cand"], [f"B16a{hd}"], out=B16[:, hd, 0:8], in_=cand[:, hd, :])
            for hd in range(8):
                P.g("vector", "match_replace", ["cand", f"B16a{hd}"], [f"cand2{hd}"], out=cand2[:, hd, :], in_to_replace=B16[:, hd, 0:8], in_values=cand[:, hd, :], imm_value=-1e30)
            yield
            for hd in range(8):
                P.g("vector", "max", [f"cand2{hd}"], [f"B16b{hd}"], out=B16[:, hd, 8:16], in_=cand2[:, hd, :])
            for hd in range(8):
                P.g("vector", "max_index", ["cand", f"B16a{hd}"], [f"PIa{hd}"], out=PI[:, hd, 0:8], in_max=B16[:, hd, 0:8], in_values=cand[:, hd, :])
            yield
            for hd in range(8):
                P.g("vector", "max_index", ["cand", f"B16b{hd}"], [f"PIb{hd}"], out=PI[:, hd, 8:16], in_max=B16[:, hd, 8:16], in_values=cand[:, hd, :])
            allB = [f"B16a{h_}" for h_ in range(8)] + [f"B16b{h_}" for h_ in range(8)]
            allP = [f"PIa{h_}" for h_ in range(8)] + [f"PIb{h_}" for h_ in range(8)]
            P.g("vector", "tensor_single_scalar", allP, ["PA"], out=PA[:], in_=PI[:], scalar=4, op=ALU.logical_shift_right)
            P.g("vector", "tensor_single_scalar", allP, ["PB"], out=PB[:], in_=PI[:], scalar=15, op=ALU.bitwise_and)
            P.g("vector", "tensor_copy", ["PA"], ["paf"], out=paf[:], in_=PA[:])
            P.g("vector", "tensor_copy", ["PB"], ["pbf"], out=pbf[:], in_=PB[:])
            yield
            for (pf_, pk, tt, dst, dk) in ((paf, "paf", 0, isel, "isel"), (pbf, "pbf", 1, jsel, "jsel")):
                OHv = OH.rearrange("p h (k a) -> p h k a", k=16)
                P.g("vector", "tensor_tensor", [pk, "cst"] + [f"cand2{h_}" for h_ in range(8)], ["OH"], out=OHv, in0=iota16.unsqueeze(1).unsqueeze(1).to_broadcast([128, 8, 16, 16]),
                    in1=pf_[:].unsqueeze(3).to_broadcast([128, 8, 16, 16]), op=ALU.is_equal)
                yield
                P.g("vector", "tensor_tensor", ["OH", "I16f"], ["OH"], out=OHv, in0=OHv, in1=I16v[:, :, tt, :].unsqueeze(2).to_broadcast([128, 8, 16, 16]), op=ALU.mult)
                P.g("vector", "tensor_reduce", ["OH"], [dk], out=dst[:], in_=OH.rearrange("p h (k a) -> p (h k) a", k=16), axis=AX.X, op=ALU.add)
                yield
            P.g("vector", "scalar_tensor_tensor", ["isel", "jsel"], ["eif"], out=eif[:], in0=isel[:], scalar=128.0, in1=jsel[:], op0=ALU.mult, op1=ALU.add)
            P.g("vector", "tensor_copy", ["eif"], [f"EI{b}"], out=EI[b][:], in_=eif[:])
            P.g("vector", "tensor_tensor", allB, [f"EGs{b}"], out=EGs[b][:], in0=B16[:], in1=B16[:, :, 0:1].to_broadcast([128, 8, 16]), op=ALU.subtract)

        def routing(n):
            for _ in routing_gen(n):
                pass

        def routing_b(n):
            b = n % 2
            P.g("scalar", "activation", [f"EGs{b}"], ["EG"], out=EG[:], in_=EGs[b][:], func=AF.Exp)
            P.g("vector", "tensor_reduce", ["EG"], ["gsm"], out=gsm[:, 0:8], in_=EG[:], axis=AX.X, op=ALU.add)
            P.g("vector", "reciprocal", ["gsm"], ["gsm2"], out=gsm[:, 8:16], in_=gsm[:, 0:8])
            P.g("vector", "tensor_tensor", ["EG", "gsm2"], [f"GT{b}"], out=GT[b][:].rearrange("p (h k) -> p h k", h=8), in0=EG[:],
                in1=gsm[:, 8:16].unsqueeze(2).to_broadcast([128, 8, 16]), op=ALU.mult)

        def epilogue(n):
            b = n % 2; b3 = n % 3; pX = pXs[n % 2]
            P.g("vector", "tensor_tensor", [f"pX{n % 2}"], ["acc0"], out=acc[b][:], in0=pX[:], in1=g2, op=ALU.mult)
            P.g("vector", "tensor_tensor", ["acc0", f"xmt{b3}"], ["acc0"], out=acc[b][:], in0=acc[b][:], in1=xmt[b3][:], op=ALU.add)
            norm_tile(acc[b], 128, FG, None, xmt[b3], "acc0", f"xmt{b3}", "f")
            out_tokens.append(P.dma(out_d[n * 128:(n + 1) * 128, :], xmt[b3][:], reads=[f"xmt{b3}"], writes=["out_d"]))

        def tile_body(n):
            b = n % 2; b3 = n % 3; SC = SCs[n % 2]; pX = pXs[n % 2]
            fp = front_pieces(n + 2) if n + 2 < NT else None
            rg = routing_gen(n + 1) if n + 1 < NT else None
            for g in range(16):
                gb = g % 2
                ks = []
                for j in range(8):
                    s = g * 8 + j
                    k = cnt["u"] % NBU; cnt["u"] += 1; ks.append(k)
                    P.gather(UV[k][:], uv16_d, EI[b][:, s:s + 1], reads=[f"EI{b}", "p16"], writes=[f"UV{k}"])
                    P.g("vector", "scalar_tensor_tensor", [f"UV{k}", f"h2{b3}"], ["junk", f"Ad{b}_{s}"], out=junk[:], in0=UV[k][:, 0:1024], scalar=1.0, in1=h2[b3][:],
                        op0=ALU.mult, op1=ALU.mult, accum_out=Adot[b][:, s:s + 1])
                akeys = [f"Ad{b}_{g * 8 + j}" for j in range(8)]
                av = Adot[b][:, g * 8:(g + 1) * 8]
                P.g("vector", "tensor_tensor", akeys, [f"gw{gb}"], out=gw[gb][:], in0=av, in1=av, op=ALU.mult)
                P.g("vector", "tensor_scalar", [f"gw{gb}"], [f"gw{gb}"], out=gw[gb][:], in0=gw[gb][:], scalar1=0.044715, scalar2=1.0, op0=ALU.mult, op1=ALU.add)
                P.g("vector", "tensor_tensor", [f"gw{gb}"] + akeys, [f"gw{gb}"], out=gw[gb][:], in0=gw[gb][:], in1=av, op=ALU.mult)
                P.g("scalar", "activation", [f"gw{gb}"], [f"gw2{gb}"], out=gw2[gb][:], in_=gw[gb][:], func=AF.Sigmoid, scale=2.0 * math.sqrt(2.0 / math.pi))
                P.g("vector", "tensor_tensor", [f"gw2{gb}"] + akeys, [f"gw2{gb}"], out=gw2[gb][:], in0=gw2[gb][:], in1=av, op=ALU.mult)
                P.g("vector", "tensor_tensor", [f"gw2{gb}", f"GT{b}"], [f"GA{b}_{g}"], out=GA[b][:, g * 8:(g + 1) * 8], in0=gw2[gb][:], in1=GT[b][:, g * 8:(g + 1) * 8], op=ALU.mult)
                for j in range(8):
                    s = g * 8 + j; k = ks[j]
                    kd = cnt["d"] % 4; cnt["d"] += 1
                    P.g("scalar", "activation", ["cst", f"GA{b}_{g}"], [f"Dg{kd}"], out=Dg[kd][:], in_=ident, func=AF.Copy, scale=GA[b][:, s:s + 1])
                    for hf in range(2):
                        P.g("tensor", "matmul", [f"Dg{kd}", f"UV{k}"], [f"pX{n % 2}"], pX[:, hf * 512:(hf + 1) * 512], lhsT=Dg[kd][:], rhs=UV[k][:, 1024 + hf * 512:1024 + (hf + 1) * 512],
                            start=(s == 0), stop=(s == 127), silent=(hf == 0))
                if g == 1 and n > 0:
                    epilogue(n - 1)
                if fp is not None and 2 <= g < len(fp) + 2:
                    fp[g - 2]()
                if rg is not None:
                    if next(rg, "done") == "done":
                        rg = None
            if dbg and stop_after == 5:
                P.barrier()
                P.g("vector", "tensor_copy", [], ["acc0"], out=acc[0][:], in_=pX[:])
                P.barrier()
                P.dma(dbg_d[:, 0:2048], SC[:].rearrange("p g k -> p (g k)"), reads=[])
                P.dma(dbg_d[:, 2048:2304], V16[:].rearrange("p g k -> p (g k)"), reads=[])
                P.dma(dbg_d[:, 2304:2560], I16f[:].rearrange("p g k -> p (g k)"), reads=[])
                P.dma(dbg_d[:, 2560:2688], B16[:].rearrange("p g k -> p (g k)"), reads=[])
                P.dma(dbg_d[:, 2688:2816], eif[:], reads=[])
                P.dma(dbg_d[:, 2816:2944], GT[0][:], reads=[])
                P.dma(dbg_d[:, 2944:3072], Adot[0][:], reads=[])
                P.dma(dbg_d[:, 3072:4096], acc[0][:], reads=[])
                P.barrier()
            if rg is not None:
                for _ in rg:
                    pass
            if n + 1 < NT:
                routing_b(n + 1)
            if n == NT - 1:
                epilogue(n)

        front(0); routing(0); routing_b(0)
        if NT > 1:
            front(1)
        for n in range(NT):
            tile_body(n)
    P.finish(P.all_tokens())
    P.emit()
    return nc


_CONST_CACHE = {}


def _constants():
    if _CONST_CACHE:
        return _CONST_CACHE
    N = 2 * L
    base = np.cos(2 * np.pi * np.arange(N) / N)
    bases = np.sin(2 * np.pi * np.arange(N) / N)
    s = np.arange(L, dtype=np.int64)[:, None]
    k = np.arange(33 * 128, dtype=np.int64)[None, :]
    idx = (s * k) % N
    valid = (k <= L)
    C = np.where(valid, base[idx], 0.0).astype(np.float32)
    S = np.where(valid, bases[idx], 0.0).astype(np.float32)
    def fwd(T):
        return np.ascontiguousarray(T.reshape(32, 128, 33, 128).transpose(2, 1, 0, 3)).astype(ml_dtypes.bfloat16)
    _CONST_CACHE["dftc"] = fwd(C); _CONST_CACHE["dfts"] = fwd(S)
    kk = np.arange(33 * 128, dtype=np.int64)[:, None]
    t = np.arange(L, dtype=np.int64)[None, :]
    idx2 = (kk * t) % N
    wk = np.where((kk == 0) | (kk == L), 1.0, 2.0) / N
    wk = np.where(kk <= L, wk, 0.0)
    IC = (base[idx2] * wk).astype(np.float32).reshape(33, 128, L)
    IS = (bases[idx2] * wk).astype(np.float32).reshape(33, 128, L)
    _CONST_CACHE["idc"] = [np.ascontiguousarray(IC[:, :, h * 2048:(h + 1) * 2048]).astype(ml_dtypes.bfloat16) for h in range(2)]
    _CONST_CACHE["ids"] = [np.ascontiguousarray(IS[:, :, h * 2048:(h + 1) * 2048]).astype(ml_dtypes.bfloat16) for h in range(2)]
    f = 16
    inv = (10000.0 ** (-np.arange(f, dtype=np.float32) / f)).astype(np.float32)
    tok = np.arange(L)
    row = (tok // 64).astype(np.float32); col = (tok % 64).astype(np.float32)
    cosT = np.zeros((64, L), np.float32); sinT = np.zeros((64, L), np.float32)
    for d in range(64):
        pos = row if d < 32 else col
        j = d % 32
        ang = pos * inv[j % 16]
        cosT[d] = np.cos(ang)
        sinT[d] = -np.sin(ang) if j < 16 else np.sin(ang)
    _CONST_CACHE["rope"] = np.ascontiguousarray(np.stack([np.tile(cosT, (2, 1)), np.tile(sinT, (2, 1))])).astype(np.float32)
    def feats(tt):
        bands = np.arange(1, 17, dtype=np.float32)
        ang = (2.0 * np.pi * tt[:, None] * bands[None, :]).astype(np.float32)
        return np.concatenate([tt[:, None], np.cos(ang), np.sin(ang)], axis=-1).astype(np.float32)
    tt = (np.arange(L, dtype=np.float32) / L).astype(np.float32)
    tts = (np.maximum(np.arange(L) - 1, 0).astype(np.float32) / L).astype(np.float32)
    _CONST_CACHE["zf"] = np.ascontiguousarray(np.stack([feats(tt).T, feats(tts).T])).astype(np.float32)
    deltas = np.linspace(math.log(1e-2) / 1.5, math.log(1e-2) / 0.3, 512, dtype=np.float32)
    _CONST_CACHE["ndelta"] = np.ascontiguousarray(np.tile(-np.abs(deltas)[None, :], (128, 1))).astype(np.float32)
    tp = np.zeros((128, 64), np.float32)
    tp[:, 0:32] = tt.reshape(32, 128).T
    tp[:, 32:64] = tts.reshape(32, 128).T
    _CONST_CACHE["tpos"] = tp
    cst = np.zeros((128, 1024), np.float32)
    cst[:, 0:128] = np.eye(128, dtype=np.float32)
    i = np.arange(128)[:, None]; j = np.arange(384)[None, :]
    cst[:, 128:512] = np.where((j >= i) & (j <= i + 256), 0.0, NEG)
    cst[:, 512:528] = np.arange(16, dtype=np.float32)[None, :]
    cst[:, 528:656] = 1.0
    _CONST_CACHE["consts"] = cst
    _CONST_CACHE["constsb"] = np.eye(128, dtype=np.float32).astype(ml_dtypes.bfloat16)
    return _CONST_CACHE


def _chunk_rows(w, nk):
    return np.ascontiguousarray(w.reshape(nk, 128, -1).transpose(1, 0, 2))


def prepare_inputs(inp):
    cs = _constants()
    f32 = lambda a: np.ascontiguousarray(np.asarray(a, dtype=np.float32))
    w_in = f32(inp["w_in"])[0]
    swap64 = np.concatenate([np.arange(16, 32), np.arange(0, 16), np.arange(48, 64), np.arange(32, 48)])
    cols = []
    for c in range(4):
        cols.append(np.concatenate([c * 64 + np.arange(64), (4 + c) * 64 + np.arange(64)]))
    for c in range(4):
        cols.append(np.concatenate([c * 64 + swap64, (4 + c) * 64 + swap64]))
    cols.append(512 + np.arange(128))
    cols.append(512 + np.concatenate([swap64, 64 + swap64]))
    cols.append(640 + np.arange(128))
    for c in range(12):
        cols.append(768 + c * 128 + np.arange(128))
    for c in range(16):
        cols.append(2304 + c * 128 + np.arange(128))
    wch = np.stack([_chunk_rows(w_in[:, cc], 8) for cc in cols])
    w_mod = f32(inp["w_mod"])[0]
    wm = np.ascontiguousarray(w_mod.reshape(8, 128, 12, 512).transpose(2, 1, 0, 3))
    bc = lambda v: np.ascontiguousarray(np.tile(f32(v).reshape(1, -1), (128, 1)))
    ng = np.ascontiguousarray(np.stack([bc(inp["norm1_g"][0]), bc(inp["norm2_g"][0]), bc(inp["final_g"])], axis=1))
    convw = np.zeros((128, 12, 4), np.float32)
    cw = f32(inp["hy_conv_w"])[0]; cbias = f32(inp["hy_conv_b"])[0]
    for j in range(3):
        convw[:, :, j] = cw[j].reshape(12, 128).T
    convw[:, :, 3] = cbias.reshape(12, 128).T
    fvec = np.stack([f32(inp["hy_fb1"])[0], f32(inp["hy_fb2"])[0], f32(inp["hy_freq"])[0], np.zeros(64, np.float32)], axis=1)
    keys = f32(inp["peer_keys"])[0]
    kbd = np.zeros((128, 8, 256), np.float32)
    for h in range(8):
        for p in range(2):
            kbd[p * 64:(p + 1) * 64, h, p * 128:(p + 1) * 128] = keys[h, p].T
    shared = {
        "w_mod": wm, "b_mod": bc(inp["b_mod"][0]), "ng": ng, "w_in": wch, "rope": cs["rope"], "sink": bc(inp["attn_sink"][0]),
        "convw": convw, "fw1": f32(inp["hy_fw1"])[0], "fvec": np.ascontiguousarray(fvec), "fw2": f32(inp["hy_fw2"])[0],
        "fw3": f32(inp["hy_fw3"])[0], "fb3": bc(inp["hy_fb3"][0]), "skip": bc(inp["hy_skip"][0]), "zf": cs["zf"],
        "ndelta": cs["ndelta"], "tpos": cs["tpos"], "w_oa": _chunk_rows(f32(inp["w_o_attn"])[0], 4),
        "w_oh": _chunk_rows(f32(inp["w_o_hy"])[0], 4), "w_out": _chunk_rows(f32(inp["w_out"])[0], 8),
        "wq": _chunk_rows(f32(inp["peer_wq"])[0], 8), "kbd": kbd, "peer_u": f32(inp["peer_u"])[0], "peer_v": f32(inp["peer_v"])[0],
        "dftc": cs["dftc"], "dfts": cs["dfts"], "consts": cs["consts"], "constsb": cs["constsb"],
    }
    x = f32(inp["x"]); ctx = f32(inp["ctx"]); c = f32(inp["c"]); c_ctx = f32(inp["c_ctx"])
    maps = []
    for core in range(8):
        b, half = core // 2, core % 2
        cvec = np.concatenate([c[b].reshape(8, 128).T, c_ctx.reshape(8, 128).T], axis=1)
        m = dict(shared)
        cst = cs["consts"].copy()
        cst[:, 656] = 1.0 if half == 0 else -1.0
        xb = x[b] if half == 0 else np.ascontiguousarray(x[b][::-1])
        m.update({"x": xb, "ctx": ctx[b], "cvec": np.ascontiguousarray(cvec), "idftc": cs["idc"][0], "idfts": cs["ids"][0], "consts": cst})
        if half == 1:
            m["rope"] = np.ascontiguousarray(cs["rope"][:, :, ::-1])
            m["convw"] = np.ascontiguousarray(convw[:, :, [2, 1, 0, 3]])
        maps.append(m)
    return maps


_PROG_CACHE = {}


def kernel(**inputs):
    maps = prepare_inputs(inputs)
    nc = build_program(0)
    res = run_bass_kernel_spmd(nc, maps, core_ids=list(range(8)))
    out = np.zeros((4, L, D), np.float32)
    for core in range(8):
        b, half = core // 2, core % 2
        o = res.results[core]["out"]
        if half == 0:
            out[b, 0:2048] = o
        else:
            out[b, 2048:4096] = o[::-1]
    return out
```
